# Optimizing a Trainium2 kernel written in Bass

```python
import math
import jax
import jax.numpy as jnp
from jax import lax
import numpy as np

D_MODEL = 2048
BATCH = 16
SEQ = 256
DEPTH = 2
DEC_BATCH = 4
DEC_SEQ = 1024
PAST_LEN = 256

GRID_W = 64
HEAD_DIM = 64
GROUP_W = D_MODEL // 4
SWA_HEADS = GROUP_W // HEAD_DIM
SWA_KV_HEADS = 2
SWA_GROUP = SWA_HEADS // SWA_KV_HEADS
SWA_WINDOW = 128
SWA_BLOCK = 128
NA_HEADS = GROUP_W // HEAD_DIM
NA_KH = 8
NA_KW = 16
DIFF_HEADS = GROUP_W // (2 * HEAD_DIM)
DIFF_VDIM = 2 * HEAD_DIM
LRU_WIDTH = GROUP_W
LRU_BLOCKS = 8
LRU_BLOCK_W = LRU_WIDTH // LRU_BLOCKS
LRU_C = 8.0
CONV_W = 4
FFN_HIDDEN = (-(-8 * D_MODEL // 3) + 255) // 256 * 256
Q_BLOCK = 128
ROPE_BASE = 10000.0
NORM_EPS = 1e-6
NEG_INF = -1e30
MOD_CHUNKS = 6
PROJ_SIZES = (SWA_HEADS * HEAD_DIM, SWA_KV_HEADS * HEAD_DIM, SWA_KV_HEADS * HEAD_DIM,
              NA_HEADS * HEAD_DIM, NA_HEADS * HEAD_DIM, NA_HEADS * HEAD_DIM,
              DIFF_HEADS * 2 * HEAD_DIM, DIFF_HEADS * 2 * HEAD_DIM, DIFF_HEADS * DIFF_VDIM,
              LRU_WIDTH, LRU_WIDTH)
PROJ_W = sum(PROJ_SIZES)
MIX_W = SWA_HEADS * HEAD_DIM + NA_HEADS * HEAD_DIM + DIFF_HEADS * DIFF_VDIM + LRU_WIDTH

kernel_name = 'hybrid_parallel_heads_flow_step'


def _rms(x, g):
    xf = x.astype(jnp.float32)
    y = xf * lax.rsqrt(jnp.mean(xf * xf, axis=-1, keepdims=True) + NORM_EPS)
    return (y * g.astype(jnp.float32)).astype(x.dtype)


def _modulation(cvec, w_mod, b_mod):
    m = jax.nn.silu(cvec) @ w_mod + b_mod
    return jnp.split(m, MOD_CHUNKS, axis=-1)


def _modnorm(x, g, shift, scale):
    return _rms(x, g) * (1 + scale) + shift


def _swiglu(h, wg, wu, wd):
    return (jax.nn.silu(h @ wg) * (h @ wu)) @ wd


def _axial_angles(n_tok):
    t = jnp.arange(n_tok)
    pos = jnp.stack([t // GRID_W, t % GRID_W], axis=-1).astype(jnp.float32)
    nf = HEAD_DIM // 4
    inv = ROPE_BASE ** (-jnp.arange(nf, dtype=jnp.float32) / nf)
    return pos[:, :, None] * inv


def _rope(x, ang):
    nf = HEAD_DIM // 4
    shp = x.shape
    bshape = (1, shp[1]) + (1,) * (x.ndim - 3) + (2, nf)
    cos = jnp.cos(ang).reshape(bshape)
    sin = jnp.sin(ang).reshape(bshape)
    xf = x.astype(jnp.float32).reshape(shp[:-1] + (2, 2, nf))
    x1, x2 = xf[..., 0, :], xf[..., 1, :]
    out = jnp.stack([x1 * cos - x2 * sin, x2 * cos + x1 * sin], axis=-2)
    return out.reshape(shp).astype(x.dtype)


def _project(h, w_in, qk_gain):
    p = h @ w_in
    b, t, _ = p.shape
    parts = []
    off = 0
    for size in PROJ_SIZES:
        parts.append(p[..., off:off + size])
        off += size
    sq, sk, sv, nq, nk, nv, dq, dk, dv, lx, lg = parts
    sq = _rms(sq.reshape(b, t, SWA_KV_HEADS, SWA_GROUP, HEAD_DIM), qk_gain[0, 0])
    sk = _rms(sk.reshape(b, t, SWA_KV_HEADS, HEAD_DIM), qk_gain[0, 1])
    sv = sv.reshape(b, t, SWA_KV_HEADS, HEAD_DIM)
    nq = _rms(nq.reshape(b, t, NA_HEADS, HEAD_DIM), qk_gain[1, 0])
    nk = _rms(nk.reshape(b, t, NA_HEADS, HEAD_DIM), qk_gain[1, 1])
    nv = nv.reshape(b, t, NA_HEADS, HEAD_DIM)
    dq = _rms(dq.reshape(b, t, DIFF_HEADS, 2, HEAD_DIM), qk_gain[2, 0])
    dk = _rms(dk.reshape(b, t, DIFF_HEADS, 2, HEAD_DIM), qk_gain[2, 1])
    dv = dv.reshape(b, t, DIFF_HEADS, DIFF_VDIM)
    return sq, sk, sv, nq, nk, nv, dq, dk, dv, lx, lg


def _dense_attn(q, k, v, sink):
    b, t, hk, g, d = q.shape
    nb = t // Q_BLOCK
    scale = d ** -0.5
    qb = jnp.moveaxis(q.reshape(b, nb, Q_BLOCK, hk, g, d), 1, 0)

    def one(qblk):
        s = jnp.einsum('bqhgd,bshd->bhgqs', qblk, k).astype(jnp.float32) * scale
        if sink is None:
            p = jax.nn.softmax(s, axis=-1)
        else:
            s_sink = jnp.broadcast_to(sink.astype(jnp.float32)[None, :, :, None, None], s.shape[:-1] + (1,))
            p = jax.nn.softmax(jnp.concatenate([s, s_sink], axis=-1), axis=-1)[..., :-1]
        return jnp.einsum('bhgqs,bshe->bqhge', p.astype(v.dtype), v)

    o = lax.map(one, qb)
    return jnp.moveaxis(o, 0, 1).reshape(b, t, hk, g, -1)


def _diff_attn(q, k, v, lam):
    b, t, h, _, d = q.shape
    nb = t // Q_BLOCK
    scale = d ** -0.5
    qb = jnp.moveaxis(q.reshape(b, nb, Q_BLOCK, h, 2, d), 1, 0)

    def one(qblk):
        s = jnp.einsum('bqhid,bshid->bihqs', qblk, k).astype(jnp.float32) * scale
        p = jax.nn.softmax(s, axis=-1)
        pd = p[:, 0] - lam * p[:, 1]
        return jnp.einsum('bhqs,bshe->bqhe', pd.astype(v.dtype), v)

    o = lax.map(one, qb)
    return jnp.moveaxis(o, 0, 1).reshape(b, t, h, -1)


def _lambda_init(layer):
    return 0.8 - 0.6 * math.exp(-0.3 * layer)


def _diff_lambda(lp, lam_init):
    lp = lp.astype(jnp.float32)
    return jnp.exp(jnp.sum(lp[0] * lp[1])) - jnp.exp(jnp.sum(lp[2] * lp[3])) + lam_init


def _diff_out(o, subln, lam_init):
    b, t, h, e = o.shape
    return (_rms(o, subln) * (1.0 - lam_init)).reshape(b, t, h * e)


def _swa_latent(q, k, v, k_ctx, v_ctx, sink):
    b, t, hk, g, d = q.shape
    nb = t // SWA_BLOCK
    span = 3 * SWA_BLOCK
    scale = d ** -0.5
    pad = ((0, 0), (SWA_BLOCK, SWA_BLOCK), (0, 0), (0, 0))
    idx = np.arange(nb)[:, None] * SWA_BLOCK + np.arange(span)[None, :]
    kb = jnp.pad(k, pad)[:, idx]
    vb = jnp.pad(v, pad)[:, idx]
    qb = q.reshape(b, nb, SWA_BLOCK, hk, g, d)
    s_loc = jnp.einsum('bnqhgd,bnkhd->bnhgqk', qb, kb).astype(jnp.float32) * scale
    qpos = np.arange(t).reshape(nb, SWA_BLOCK)
    kpos = idx - SWA_BLOCK
    valid = ((kpos[:, None, :] >= 0) & (kpos[:, None, :] < t)
             & (np.abs(qpos[:, :, None] - kpos[:, None, :]) <= SWA_WINDOW))
    s_loc = jnp.where(valid[None, :, None, None], s_loc, NEG_INF)
    s_ctx = jnp.einsum('bnqhgd,bshd->bnhgqs', qb, k_ctx).astype(jnp.float32) * scale
    s_sink = jnp.broadcast_to(sink.astype(jnp.float32)[None, None, :, :, None, None], s_loc.shape[:-1] + (1,))
    p = jax.nn.softmax(jnp.concatenate([s_loc, s_ctx, s_sink], axis=-1), axis=-1).astype(v.dtype)
    n_ctx = k_ctx.shape[1]
    o = (jnp.einsum('bnhgqk,bnkhe->bnqhge', p[..., :span], vb)
         + jnp.einsum('bnhgqs,bshe->bnqhge', p[..., span:span + n_ctx], v_ctx))
    return o.reshape(b, t, hk, g, -1)


def _na_latent(q, k, v, k_ctx, v_ctx, rpb):
    b, t, h, d = q.shape
    rows = t // GRID_W
    kh = min(NA_KH, rows)
    n_loc = kh * GRID_W
    scale = d ** -0.5
    r = np.arange(rows)
    row_idx = np.clip(r - kh // 2, 0, rows - kh)[:, None] + np.arange(kh)[None, :]
    cq = np.arange(GRID_W)
    col_start = np.clip(cq - NA_KW // 2, 0, GRID_W - NA_KW)
    kg = k.reshape(b, rows, GRID_W, h, d)[:, row_idx].reshape(b, rows, n_loc, h, d)
    vg = v.reshape(b, rows, GRID_W, h, d)[:, row_idx].reshape(b, rows, n_loc, h, d)
    qg = q.reshape(b, rows, GRID_W, h, d)
    s_loc = jnp.einsum('brqhd,brkhd->brhqk', qg, kg).astype(jnp.float32) * scale
    dr = row_idx - r[:, None] + (NA_KH - 1)
    dc = np.clip(cq[None, :] - cq[:, None] + (NA_KW - 1), 0, 2 * NA_KW - 2)
    bias = rpb[:, dr[:, None, :, None], dc[None, :, None, :]]
    bias = jnp.moveaxis(bias.reshape(h, rows, GRID_W, n_loc), 0, 1).astype(jnp.float32)
    col_ok = (cq[None, :] >= col_start[:, None]) & (cq[None, :] < col_start[:, None] + NA_KW)
    col_ok = np.tile(col_ok, (1, kh))
    s_loc = jnp.where(col_ok, s_loc + bias, NEG_INF)
    s_ctx = jnp.einsum('brqhd,bshd->brhqs', qg, k_ctx).astype(jnp.float32) * scale
    p = jax.nn.softmax(jnp.concatenate([s_loc, s_ctx], axis=-1), axis=-1).astype(v.dtype)
    o = (jnp.einsum('brhqk,brkhe->brqhe', p[..., :n_loc], vg)
         + jnp.einsum('brhqs,bshe->brqhe', p[..., n_loc:], v_ctx))
    return o.reshape(b, t, h, -1)


def _dwconv_centred(x, w, bias):
    n = x.shape[1]
    left = CONV_W // 2
    xp = jnp.pad(x, ((0, 0), (left, CONV_W - 1 - left), (0, 0)))
    y = bias
    for tap in range(CONV_W):
        y = y + w[tap] * xp[:, tap:tap + n]
    return y


def _lin_combine(left, right):
    a1, b1 = left
    a2, b2 = right
    return a1 * a2, a2 * b1 + b2


def _lru_scan(u, w_a, b_a, w_x, b_x, lam_l, h0, reverse):
    b, t, ch = u.shape
    uf = u.astype(jnp.float32)
    ub = uf.reshape(b, t, LRU_BLOCKS, LRU_BLOCK_W)
    r = jax.nn.sigmoid(jnp.einsum('btnc,ncd->btnd', ub, w_a.astype(jnp.float32)).reshape(b, t, ch) + b_a)
    i = jax.nn.sigmoid(jnp.einsum('btnc,ncd->btnd', ub, w_x.astype(jnp.float32)).reshape(b, t, ch) + b_x)
    log_a = -LRU_C * r * jax.nn.softplus(-lam_l.astype(jnp.float32))
    a = jnp.exp(log_a)
    bx = jnp.sqrt(-jnp.expm1(2.0 * log_a)) * (i * uf)
    edge = t - 1 if reverse else 0
    bx = bx.at[:, edge].add(a[:, edge] * h0.astype(jnp.float32))
    _, hs = lax.associative_scan(_lin_combine, (a, bx), axis=1, reverse=reverse)
    final = hs[:, 0] if reverse else hs[:, -1]
    return hs, final


def _lru_mixer(xb, gb, conv_w, conv_b, wa, ba, wx, bx, lam_l, h0):
    u = _dwconv_centred(xb, conv_w, conv_b)
    hf, sf = _lru_scan(u, wa[0], ba[0], wx[0], bx[0], lam_l[0], h0[:, 0], False)
    hb, sb = _lru_scan(u, wa[1], ba[1], wx[1], bx[1], lam_l[1], h0[:, 1], True)
    y = ((hf + hb) * jax.nn.gelu(gb.astype(jnp.float32))).astype(xb.dtype)
    return y, jnp.stack([sf, sb], axis=1).astype(xb.dtype)


def setup_inputs(seed: int = 0) -> dict:
    key = jax.random.key(seed)
    ks = iter(jax.random.split(key, 40))
    f32 = jnp.float32
    D = D_MODEL

    def nrm(shape, scale):
        return jax.random.normal(next(ks), shape, f32) * scale

    x_prompt = nrm((BATCH, SEQ, D), 1.0)
    x_sample = nrm((DEC_BATCH, DEC_SEQ, D), 1.0)
    cache_swa_k = nrm((DEC_BATCH, DEPTH, PAST_LEN, SWA_KV_HEADS, HEAD_DIM), 1.0)
    cache_swa_v = nrm((DEC_BATCH, DEPTH, PAST_LEN, SWA_KV_HEADS, HEAD_DIM), 1.0)
    cache_na_k = nrm((DEC_BATCH, DEPTH, PAST_LEN, NA_HEADS, HEAD_DIM), 1.0)
    cache_na_v = nrm((DEC_BATCH, DEPTH, PAST_LEN, NA_HEADS, HEAD_DIM), 1.0)
    cache_diff_k = nrm((DEC_BATCH, DEPTH, PAST_LEN, DIFF_HEADS, 2, HEAD_DIM), 1.0)
    cache_diff_v = nrm((DEC_BATCH, DEPTH, PAST_LEN, DIFF_HEADS, DIFF_VDIM), 1.0)
    state_lru = nrm((DEC_BATCH, DEPTH, 2, LRU_WIDTH), 0.5)
    c = nrm((DEC_BATCH, D), 1.0)
    c_ctx = nrm((D,), 1.0)
    norm_mix = 1.0 + nrm((DEPTH, D), 0.02)
    norm_ffn = 1.0 + nrm((DEPTH, D), 0.02)
    w_mod = nrm((DEPTH, D, MOD_CHUNKS * D), 0.5 * D ** -0.5)
    b_mod = nrm((DEPTH, MOD_CHUNKS * D), 0.02)
    w_in = nrm((DEPTH, D, PROJ_W), D ** -0.5)
    w_out = nrm((DEPTH, MIX_W, D), MIX_W ** -0.5)
    qk_gain = 1.0 + nrm((DEPTH, 3, 2, HEAD_DIM), 0.02)
    swa_sink = nrm((DEPTH, SWA_HEADS), 0.5)
    na_rpb = nrm((DEPTH, NA_HEADS, 2 * NA_KH - 1, 2 * NA_KW - 1), 0.1)
    diff_lambda = nrm((DEPTH, 4, HEAD_DIM), 0.1)
    diff_subln = 1.0 + nrm((DEPTH, DIFF_VDIM), 0.02)
    conv_w = nrm((DEPTH, CONV_W, LRU_WIDTH), CONV_W ** -0.5)
    conv_b = nrm((DEPTH, LRU_WIDTH), 0.02)
    lru_wa = nrm((DEPTH, 2, LRU_BLOCKS, LRU_BLOCK_W, LRU_BLOCK_W), LRU_BLOCK_W ** -0.5)
    lru_ba = nrm((DEPTH, 2, LRU_WIDTH), 0.02)
    lru_wx = nrm((DEPTH, 2, LRU_BLOCKS, LRU_BLOCK_W, LRU_BLOCK_W), LRU_BLOCK_W ** -0.5)
    lru_bx = nrm((DEPTH, 2, LRU_WIDTH), 0.02)
    a_c = jax.random.uniform(next(ks), (DEPTH, 2, LRU_WIDTH), f32, 0.9, 0.999)
    a_base = a_c ** (1.0 / LRU_C)
    lru_L = jnp.log(a_base) - jnp.log1p(-a_base)
    w_ffn_gate = nrm((DEPTH, D, FFN_HIDDEN), D ** -0.5)
    w_ffn_up = nrm((DEPTH, D, FFN_HIDDEN), D ** -0.5)
    w_ffn_down = nrm((DEPTH, FFN_HIDDEN, D), FFN_HIDDEN ** -0.5)
    return {'x_prompt': x_prompt, 'x_sample': x_sample,
            'cache_swa_k': cache_swa_k, 'cache_swa_v': cache_swa_v,
            'cache_na_k': cache_na_k, 'cache_na_v': cache_na_v,
            'cache_diff_k': cache_diff_k, 'cache_diff_v': cache_diff_v,
            'state_lru': state_lru, 'c': c, 'c_ctx': c_ctx,
            'norm_mix': norm_mix, 'norm_ffn': norm_ffn, 'w_mod': w_mod, 'b_mod': b_mod,
            'w_in': w_in, 'w_out': w_out, 'qk_gain': qk_gain, 'swa_sink': swa_sink,
            'na_rpb': na_rpb, 'diff_lambda': diff_lambda, 'diff_subln': diff_subln,
            'conv_w': conv_w, 'conv_b': conv_b, 'lru_wa': lru_wa, 'lru_ba': lru_ba,
            'lru_wx': lru_wx, 'lru_bx': lru_bx, 'lru_L': lru_L,
            'w_ffn_gate': w_ffn_gate, 'w_ffn_up': w_ffn_up, 'w_ffn_down': w_ffn_down}


def reference(x_prompt, x_sample, cache_swa_k, cache_swa_v, cache_na_k, cache_na_v,
              cache_diff_k, cache_diff_v, state_lru, c, c_ctx,
              norm_mix, norm_ffn, w_mod, b_mod, w_in, w_out, qk_gain, swa_sink,
              na_rpb, diff_lambda, diff_subln, conv_w, conv_b, lru_wa, lru_ba,
              lru_wx, lru_bx, lru_L, w_ffn_gate, w_ffn_up, w_ffn_down):
    xc = x_prompt
    bc, s_ctx, _ = xc.shape
    swa_k_l, swa_v_l, na_k_l, na_v_l, diff_k_l, diff_v_l, lru_s_l = [], [], [], [], [], [], []
    for l in range(DEPTH):
        lam_init = _lambda_init(l)
        sh1, sc1, g1, sh2, sc2, g2 = _modulation(c_ctx, w_mod[l], b_mod[l])
        h = _modnorm(xc, norm_mix[l], sh1, sc1)
        sq, sk, sv, nq, nk, nv, dq, dk, dv, lx, lg = _project(h, w_in[l], qk_gain[l])
        oa = _dense_attn(sq, sk, sv, swa_sink[l].reshape(SWA_KV_HEADS, SWA_GROUP))
        ob = _dense_attn(nq[:, :, :, None], nk, nv, None)
        lam = _diff_lambda(diff_lambda[l], lam_init)
        oc = _diff_out(_diff_attn(dq, dk, dv, lam), diff_subln[l], lam_init)
        h0 = jnp.zeros((bc, 2, LRU_WIDTH), xc.dtype)
        od, st = _lru_mixer(lx, lg, conv_w[l], conv_b[l], lru_wa[l], lru_ba[l],
                            lru_wx[l], lru_bx[l], lru_L[l], h0)
        mix = jnp.concatenate([oa.reshape(bc, s_ctx, -1), ob.reshape(bc, s_ctx, -1), oc, od], axis=-1)
        xc = xc + g1 * (mix @ w_out[l])
        xc = xc + g2 * _swiglu(_modnorm(xc, norm_ffn[l], sh2, sc2),
                               w_ffn_gate[l], w_ffn_up[l], w_ffn_down[l])
        swa_k_l.append(sk)
        swa_v_l.append(sv)
        na_k_l.append(nk)
        na_v_l.append(nv)
        diff_k_l.append(dk)
        diff_v_l.append(dv)
        lru_s_l.append(st)
    y_prompt = xc
    new_swa_k = jnp.stack(swa_k_l, axis=1)
    new_swa_v = jnp.stack(swa_v_l, axis=1)
    new_na_k = jnp.stack(na_k_l, axis=1)
    new_na_v = jnp.stack(na_v_l, axis=1)
    new_diff_k = jnp.stack(diff_k_l, axis=1)
    new_diff_v = jnp.stack(diff_v_l, axis=1)
    new_state_lru = jnp.stack(lru_s_l, axis=1)

    xs = x_sample
    bd, t_lat, _ = xs.shape
    ang = _axial_angles(t_lat)
    for l in range(DEPTH):
        lam_init = _lambda_init(l)
        sh1, sc1, g1, sh2, sc2, g2 = [m[:, None, :] for m in _modulation(c, w_mod[l], b_mod[l])]
        h = _modnorm(xs, norm_mix[l], sh1, sc1)
        sq, sk, sv, nq, nk, nv, dq, dk, dv, lx, lg = _project(h, w_in[l], qk_gain[l])
        oa = _swa_latent(_rope(sq, ang), _rope(sk, ang), sv, cache_swa_k[:, l], cache_swa_v[:, l],
                         swa_sink[l].reshape(SWA_KV_HEADS, SWA_GROUP))
        ob = _na_latent(nq, nk, nv, cache_na_k[:, l], cache_na_v[:, l], na_rpb[l])
        k_all = jnp.concatenate([_rope(dk, ang), cache_diff_k[:, l]], axis=1)
        v_all = jnp.concatenate([dv, cache_diff_v[:, l]], axis=1)
        lam = _diff_lambda(diff_lambda[l], lam_init)
        oc = _diff_out(_diff_attn(_rope(dq, ang), k_all, v_all, lam), diff_subln[l], lam_init)
        od, _ = _lru_mixer(lx, lg, conv_w[l], conv_b[l], lru_wa[l], lru_ba[l],
                           lru_wx[l], lru_bx[l], lru_L[l], state_lru[:, l])
        mix = jnp.concatenate([oa.reshape(bd, t_lat, -1), ob.reshape(bd, t_lat, -1), oc, od], axis=-1)
        xs = xs + g1 * (mix @ w_out[l])
        xs = xs + g2 * _swiglu(_modnorm(xs, norm_ffn[l], sh2, sc2),
                               w_ffn_gate[l], w_ffn_up[l], w_ffn_down[l])
    y_sample = xs
    return (y_prompt, y_sample, new_swa_k, new_swa_v, new_na_k, new_na_v,
            new_diff_k, new_diff_v, new_state_lru)
```

```python
import numpy as np
import concourse.bass as bass
import concourse.mybir as mybir

F32 = mybir.dt.float32
BF16 = mybir.dt.bfloat16
AF = mybir.ActivationFunctionType
ALU = mybir.AluOpType


class _I:
    __slots__ = ("eng", "fn", "deps", "dsem", "signal", "semval", "idx", "total")


def _isz(dt):
    return 2 if dt == BF16 else 4


class Prog:
    ENGS = ("pe", "act", "dve", "pool", "sp")

    def __init__(self, nc):
        self.nc = nc
        self.ins = []
        self.trk = {}
        self.dma_cnt = {}
        self.out_dmas = []

    def _iv(self, ap):
        sp = str(ap.space)
        if "DRAM" in sp.upper():
            return None
        steps = ap.ap
        rs = steps[0][0]
        off = ap.offset
        p0 = off // rs if rs > 0 else 0
        f0 = off - p0 * rs
        lo = f0
        hi = f0
        for st, cnt in steps[1:]:
            ext = st * (cnt - 1)
            if ext < 0:
                lo += ext
            else:
                hi += ext
        isz = _isz(ap.dtype)
        return ((sp, ap.tensor.name), lo * isz, (hi + 1) * isz, p0, p0 + steps[0][1])

    def _rec(self, eng, fn, reads, writes, dsem=None, total=False):
        ins = _I()
        ins.eng = eng
        ins.fn = fn
        ins.dsem = dsem
        ins.signal = False
        ins.semval = 0
        ins.total = total
        ins.idx = len(self.ins)
        deps = set()
        acc = []
        for ap in reads:
            if ap is None or isinstance(ap, (int, float)):
                continue
            iv = self._iv(ap)
            if iv is not None:
                acc.append((iv, False))
        for ap in writes:
            iv = self._iv(ap)
            if iv is not None:
                acc.append((iv, True))
        for (key, lo, hi, plo, phi), isw in acc:
            if "PSUM" in key[0].upper():
                lo, hi, plo, phi, isw = 0, 2048, 0, 128, True
            recs = self.trk.setdefault(key, [])
            keep = []
            for r in recs:
                rlo, rhi, rplo, rphi, ridx, rw = r
                if ridx == ins.idx:
                    keep.append(r)
                    continue
                ov = (rlo < hi and lo < rhi and rplo < phi and plo < rphi)
                if ov and (rw or isw):
                    deps.add(ridx)
                if isw and ov and rlo >= lo and rhi <= hi and rplo >= plo and rphi <= phi:
                    continue
                if (not isw) and (not rw) and rlo == lo and rhi == hi and rplo == plo and rphi == phi \
                        and self.ins[ridx].eng == eng and self.ins[ridx].dsem is None and dsem is None:
                    continue
                keep.append(r)
            keep.append((lo, hi, plo, phi, ins.idx, isw))
            self.trk[key] = keep
        deps.discard(ins.idx)
        best = {}
        red = set()
        for d_ in deps:
            p_ = self.ins[d_]
            if p_.dsem is not None:
                red.add(d_)
            elif best.get(p_.eng, -1) < d_:
                best[p_.eng] = d_
        red.update(best.values())
        deps = red
        ins.deps = deps
        self.ins.append(ins)
        if dsem is not None:
            self.dma_cnt[dsem] = self.dma_cnt.get(dsem, 0) + 16
            ins.semval = self.dma_cnt[dsem]
        return ins

    def mm(self, out, lhsT, rhs, first):
        self._rec("pe", lambda e: e.matmul(out, lhsT, rhs, start=bool(first), stop=True,
                                           skip_group_check=True), [lhsT, rhs], [out])

    def tr(self, out, in_, ident):
        self._rec("pe", lambda e: e.transpose(out, in_, ident), [in_, ident], [out])

    def act(self, out, in_, func, bias=None, scale=None, eng="act"):
        kw = {}
        if bias is not None:
            kw["bias"] = bias
        if scale is not None:
            kw["scale"] = scale
        rd = [in_]
        if bias is not None and not isinstance(bias, (int, float)):
            rd.append(bias)
        if scale is not None and not isinstance(scale, (int, float)):
            rd.append(scale)
        self._rec(eng, lambda e: e.activation(out, in_, func, **kw), rd, [out])

    def tt(self, out, a, b, op, eng="dve"):
        self._rec(eng, lambda e: e.tensor_tensor(out, a, b, op), [a, b], [out])

    def ts(self, out, a, s1, op0, s2=None, op1=None, eng="dve"):
        rd = [a]
        if not isinstance(s1, (int, float)):
            rd.append(s1)
        if s2 is not None and not isinstance(s2, (int, float)):
            rd.append(s2)
        if op1 is None:
            self._rec(eng, lambda e: e.tensor_scalar(out, a, s1, None, op0), rd, [out])
        else:
            self._rec(eng, lambda e: e.tensor_scalar(out, a, s1, s2, op0, op1), rd, [out])

    def stt(self, out, a, s, b, op0, op1):
        rd = [a, b]
        if not isinstance(s, (int, float)):
            rd.append(s)
        self._rec("dve", lambda e: e.scalar_tensor_tensor(out, a, s, b, op0, op1), rd, [out])

    def scan(self, out, d0, d1, init):
        rd = [d0, d1]
        if not isinstance(init, (int, float)):
            rd.append(init)
        self._rec("dve", lambda e: e.tensor_tensor_scan(out, d0, d1, init, ALU.mult, ALU.add), rd, [out])

    def copy(self, out, in_, eng="dve"):
        if eng == "act":
            self._rec(eng, lambda e: e.copy(out, in_), [in_], [out])
        else:
            self._rec(eng, lambda e: e.tensor_copy(out, in_), [in_], [out])

    def memset(self, out, val, eng="dve"):
        self._rec(eng, lambda e: e.memset(out, val), [], [out])

    def recip(self, out, in_):
        self._rec("dve", lambda e: e.reciprocal(out, in_), [in_], [out])

    def dma(self, out, in_, sem, q="sp", total=False, **kw):
        ins = self._rec(q, lambda e: e.dma_start(out, in_, **kw), [in_], [out], dsem=sem, total=total)
        if "DRAM" in str(out.space).upper():
            self.out_dmas.append(ins)
        return ins

    def emit(self):
        nc = self.nc
        ins = self.ins
        for i in ins:
            for d in i.deps:
                p = ins[d]
                if p.dsem is not None:
                    continue
                if p.eng == i.eng and i.dsem is None:
                    if p.eng == "pe":
                        continue
                p.signal = True
        cnt = {e: 0 for e in self.ENGS}
        for i in ins:
            if i.dsem is None and i.signal:
                cnt[i.eng] += 1
                i.semval = cnt[i.eng]
        names = sorted(self.dma_cnt.keys())
        import contextlib
        with contextlib.ExitStack() as st:
            esem = {e: st.enter_context(nc.semaphore("s_" + e)) for e in self.ENGS}
            dsem = {n: st.enter_context(nc.semaphore("d_" + n)) for n in names}
            block = st.enter_context(nc.Block())
            per = {e: [i for i in ins if i.eng == e] for e in self.ENGS}
            final_waits = {}
            for i in self.out_dmas:
                final_waits[i.dsem] = self.dma_cnt[i.dsem]

            def gen(ename, last_waits=None):
                def body(e):
                    seen = {}
                    for i in per[ename]:
                        need = {}
                        for d in i.deps:
                            p = ins[d]
                            if p.dsem is not None:
                                s = ("d", p.dsem)
                                v = self.dma_cnt[p.dsem] if p.total else p.semval
                            else:
                                if p.eng == i.eng and i.dsem is None and p.eng == "pe":
                                    continue
                                if not p.signal:
                                    continue
                                s = ("e", p.eng)
                                v = p.semval
                            if need.get(s, 0) < v:
                                need[s] = v
                        for s, v in need.items():
                            if seen.get(s, 0) >= v:
                                continue
                            seen[s] = v
                            e.wait_ge(dsem[s[1]] if s[0] == "d" else esem[s[1]], v)
                        h = i.fn(e)
                        if i.dsem is not None:
                            h.then_inc(dsem[i.dsem], 16)
                        elif i.signal:
                            h.then_inc(esem[i.eng], 1)
                    if last_waits:
                        for n, v in last_waits.items():
                            e.wait_ge(dsem[n], v)
                return body

            block.tensor(gen("pe"))
            block.scalar(gen("act"))
            block.vector(gen("dve"))
            block.gpsimd(gen("pool"))
            block.sync(gen("sp", final_waits))
        return cnt

import contextlib
import math
from concourse.bass_utils import run_bass_kernel_spmd

D = 2048
T = 1024
HID = 5632
PW = 4864
O_SQ, O_SK, O_SV, O_NQ, O_NK, O_NV, O_DQ, O_DK, O_DV, O_LX, O_LG = 0, 512, 640, 768, 1280, 1792, 2304, 2816, 3328, 3840, 4352
EPS = 1e-6


_DBG = {"on": False, "stage": 99, "sub": 99}


def build_nc():
    nc = bass.Bass("TRN2", target_bir_lowering=False)
    SUB = _DBG["sub"]

    def dbig(name, shape):
        if _DBG["on"]:
            return nc.dram_tensor(name, list(shape), F32, kind="Internal").ap()
        return nc.dram_tensor(name, list(shape), F32, kind="ExternalInput").ap()

    def din(name, shape):
        return nc.dram_tensor(name, list(shape), F32, kind="ExternalInput").ap()

    def dout(name, shape):
        return nc.dram_tensor(name, list(shape), F32, kind="ExternalOutput").ap()

    x_d = din("x", [T, D]); cvec_d = din("cvec", [D]); flags_d = din("flags", [128, 2])
    c_swa_k = din("c_swa_k", [2, 256, 128]); c_swa_v = din("c_swa_v", [2, 256, 128])
    c_na_k = din("c_na_k", [2, 256, 512]); c_na_v = din("c_na_v", [2, 256, 512])
    c_diff_k = din("c_diff_k", [2, 256, 512]); c_diff_v = din("c_diff_v", [2, 256, 512])
    state_d = din("state", [2, 2, 512])
    w_mod = dbig("w_mod", [2, D, 6 * D]); b_mod = din("b_mod", [2, 6 * D])
    w_in = dbig("w_in", [2, D, PW]); w_out = dbig("w_out", [2, D, D])
    wg = dbig("wg", [2, D, HID]); wu = dbig("wu", [2, D, HID]); wd = dbig("wd", [2, HID, D])
    norm_mix = din("norm_mix", [2, D]); norm_ffn = din("norm_ffn", [2, D])
    qk_gain = din("qk_gain", [2, 6, 64]); swa_sink = din("swa_sink", [2, 8])
    rpbrev = din("rpbrev", [2, 8, 15, 127]); dlam = din("dlam", [2, 256]); dsub = din("dsub", [2, 128])
    conv_w = din("conv_w", [2, 4, 512]); conv_b = din("conv_b", [2, 512])
    lru_wa = din("lru_wa", [2, 2, 8, 64, 64]); lru_ba = din("lru_ba", [2, 2, 512])
    lru_wx = din("lru_wx", [2, 2, 8, 64, 64]); lru_bx = din("lru_bx", [2, 2, 512]); lru_L = din("lru_L", [2, 2, 512])
    cos_d = din("cos_t", [128, T]); sin_d = din("sin_t", [128, T])
    cmat_d = din("cmat", [128, 6, 128]); jmw_d = din("jmw", [64, 2, 128]); negc_d = din("negc", [128, 64]); sel_d = din("sel", [128, 2, 128]); band_d = din("band", [128, 384])

    y_d = dout("y", [T, D])
    o_swa_k = dout("o_swa_k", [2, T, 128]); o_swa_v = dout("o_swa_v", [2, T, 128])
    o_na_k = dout("o_na_k", [2, T, 512]); o_na_v = dout("o_na_v", [2, T, 512])
    o_diff_k = dout("o_diff_k", [2, T, 512]); o_diff_v = dout("o_diff_v", [2, T, 512])
    o_state = dout("o_state", [2, 4, 2, 512])

    st = contextlib.ExitStack()
    NW = 53000
    A = st.enter_context(nc.sbuf_tensor("arena", [128, NW], F32))
    PS = [st.enter_context(nc.psum_tensor("ps%d" % i, [128, 512], F32)) for i in range(8)]
    P = Prog(nc)
    top = [0]
    marks = []

    def al(n):
        o = top[0]
        top[0] += n
        assert top[0] <= NW, top[0]
        return o

    def f32v(o, n):
        return A[:, o:o + n]

    def bfv(o, nbf):
        return A[:, o:o + (nbf + 1) // 2].bitcast(BF16)

    rr = [0, 0]

    wide = [True]

    rrole = {"p": [0, (0, 1, 2, 3)], "m": [0, (4, 5)], "r": [0, (6,)], "t": [0, (7,)]}

    def rb(role=None):
        if role is not None:
            st_ = rrole[role]
            st_[0] += 1
            return PS[st_[1][st_[0] % len(st_[1])]]
        rr[0] += 1
        if wide[0]:
            return PS[rr[0] % 8]
        return PS[4 + rr[0] % 4]

    def ab():
        rr[1] += 1
        return PS[rr[1] % 4]

    XF = f32v(al(16 * T), 16 * T).rearrange("p (c t) -> p c t", c=16)
    HB = bfv(al(8 * T), 16 * T).rearrange("p (c t) -> p c t", c=16)
    RING = [al(2048) for _ in range(3)]
    wk = [0]

    def wslab(src, shape):
        s = wk[0] % 3
        wk[0] += 1
        n = 1
        for d_ in shape[1:]:
            n *= d_
        v = bfv(RING[s], 4096)[0:shape[0], 0:n]
        if len(shape) == 3:
            v = v.rearrange("p (a b) -> p a b", a=shape[1])
        if _DBG.get("tinyw") and len(shape) == 3:
            P.dma(v[:, 0:1, :], src[:, 0:1, :], "w%d" % s, q="pool")
        else:
            P.dma(v, src, "w%d" % s, q="pool")
        return v

    COS = f32v(al(T), T); SIN = f32v(al(T), T)
    P.dma(COS, cos_d, "c0", total=True); P.dma(SIN, sin_d, "c0", total=True)
    stg0 = 46000
    cm32 = stg0
    P.dma(f32v(cm32, 768).rearrange("p (k n) -> p k n", k=6), cmat_d, "c0", total=True)
    cmb = al(384)
    CM = bfv(cmb, 768).rearrange("p (k n) -> p k n", k=6)
    P.copy(CM, f32v(cm32, 768).rearrange("p (k n) -> p k n", k=6))
    IDENT, RMAT, BONES, ONESD, ONES128, ONES1 = [CM[:, k, :] for k in range(6)]
    j32 = stg0 + 768; P.dma(A[0:64, j32:j32 + 256].rearrange("p (k n) -> p k n", k=2), jmw_d, "c0", total=True)
    JMW = bfv(al(128), 256)[0:64, :].rearrange("p (k n) -> p k n", k=2); P.copy(JMW, A[0:64, j32:j32 + 256].rearrange("p (k n) -> p k n", k=2))
    s32 = stg0 + 1024; P.dma(f32v(s32, 256).rearrange("p (k n) -> p k n", k=2), sel_d, "c0", total=True)
    SELb = bfv(al(128), 256).rearrange("p (k n) -> p k n", k=2); P.copy(SELb, f32v(s32, 256).rearrange("p (k n) -> p k n", k=2))
    NEGC = f32v(al(64), 64); P.dma(NEGC, negc_d, "c0", total=True)
    ONEC = f32v(al(1), 1); P.memset(ONEC, 1.0)
    b32 = stg0 + 1280; P.dma(f32v(b32, 384), band_d, "c0", total=True)
    BAND = bfv(al(192), 384); P.copy(BAND, f32v(b32, 384))
    FL = f32v(al(2), 2); P.dma(FL, flags_d, "c0", total=True)
    FP_, FS_ = FL[:, 0:1], FL[:, 1:2]
    EPSC = f32v(al(1), 1); P.memset(EPSC, EPS)
    GN1 = f32v(al(32), 32).rearrange("p (l c) -> p l c", l=2)
    GN2 = f32v(al(32), 32).rearrange("p (l c) -> p l c", l=2)
    BMOD = f32v(al(192), 192).rearrange("p (l c) -> p l c", l=2)
    P.dma(GN1, norm_mix.rearrange("l (c p) -> p l c", p=128), "c0", total=True, allow_slow_non_contiguous=True)
    P.dma(GN2, norm_ffn.rearrange("l (c p) -> p l c", p=128), "c0", total=True, allow_slow_non_contiguous=True)
    P.dma(BMOD, b_mod.rearrange("l (c p) -> p l c", p=128), "c0", total=True, allow_slow_non_contiguous=True)
    QKG = f32v(al(12), 12).rearrange("p (l k) -> p l k", l=2)
    for hh in range(2):
        P.dma(A[hh * 64:(hh + 1) * 64, QKG.offset % NW:QKG.offset % NW + 12].rearrange("p (l k) -> p l k", l=2),
              qk_gain.rearrange("l k d -> d l k"), "c0", total=True, allow_slow_non_contiguous=True)
    SINK = f32v(al(16), 16).rearrange("p (l h) -> p l h", l=2)
    P.dma(SINK, bass.AP(swa_sink.tensor, 0, [[0, 128], [8, 2], [1, 8]]), "c0", total=True)
    ESINK = f32v(al(16), 16).rearrange("p (l h) -> p l h", l=2)
    P.act(ESINK, SINK, AF.Exp)
    DLAM = f32v(stg0 + 1664, 512).rearrange("p (l k) -> p l k", l=2)
    P.dma(DLAM, bass.AP(dlam.tensor, 0, [[0, 128], [256, 2], [1, 256]]), "c0", total=True)
    DSUB = f32v(al(2), 2)
    P.dma(DSUB, dsub.rearrange("l p -> p l"), "c0", total=True, allow_slow_non_contiguous=True)
    CW = f32v(al(32), 32).rearrange("p (l k c) -> p l k c", l=2, k=4)
    P.dma(CW, conv_w.rearrange("l k (c p) -> p l k c", p=128), "c0", total=True, allow_slow_non_contiguous=True)
    CB = f32v(al(8), 8).rearrange("p (l c) -> p l c", l=2)
    P.dma(CB, conv_b.rearrange("l (c p) -> p l c", p=128), "c0", total=True, allow_slow_non_contiguous=True)
    LBA = f32v(al(16), 16).rearrange("p (l d c) -> p l d c", l=2, d=2)
    LBX = f32v(al(16), 16).rearrange("p (l d c) -> p l d c", l=2, d=2)
    LL = f32v(al(16), 16).rearrange("p (l d c) -> p l d c", l=2, d=2)
    P.dma(LBA, lru_ba.rearrange("l d (c p) -> p l d c", p=128), "c0", total=True, allow_slow_non_contiguous=True)
    P.dma(LBX, lru_bx.rearrange("l d (c p) -> p l d c", p=128), "c0", total=True, allow_slow_non_contiguous=True)
    P.dma(LL, lru_L.rearrange("l d (c p) -> p l d c", p=128), "c0", total=True, allow_slow_non_contiguous=True)
    H0S = f32v(al(16), 16).rearrange("p (l d c) -> p l d c", l=2, d=2)
    P.dma(H0S, state_d.rearrange("l d (c p) -> p l d c", p=128), "c0", total=True, allow_slow_non_contiguous=True)
    CL = f32v(al(16), 16).rearrange("p (l d c) -> p l d c", l=2, d=2)
    P.act(CL, LL, AF.Exp, scale=-1.0)
    P.act(CL, CL, AF.Ln, bias=ONEC)
    P.ts(CL, CL, -8.0, ALU.mult)
    LAMV = f32v(al(8), 8)
    dl_t = f32v(stg0 + 2176, 128)
    for l in range(2):
        li = 0.8 - 0.6 * math.exp(-0.3 * l)
        sacc = LAMV[:, 4 * l:4 * l + 2]
        P.tt(dl_t.rearrange("p (a d) -> p a d", a=2), DLAM[:, l, :].rearrange("p (a b d) -> p a b d", a=2, b=2)[:, :, 0, :],
             DLAM[:, l, :].rearrange("p (a b d) -> p a b d", a=2, b=2)[:, :, 1, :], ALU.mult)
        P._rec("dve", (lambda o_, i_: (lambda e: e.reduce_sum(o_, i_, axis=mybir.AxisListType.X)))(sacc, dl_t.rearrange("p (a d) -> p a d", a=2)),
               [dl_t], [sacc])
        P.act(sacc, sacc, AF.Exp)
        P.tt(LAMV[:, 4 * l + 2:4 * l + 3], sacc[:, 0:1], sacc[:, 1:2], ALU.subtract)
        P.ts(LAMV[:, 4 * l + 3:4 * l + 4], LAMV[:, 4 * l + 2:4 * l + 3], li, ALU.add, -1.0, ALU.mult)
    DSC = f32v(al(4), 4).rearrange("p (l s) -> p l s", l=2)
    for l in range(2):
        li = 0.8 - 0.6 * math.exp(-0.3 * l)
        P.ts(DSC[:, l, 0:1], DSUB[:, l:l + 1], 1.0 - li, ALU.mult, FP_, ALU.mult)
        P.ts(DSC[:, l, 1:2], DSUB[:, l:l + 1], 1.0 - li, ALU.mult, FS_, ALU.mult)
    MODV = f32v(al(192), 192).rearrange("p (l c) -> p l c", l=2)
    AB = f32v(al(128), 128).rearrange("p (l k c) -> p l k c", l=2, k=4)
    CV = f32v(al(16), 16)
    P.dma(CV, cvec_d.rearrange("(c p) -> p c", p=128), "c0", total=True, allow_slow_non_contiguous=True)
    SCV = bfv(al(8), 16)
    P.act(SCV, CV, AF.Silu)
    static_top = top[0]

    m0 = top[0]
    xs = [al(2048), al(2048)]
    xh = bfv(al(1024), 2048); xl = bfv(al(1024), 2048)
    for i in range(8):
        xt = f32v(xs[i % 2], 2048)
        P.dma(xt, x_d[i * 128:(i + 1) * 128, :], "x%d" % (i % 2))
        P.copy(xh, xt, eng="act")
        P.tt(xl, xt, xh, ALU.subtract)
        for g in range(4):
            ps = rb()
            for j in range(4):
                c = g * 4 + j
                P.mm(ps[:, j * 128:(j + 1) * 128], xh[:, c * 128:(c + 1) * 128], IDENT, j == 0)
                P.mm(ps[:, j * 128:(j + 1) * 128], xl[:, c * 128:(c + 1) * 128], IDENT, False)
            P.copy(XF[:, g * 4:(g + 1) * 4, i * 128:(i + 1) * 128], ps[:, :].rearrange("p (j n) -> p j n", j=4),
                   eng=("act" if g % 2 else "dve"))
    top[0] = m0

    modq = [(l_, s_) for l_ in range(2) for s_ in range(48)]
    modpos = [0]

    def mod_step(n=1):
        for _ in range(n):
            if modpos[0] >= len(modq):
                return
            l_, s_ = modq[modpos[0]]
            modpos[0] += 1
            W = wslab(w_mod[l_].rearrange("(kc p) n -> p kc n", p=128)[:, :, s_ * 256:(s_ + 1) * 256], [128, 16, 256])
            ps = rb("m")
            for j in range(2):
                for kc in range(16):
                    P.mm(ps[:, j:j + 1], W[:, kc, j * 128:(j + 1) * 128], SCV[:, kc:kc + 1], (j == 0 and kc == 0))
            P.tt(MODV[:, l_, 2 * s_:2 * s_ + 2], ps[:, 0:2], BMOD[:, l_, 2 * s_:2 * s_ + 2], ALU.add)

    def mod_require(l_, upto):
        while modpos[0] < l_ * 48 + upto:
            mod_step(1)

    def mod_ab(l_, which):
        if which == 0:
            mod_require(l_, 16)
            P.stt(AB[:, l_, 0, :], MODV[:, l_, 16:32], 1.0, GN1[:, l_, :], ALU.add, ALU.mult)
            P.copy(AB[:, l_, 1, :], MODV[:, l_, 0:16])
        else:
            mod_require(l_, 40)
            P.stt(AB[:, l_, 2, :], MODV[:, l_, 64:80], 1.0, GN2[:, l_, :], ALU.add, ALU.mult)
            P.copy(AB[:, l_, 3, :], MODV[:, l_, 48:64])

    def modnorm(l, which):
        m = top[0]
        sqb = [bfv(al(256), 512) for _ in range(2)]
        RS = f32v(al(512), 512)
        tmp = [f32v(al(512), 512) for _ in range(2)]
        for t in range(2):
            ts_ = slice(t * 512, (t + 1) * 512)
            ms = rb()
            for c in range(16):
                P.act(sqb[c % 2], XF[:, c, ts_], AF.Square)
                P.mm(ms[:, :], ONESD, sqb[c % 2], c == 0)
            P.act(RS, ms[:, :], AF.Sqrt, bias=EPSC)
            P.recip(RS, RS)
            for c in range(16):
                P.tt(tmp[c % 2], XF[:, c, ts_], RS, ALU.mult)
                P.act(HB[:, c, ts_], tmp[c % 2], AF.Identity, bias=AB[:, l, 2 * which + 1, c:c + 1], scale=AB[:, l, 2 * which, c:c + 1])
        top[0] = m

    def proj_fm(l, col0, ncols, consume):
        for s0 in range(0, ncols, 256):
            nn = min(256, ncols - s0)
            W = wslab(w_in[l].rearrange("(kc p) n -> p kc n", p=128)[:, :, col0 + s0:col0 + s0 + nn], [128, 16, nn])
            for j in range(nn // 128):
                for t in range(2):
                    ps = rb("p")
                    for kc in range(16):
                        P.mm(ps[:, :], W[:, kc, j * 128:(j + 1) * 128], HB[:, kc, t * 512:(t + 1) * 512], kc == 0)
                    consume((s0 + j * 128) // 128, t, ps)
            mod_step()

    qpend = []
    qcnt = [0]

    def qk_advance():
        for u in list(qpend):
            u.pop(0)()
            if not u:
                qpend.remove(u)

    def qk_flush():
        while qpend:
            qk_advance()

    def qk_unit(l, ps, t, gidx, dst_n, dst_r, kout, scr):
        ts_ = slice(t * 512, (t + 1) * 512)
        n_ = qcnt[0]
        qcnt[0] += 1
        sqb = scr["sqb"][n_ % 2]; QN = scr["QN"][n_ % 2]; lo = scr["lo"][n_ % 2]
        sd, t1, t2 = scr["sd"], scr["t1"], scr["t2"]

        def c1():
            P.act(sqb, ps[:, :], AF.Square)
            ms = rb("m")
            P.mm(ms[:, :], BONES, sqb, True)
            P.act(sd, ms[:, :], AF.Sqrt, bias=EPSC)
            P.recip(sd, sd)
            P.stt(QN, ps[:, :], QKG[:, l, gidx:gidx + 1], sd, ALU.mult, ALU.mult)
            P.copy(dst_n[:, ts_], QN, eng="act")

        def c2():
            if dst_r is not None:
                rq = rb("r")
                P.mm(rq[:, :], RMAT, dst_n[:, ts_], True)
                P.tt(t1, QN, COS[:, ts_], ALU.mult)
                P.tt(t2, rq[:, :], SIN[:, ts_], ALU.mult)
                P.tt(dst_r[:, ts_], t1, t2, ALU.add)
            if kout is not None:
                P.tt(lo, QN, dst_n[:, ts_], ALU.subtract)

        def c3():
            if kout is not None:
                od, c0, stg = kout
                tp = rb("t")
                for j in range(4):
                    P.mm(tp[:, j * 128:(j + 1) * 128], dst_n[:, t * 512 + j * 128:t * 512 + (j + 1) * 128], IDENT, j == 0)
                    P.mm(tp[:, j * 128:(j + 1) * 128], lo[:, j * 128:(j + 1) * 128], IDENT, False)
                P.copy(stg, tp[:, :], eng="act")
                P.dma(od[l].rearrange("(j p) f -> p j f", p=128)[:, t * 4:(t + 1) * 4, c0:c0 + 128],
                      stg.rearrange("p (j f) -> p j f", j=4), "so%d" % t)
        qk_advance()
        qpend.append([c1, c2, c3])

    def proj_v(l, col0, ncols, Vb, od, stgs):
        for s0 in range(0, ncols, 256):
            nn = min(256, ncols - s0)
            W = wslab(w_in[l].rearrange("(kc p) n -> p kc n", p=128)[:, :, col0 + s0:col0 + s0 + nn], [128, 16, nn])
            for i in range(8):
                ps = rb()
                for kc in range(16):
                    P.mm(ps[:, 0:nn], HB[:, kc, i * 128:(i + 1) * 128], W[:, kc, :], kc == 0)
                P.copy(Vb[:, i, s0:s0 + nn], ps[:, 0:nn], eng="act")
                sg = stgs[i % 2]
                P.copy(sg[:, 0:nn], ps[:, 0:nn])
                P.dma(od[l, i * 128:(i + 1) * 128, s0:s0 + nn], sg[:, 0:nn], "sv%d" % (i % 2))
            mod_step()

    def attend(qfn, chunks, dv, finish, ptb, base=0):
        wide[0] = False
        jobs = []
        for ch in chunks:
            for t in range(2):
                lo = max(ch["qlo"], 512 * t); hi = min(ch["qhi"], 512 * (t + 1))
                if lo < hi:
                    jobs.append((ch, t, lo, hi))
        first = [True, True]

        def p1(k):
            ch, t, lo, hi = jobs[k]
            n = hi - lo
            a0 = lo - 512 * t
            sp = rb()
            if ch.get("parts") is not None:
                for ip, pr in enumerate(ch["parts"]):
                    P.mm(sp[:, pr["qlo"] - 512 * t:pr["qhi"] - 512 * t], pr["kT"], qfn(pr["qlo"], pr["qhi"]), ip == 0)
            else:
                P.mm(sp[:, a0:a0 + n], ch["kT"], qfn(lo, hi), True)
            if ch.get("bias") is not None:
                for sel, fn_ in ch["bias"]:
                    P.mm(sp[:, a0:a0 + n], sel, fn_(lo, hi), False)
            pt = ptb[k % len(ptb)]
            P.act(pt[:, 0:n], sp[:, a0:a0 + n], AF.Exp, scale=0.125)
            if ch.get("mask") is not None:
                mk = ch["mask"]
                P.tt(pt[:, 0:n], pt[:, 0:n], mk[:, lo - ch["qlo"]:hi - ch["qlo"]], ALU.mult)
            if ch.get("zero") is not None:
                p0, c0, c1 = ch["zero"]
                z0 = max(c0, lo); z1 = min(c1, hi)
                if z0 < z1:
                    P.memset(pt[p0:p0 + 64, z0 - lo:z1 - lo], 0.0)

        def p2(k):
            ch, t, lo, hi = jobs[k]
            n = hi - lo
            a0 = lo - 512 * t
            pt = ptb[k % len(ptb)]
            if ch.get("parts") is not None:
                for pr in ch["parts"]:
                    b0 = pr["qlo"] - 512 * t; b1 = pr["qhi"] - 512 * t
                    P.mm(PS[t][0:dv, b0:b1], pr["V"], pt[:, pr["qlo"] - lo:pr["qhi"] - lo], first[t])
                    P.mm(PS[2 + t][0:dv, b0:b1], ONES1[:, 0:dv], pt[:, pr["qlo"] - lo:pr["qhi"] - lo], first[t])
                    first[t] = False
                return
            P.mm(PS[t][0:dv, a0:a0 + n], ch["V"], pt[:, 0:n], first[t])
            P.mm(PS[2 + t][0:dv, a0:a0 + n], ONES1[:, 0:dv], pt[:, 0:n], first[t])
            first[t] = False

        jobs.sort(key=lambda jb: jb[1])
        nj = len(jobs)
        LA = 2
        last_of = {}
        for k, jb in enumerate(jobs):
            last_of[jb[1]] = k
        for k in range(min(LA, nj)):
            p1(k)
        for k in range(nj):
            if k + LA < nj:
                p1(k + LA)
            p2(k)
            if last_of[jobs[k][1]] == k:
                finish(jobs[k][1], PS[jobs[k][1]], PS[2 + jobs[k][1]])
        wide[0] = True

    def wout_group(l, tiles, K, row0):
        nk = len(tiles)
        cols = 4096 // nk
        mod_require(l, 24)
        for c0 in range(0, D, cols):
            W = wslab(w_out[l, row0:row0 + nk * K, c0:c0 + cols].rearrange("(h p) n -> p h n", p=K), [K, nk, cols])
            for j in range(cols // 128):
                c = (c0 // 128) + j
                for t in range(2):
                    acc = ab()
                    for i in range(nk):
                        P.mm(acc[:, :], W[:, i, j * 128:(j + 1) * 128], tiles[i][:, t * 512:(t + 1) * 512], i == 0)
                    P.stt(XF[:, c, t * 512:(t + 1) * 512], acc[:, :], MODV[:, l, 32 + c:33 + c], XF[:, c, t * 512:(t + 1) * 512], ALU.mult, ALU.add)
            mod_step()

    STAGE = _DBG["stage"]
    for l in range(2 if STAGE > 0 else 0):
        mod_ab(l, 0)
        modnorm(l, 0)
        if STAGE == 1:
            continue
        lay_mark = top[0]
        m = top[0]
        LXP = f32v(al(4 * 1036), 4 * 1036).rearrange("p (c s w) -> p c s w", c=4, s=4)
        P.memset(LXP, 0.0)
        Gb = bfv(al(2048), 4096).rearrange("p (c t) -> p c t", c=4)
        MIXL = bfv(al(2048), 4096).rearrange("p (c t) -> p c t", c=4)
        STB = f32v(al(32), 32).rearrange("p (s d c) -> p s d c", s=4, d=2)

        def lx_consume(c, t, ps):
            P.copy(LXP[:, c, 2 * t:2 * t + 2, 2:258], ps[:, :].rearrange("p (s w) -> p s w", s=2), eng="act")
        proj_fm(l, O_LX, 512, lx_consume)

        def lg_consume(c, t, ps):
            P.act(Gb[:, c, t * 512:(t + 1) * 512], ps[:, :], AF.Gelu_apprx_tanh)
        proj_fm(l, O_LG, 512, lg_consume)
        U = f32v(al(1024), 1024); UB = bfv(al(512), 1024)
        Rb = f32v(al(1024), 1024); Ib = f32v(al(1024), 1024); Ab = f32v(al(1024), 1024)
        Hb = [f32v(al(1024), 1024), f32v(al(1024), 1024)]
        WBD = bfv(al(256), 512).rearrange("p (k n) -> p k n", k=4)
        wst = f32v(al(512), 512).rearrange("p (k n) -> p k n", k=4)
        for c in range(4):
            P.ts(LXP[:, c, 1:4, 0:2], LXP[:, c, 0:3, 256:258], FS_, ALU.mult)
            P.ts(LXP[:, c, 0:3, 258:259], LXP[:, c, 1:4, 2:3], FS_, ALU.mult)
            U4 = U.rearrange("p (s w) -> p s w", s=4)
            P.act(U4, LXP[:, c, :, 0:256], AF.Identity, bias=CB[:, l, c:c + 1], scale=CW[:, l, 0, c:c + 1])
            for k in range(1, 4):
                P.stt(U4, LXP[:, c, :, k:k + 256], CW[:, l, k, c:c + 1], U4, ALU.mult, ALU.add)
            P.copy(UB, U, eng="act")
            P.memset(wst, 0.0)
            for d_ in range(2):
                for g_, wsrc in enumerate((lru_wa, lru_wx)):
                    for b_ in range(2):
                        P.dma(A[b_ * 64:(b_ + 1) * 64, wst.offset % NW + (d_ * 2 + g_) * 128 + b_ * 64: wst.offset % NW + (d_ * 2 + g_) * 128 + b_ * 64 + 64],
                              wsrc[l, d_, 2 * c + b_], "lw%d" % (d_ * 4 + g_ * 2 + b_))
            P.copy(WBD, wst)
            for d_ in range(2):
                for t in range(2):
                    ts_ = slice(t * 512, (t + 1) * 512)
                    pr = rb()
                    P.mm(pr[:, :], WBD[:, d_ * 2, :], UB[:, ts_], True)
                    P.act(Rb[:, ts_], pr[:, :], AF.Sigmoid, bias=LBA[:, l, d_, c:c + 1])
                    pi = rb()
                    P.mm(pi[:, :], WBD[:, d_ * 2 + 1, :], UB[:, ts_], True)
                    P.act(Ib[:, ts_], pi[:, :], AF.Sigmoid, bias=LBX[:, l, d_, c:c + 1])
                P.act(Ab, Rb, AF.Exp, scale=CL[:, l, d_, c:c + 1])
                P.act(Rb, Ab, AF.Square)
                P.ts(Rb, Rb, -1.0, ALU.mult, 1.0, ALU.add)
                P.act(Rb, Rb, AF.Sqrt)
                P.tt(Ib, Ib, U, ALU.mult)
                P.tt(Ib, Ib, Rb, ALU.mult)
                if d_ == 0:
                    P.ts(Ab[:, 256:1024:256], Ab[:, 256:1024:256], FS_, ALU.mult)
                    P.scan(Hb[0], Ab, Ib, H0S[:, l, 0, c:c + 1])
                else:
                    P.ts(Ab[:, 255:1023:256], Ab[:, 255:1023:256], FS_, ALU.mult)
                    P.scan(Hb[1][:, ::-1], Ab[:, ::-1], Ib[:, ::-1], H0S[:, l, 1, c:c + 1])
            P.copy(STB[:, :, 0, c], Hb[0][:, 255:1024:256])
            P.copy(STB[:, :, 1, c], Hb[1][:, 0:1024:256])
            P.tt(Hb[0], Hb[0], Hb[1], ALU.add)
            P.tt(MIXL[:, c, :], Hb[0], Gb[:, c, :], ALU.mult)
        P.dma(o_state[l].rearrange("s d (c p) -> p s d c", p=128), STB, "sst", allow_slow_non_contiguous=True)
        wout_group(l, [MIXL[:, c, :] for c in range(4)], 128, 1536)
        top[0] = m
        if STAGE == 2:
            continue

        def attn_mixer(kind):
            m = top[0]
            nq = 4
            nk = 1 if kind == "swa" else 4
            oq = {"swa": O_SQ, "na": O_NQ, "diff": O_DQ}[kind]
            ok = {"swa": O_SK, "na": O_NK, "diff": O_DK}[kind]
            ov = {"swa": O_SV, "na": O_NV, "diff": O_DV}[kind]
            nvc = 128 if kind == "swa" else 512
            odk = {"swa": o_swa_k, "na": o_na_k, "diff": o_diff_k}[kind]
            odv = {"swa": o_swa_v, "na": o_na_v, "diff": o_diff_v}[kind]
            ck = {"swa": c_swa_k, "na": c_na_k, "diff": c_diff_k}[kind]
            cv = {"swa": c_swa_v, "na": c_na_v, "diff": c_diff_v}[kind]
            gi = {"swa": 0, "na": 2, "diff": 4}[kind]
            rope = kind != "na"
            QN = [bfv(al(512), 1024) for _ in range(nq)]
            KN = [bfv(al(512), 1024) for _ in range(nk)]
            QR = [bfv(al(512), 1024) for _ in range(nq)] if rope else QN
            KR = [bfv(al(512), 1024) for _ in range(nk)] if rope else KN
            Vb = bfv(al(4 * nvc), 8 * nvc).rearrange("p (i n) -> p i n", i=8)
            CKF = bfv(al(128 * nk), 256 * nk).rearrange("p (c n) -> p c n", c=nk)
            CVb = bfv(al(nvc), 2 * nvc).rearrange("p (i n) -> p i n", i=2)
            scr = dict(sqb=[bfv(al(256), 512)], sd=f32v(al(512), 512), QN=[f32v(al(512), 512)], t1=f32v(al(512), 512), lo=[bfv(al(256), 512)])
            stg = [f32v(al(512), 512), f32v(al(512), 512)]
            ptb = [bfv(al(256), 512) for _ in range(3)]
            RI = f32v(al(512), 512)
            ex0 = top[0]
            scr["sqb"].append(bfv(al(256), 512)); scr["QN"].append(f32v(al(512), 512)); scr["lo"].append(bfv(al(256), 512)); scr["t2"] = f32v(al(512), 512)
            top[0] = ex0
            proj_fm(l, oq, 512, lambda c, t, ps: qk_unit(l, ps, t, gi, QN[c], QR[c] if rope else None, None, scr))
            if SUB == 1:
                qk_flush()
                top[0] = m
                return
            proj_fm(l, ok, 128 * nk, lambda c, t, ps: qk_unit(l, ps, t, gi + 1, KN[c], KR[c] if rope else None, (odk, c * 128, stg[t]), scr))
            qk_flush()
            if SUB == 2:
                top[0] = m
                return
            proj_v(l, ov, nvc, Vb, odv, stg)
            if SUB == 3:
                top[0] = m
                return
            for i in range(2):
                sg = stg[i]
                P.dma(sg[:, 0:nvc], cv[l, i * 128:(i + 1) * 128, :], "cl%d" % i)
                P.copy(CVb[:, i, :], sg[:, 0:nvc])
            ktm = bfv(al(256), 512)
            for i in range(2):
                sg = stg[i]
                P.dma(sg[:, 0:128 * nk], ck[l, i * 128:(i + 1) * 128, :], "cl%d" % i)
                P.copy(ktm[:, 0:128 * nk], sg[:, 0:128 * nk])
                tp = rb()
                for c in range(nk):
                    P.mm(tp[:, c * 128:(c + 1) * 128], ktm[:, c * 128:(c + 1) * 128], IDENT, c == 0)
                P.copy(CKF[:, :, i * 128:(i + 1) * 128], tp[:, 0:128 * nk].rearrange("p (c n) -> p c n", c=nk), eng="act")

            if SUB == 4:
                top[0] = m
                return
            if kind in ("swa", "na"):
                MT = [bfv(al(512), 1024) for _ in range(8)]
                TPR = None
                if kind == "na":
                    HK = f32v(al(960), 960)
                    HKb = bfv(al(480), 960)
                    TPR = bfv(al(480), 960)
                for h in range(8):
                    if kind == "swa":
                        qc, base = h % 4, (h // 4) * 64
                        kc_, kbase = 0, (h // 4) * 64
                        vsl = slice((h // 4) * 64, (h // 4) * 64 + 64)
                    else:
                        qc, base = h // 2, (h % 2) * 64
                        kc_, kbase = h // 2, (h % 2) * 64
                        vsl = slice(h * 64, h * 64 + 64)
                    bs = slice(base, base + 64)
                    es = ESINK[0:64, l, h:h + 1] if kind == "swa" else None
                    if kind == "na":
                        P.dma(HK[0:64, :].rearrange("p (e n) -> p e n", e=15),
                              bass.AP(rpbrev.tensor, (l * 8 + h) * 15 * 127, [[1, 64], [127, 15], [1, 64]]), "hk")
                        P.copy(HKb[0:64, :], HK[0:64, :])
                        for e0, en in ((0, 8), (8, 7)):
                            tq = rb()
                            P.mm(tq[:, 0:en * 64], JMW[:, h % 2, :], HKb[0:64, e0 * 64:(e0 + en) * 64], True)
                            ng = NEGC[bs, :]
                            P.stt(TPR[bs, e0 * 64:(e0 + en) * 64].rearrange("p (e n) -> p e n", e=en),
                                  tq[bs, 0:en * 64].rearrange("p (e n) -> p e n", e=en), 8.0,
                                  bass.AP(ng.tensor, ng.offset, [list(ng.ap[0]), [0, en], [1, 64]]), ALU.mult, ALU.add)
                    chunks = []
                    for t_ in range(2):
                        for kc in range(2):
                            parts = []
                            for s in (2 * t_, 2 * t_ + 1):
                                k0 = s * 256 + kc * 128
                                parts.append(dict(kT=KN[kc_][bs, k0:k0 + 128], V=Vb[:, 2 * s + kc, vsl], qlo=s * 256, qhi=s * 256 + 256))
                            chunks.append(dict(parts=parts, qlo=t_ * 512, qhi=t_ * 512 + 512))

                    def fin_p(t, o, d, h=h, es=es):
                        if es is not None:
                            P.ts(RI[0:64, :], d[0:64, :], es, ALU.add)
                            P.recip(RI[0:64, :], RI[0:64, :])
                        else:
                            P.recip(RI[0:64, :], d[0:64, :])
                        P.stt(MT[h][0:64, t * 512:(t + 1) * 512], o[0:64, :], FP_[0:64, :], RI[0:64, :], ALU.mult, ALU.mult)
                    if SUB == 5:
                        continue
                    attend(lambda lo, hi, qc=qc, bs=bs: QN[qc][bs, lo:hi], chunks, 64, fin_p, ptb)
                    if SUB == 6:
                        continue
                    chunks = []
                    for i in range(2):
                        chunks.append(dict(kT=CKF[bs, kc_, i * 128:(i + 1) * 128], V=CVb[:, i, vsl], qlo=0, qhi=1024))
                    if kind == "swa":
                        for j in range(8):
                            qlo = max(0, j - 1) * 128; qhi = min(8, j + 2) * 128
                            mk = BAND[:, (128 if j == 0 else 0):(128 if j == 0 else 0) + (qhi - qlo)]
                            chunks.append(dict(kT=KR[0][bs, j * 128:(j + 1) * 128], V=Vb[:, j, vsl], qlo=qlo, qhi=qhi, mask=mk))
                    else:
                        for j in range(8):
                            if j <= 3:
                                r0, r1 = 0, 2 * j + 5
                                zero = (0, (2 * j + 5) * 64, (2 * j + 6) * 64)
                            else:
                                r0, r1 = 2 * j - 3, 15
                                zero = (64, (2 * j - 3) * 64, (2 * j - 2) * 64)
                            bias = []
                            for i in range(2):
                                kr = 2 * j + i
                                bias.append((SELb[bs, i, :], (lambda lo, hi, kr=kr, bs=bs: TPR[bs, (lo // 64 - kr + 7) * 64:(hi // 64 - kr + 7) * 64])))
                            chunks.append(dict(kT=KR[kc_][bs, j * 128:(j + 1) * 128], V=Vb[:, j, vsl], qlo=r0 * 64, qhi=(r1 + 1) * 64,
                                               bias=bias, zero=zero))

                    def fin_s(t, o, d, h=h, es=es):
                        if es is not None:
                            P.ts(RI[0:64, :], d[0:64, :], es, ALU.add)
                            P.recip(RI[0:64, :], RI[0:64, :])
                        else:
                            P.recip(RI[0:64, :], d[0:64, :])
                        P.stt(scr["t1"][0:64, :], o[0:64, :], FS_[0:64, :], RI[0:64, :], ALU.mult, ALU.mult)
                        P.tt(MT[h][0:64, t * 512:(t + 1) * 512], MT[h][0:64, t * 512:(t + 1) * 512], scr["t1"][0:64, :], ALU.add)
                    attend(lambda lo, hi, qc=qc, bs=bs: QR[qc][bs, lo:hi], chunks, 64, fin_s, ptb)
                if SUB in (5, 6, 7):
                    top[0] = m
                    return
                row0 = 0 if kind == "swa" else 512
                if kind == "swa":
                    order = []
                    for h0 in range(8):
                        order.append(MT[h0])
                    wout_group(l, [MT[hh][0:64, :] for hh in range(8)], 64, row0)
                else:
                    wout_group(l, [MT[hh][0:64, :] for hh in range(8)], 64, row0)
            else:
                MD = QN
                D1 = f32v(al(1024), 1024); D2 = f32v(al(1024), 1024)
                for h in range(4):
                    for style in range(2):
                        QQ, KK = (QN, KN) if style == 0 else (QR, KR)
                        for i2 in range(2):
                            bs = slice(i2 * 64, i2 * 64 + 64)
                            Dd = D1 if i2 == 0 else D2
                            chunks = []
                            if style == 0:
                                for t_ in range(2):
                                    for kc in range(2):
                                        parts = []
                                        for s in (2 * t_, 2 * t_ + 1):
                                            k0 = s * 256 + kc * 128
                                            parts.append(dict(kT=KK[h][bs, k0:k0 + 128], V=Vb[:, 2 * s + kc, h * 128:(h + 1) * 128], qlo=s * 256, qhi=s * 256 + 256))
                                        chunks.append(dict(parts=parts, qlo=t_ * 512, qhi=t_ * 512 + 512))
                            else:
                                for j in range(8):
                                    chunks.append(dict(kT=KK[h][bs, j * 128:(j + 1) * 128], V=Vb[:, j, h * 128:(h + 1) * 128], qlo=0, qhi=1024))
                                for i in range(2):
                                    chunks.append(dict(kT=CKF[bs, h, i * 128:(i + 1) * 128], V=CVb[:, i, h * 128:(h + 1) * 128], qlo=0, qhi=1024))

                            def fin_d(t, o, d, Dd=Dd):
                                P.recip(RI, d[:, :])
                                P.tt(Dd[:, t * 512:(t + 1) * 512], o[:, :], RI, ALU.mult)
                            attend(lambda lo, hi, QQ=QQ, h=h, bs=bs: QQ[h][bs, lo:hi], chunks, 128, fin_d, ptb)
                        P.stt(D1, D2, LAMV[:, 4 * l + 3:4 * l + 4], D1, ALU.mult, ALU.add)
                        for t in range(2):
                            ts_ = slice(t * 512, (t + 1) * 512)
                            P.act(scr["sqb"][0], D1[:, ts_], AF.Square)
                            ms = rb()
                            P.mm(ms[:, :], ONES128, scr["sqb"][0], True)
                            P.act(scr["sd"], ms[:, :], AF.Sqrt, bias=EPSC)
                            P.recip(scr["sd"], scr["sd"])
                            if style == 0:
                                P.stt(MD[h][:, ts_], D1[:, ts_], DSC[:, l, 0:1], scr["sd"], ALU.mult, ALU.mult)
                            else:
                                P.stt(scr["t1"], D1[:, ts_], DSC[:, l, 1:2], scr["sd"], ALU.mult, ALU.mult)
                                P.tt(MD[h][:, ts_], MD[h][:, ts_], scr["t1"], ALU.add)
                wout_group(l, MD, 128, 1024)
            top[0] = m

        attn_mixer("swa")
        if STAGE == 3:
            continue
        attn_mixer("na")
        if STAGE == 4:
            continue
        attn_mixer("diff")
        if STAGE == 5:
            continue

        mod_ab(l, 1)
        modnorm(l, 1)
        if SUB == 11 or (SUB == 13 and l == 1):
            continue
        m = top[0]
        ACT_ = bfv(al(2048), 4096).rearrange("p (c t) -> p c t", c=4)
        sg_ = [f32v(al(512), 512) for _ in range(2)]
        for g in range(11):
            for half in range(2):
                c0 = g * 512 + half * 256
                Wg_ = wslab(wg[l].rearrange("(kc p) n -> p kc n", p=128)[:, :, c0:c0 + 256], [128, 16, 256])
                Wu_ = wslab(wu[l].rearrange("(kc p) n -> p kc n", p=128)[:, :, c0:c0 + 256], [128, 16, 256])
                for j in range(2):
                    for t in range(2):
                        ts_ = slice(t * 512, (t + 1) * 512)
                        pg = rb()
                        for kc in range(16):
                            P.mm(pg[:, :], Wg_[:, kc, j * 128:(j + 1) * 128], HB[:, kc, ts_], kc == 0)
                        pu = rb()
                        for kc in range(16):
                            P.mm(pu[:, :], Wu_[:, kc, j * 128:(j + 1) * 128], HB[:, kc, ts_], kc == 0)
                        s_ = sg_[(j * 2 + t) % 2]
                        P.act(s_, pg[:, :], AF.Silu)
                        P.tt(ACT_[:, half * 2 + j, ts_], s_, pu[:, :], ALU.mult)
                mod_step(2)
            for c0 in range(0, D, 1024):
                if SUB == 12:
                    continue
                mod_require(l, 48)
                W = wslab(wd[l, g * 512:(g + 1) * 512, c0:c0 + 1024].rearrange("(h p) n -> p h n", p=128), [128, 4, 1024])
                for j in range(8):
                    c = c0 // 128 + j
                    for t in range(2):
                        ts_ = slice(t * 512, (t + 1) * 512)
                        acc = ab()
                        for i in range(4):
                            P.mm(acc[:, :], W[:, i, j * 128:(j + 1) * 128], ACT_[:, i, ts_], i == 0)
                        P.stt(XF[:, c, ts_], acc[:, :], MODV[:, l, 80 + c:81 + c], XF[:, c, ts_], ALU.mult, ALU.add)
                mod_step()
        top[0] = m

    ys = [al(2048), al(2048)]
    yh = bfv(al(1024), 2048).rearrange("p (c n) -> p c n", c=16)
    yl = bfv(al(1024), 2048).rearrange("p (c n) -> p c n", c=16)
    for i in range(8):
        tsl = slice(i * 128, (i + 1) * 128)
        yt = f32v(ys[i % 2], 2048)
        P.copy(yh, XF[:, :, tsl], eng="act")
        P.tt(yl, XF[:, :, tsl], yh, ALU.subtract)
        for g in range(4):
            ps = rb()
            for j in range(4):
                c = g * 4 + j
                P.mm(ps[:, j * 128:(j + 1) * 128], yh[:, c, :], IDENT, j == 0)
                P.mm(ps[:, j * 128:(j + 1) * 128], yl[:, c, :], IDENT, False)
            P.copy(yt[:, g * 512:(g + 1) * 512], ps[:, :], eng=("act" if g % 2 else "dve"))
        P.dma(y_d[tsl, :], yt, "y%d" % (i % 2))
    cnt = P.emit()
    st.close()
    return nc, cnt, len(P.ins)


def _consts():
    t = np.arange(T)
    pos = np.stack([t // 64, t % 64], 0).astype(np.float32)
    inv = (10000.0 ** (-np.arange(16, dtype=np.float32) / 16)).astype(np.float32)
    cos = np.zeros((128, T), np.float32); sin = np.zeros((128, T), np.float32)
    for p in range(128):
        f = p % 64
        ang = (pos[f // 32] * inv[f % 16]).astype(np.float32)
        cos[p] = np.cos(ang); sin[p] = np.sin(ang)
    cm = np.zeros((128, 6, 128), np.float32)
    cm[:, 0, :] = np.eye(128)
    for m in range(128):
        f = m % 64; a = f // 32; half = (f % 32) // 16; j = f % 16; hb = (m // 64) * 64
        if half == 0:
            k = hb + a * 32 + 16 + j; cm[k, 1, m] = -1.0
        else:
            k = hb + a * 32 + j; cm[k, 1, m] = 1.0
    for k in range(128):
        cm[k, 2, (k // 64) * 64:(k // 64) * 64 + 64] = 1.0 / 64
    cm[:, 3, :] = 1.0 / 2048
    cm[:, 4, :] = 1.0 / 128
    cm[:, 5, :] = 1.0
    jmw = np.zeros((64, 2, 128), np.float32)
    for k in range(64):
        jmw[k, 0, 63 - k] = 1.0
        jmw[k, 1, 64 + 63 - k] = 1.0
    sel = np.zeros((128, 2, 128), np.float32)
    for p in range(128):
        for i in range(2):
            sel[p, i, i * 64 + (p % 64)] = 1.0
    cq = np.arange(64)
    cstart = np.clip(cq - 8, 0, 48)
    ok = (cq[None, :] >= cstart[:, None]) & (cq[None, :] < cstart[:, None] + 16)
    negc1 = np.where(ok.T, 0.0, -1e30).astype(np.float32)
    negc = np.concatenate([negc1, negc1], 0)
    b = np.arange(128)[:, None]; a = np.arange(128)[None, :]
    band = np.concatenate([(b <= a), np.ones((128, 128), bool), (a <= b)], 1).astype(np.float32)
    return dict(cos_t=cos, sin_t=sin, cmat=cm, jmw=jmw, sel=sel, negc=negc, band=band)


_CACHE = {}


def kernel(x_prompt, x_sample, cache_swa_k, cache_swa_v, cache_na_k, cache_na_v, cache_diff_k, cache_diff_v,
           state_lru, c, c_ctx, norm_mix, norm_ffn, w_mod, b_mod, w_in, w_out, qk_gain, swa_sink, na_rpb,
           diff_lambda, diff_subln, conv_w, conv_b, lru_wa, lru_ba, lru_wx, lru_bx, lru_L,
           w_ffn_gate, w_ffn_up, w_ffn_down):
    f = lambda a: np.ascontiguousarray(np.asarray(a, dtype=np.float32))
    if "nc" not in _CACHE:
        _CACHE["nc"] = build_nc()[0]
    nc = _CACHE["nc"]
    perm = np.arange(PW)
    qperm = []
    for cc in range(4):
        qperm += list(range(cc * 64, cc * 64 + 64)) + list(range((4 + cc) * 64, (4 + cc) * 64 + 64))
    perm[0:512] = np.array(qperm)
    w_in_p = f(np.asarray(w_in)[:, :, perm])
    rp = np.asarray(na_rpb, np.float32)
    ppad = np.zeros((2, 8, 15, 127), np.float32)
    ppad[..., 48:79] = rp
    prev = ppad[..., ::-1]
    rpbrev = f(prev[:, :, ::-1, :])
    shared = dict(w_mod=f(w_mod), b_mod=f(b_mod), w_in=w_in_p, w_out=f(w_out), wg=f(w_ffn_gate), wu=f(w_ffn_up),
                  wd=f(w_ffn_down), norm_mix=f(norm_mix), norm_ffn=f(norm_ffn), qk_gain=f(np.asarray(qk_gain).reshape(2, 6, 64)),
                  swa_sink=f(swa_sink), rpbrev=rpbrev, dlam=f(np.asarray(diff_lambda).reshape(2, 256)), dsub=f(diff_subln),
                  conv_w=f(conv_w), conv_b=f(conv_b), lru_wa=f(lru_wa), lru_ba=f(lru_ba), lru_wx=f(lru_wx),
                  lru_bx=f(lru_bx), lru_L=f(lru_L))
    shared.update(_consts())
    xp = np.asarray(x_prompt, np.float32); xs = np.asarray(x_sample, np.float32)
    in_maps = []
    for i in range(8):
        m = dict(shared)
        if i < 4:
            m["x"] = f(xp[4 * i:4 * i + 4].reshape(T, D)); m["cvec"] = f(c_ctx)
            m["flags"] = f(np.tile(np.array([[1.0, 0.0]], np.float32), (128, 1)))
            m["c_swa_k"] = np.zeros((2, 256, 128), np.float32); m["c_swa_v"] = np.zeros((2, 256, 128), np.float32)
            m["c_na_k"] = np.zeros((2, 256, 512), np.float32); m["c_na_v"] = np.zeros((2, 256, 512), np.float32)
            m["c_diff_k"] = np.zeros((2, 256, 512), np.float32); m["c_diff_v"] = np.zeros((2, 256, 512), np.float32)
            m["state"] = np.zeros((2, 2, 512), np.float32)
        else:
            b = i - 4
            m["x"] = f(xs[b]); m["cvec"] = f(np.asarray(c)[b])
            m["flags"] = f(np.tile(np.array([[0.0, 1.0]], np.float32), (128, 1)))
            m["c_swa_k"] = f(np.asarray(cache_swa_k)[b].reshape(2, 256, 128)); m["c_swa_v"] = f(np.asarray(cache_swa_v)[b].reshape(2, 256, 128))
            m["c_na_k"] = f(np.asarray(cache_na_k)[b].reshape(2, 256, 512)); m["c_na_v"] = f(np.asarray(cache_na_v)[b].reshape(2, 256, 512))
            m["c_diff_k"] = f(np.asarray(cache_diff_k)[b].reshape(2, 256, 512)); m["c_diff_v"] = f(np.asarray(cache_diff_v)[b].reshape(2, 256, 512))
            m["state"] = f(np.asarray(state_lru)[b])
        in_maps.append(m)
    res = run_bass_kernel_spmd(nc, in_maps, core_ids=list(range(8)))
    R = res.results
    y_prompt = np.concatenate([R[i]["y"].reshape(4, 256, D) for i in range(4)], 0)
    y_sample = np.stack([R[4 + i]["y"] for i in range(4)], 0)

    def gat(name, shp):
        return np.concatenate([np.transpose(R[i][name].reshape((2, 4, 256) + shp), (1, 0, 2) + tuple(range(3, 3 + len(shp)))) for i in range(4)], 0)
    nsk = gat("o_swa_k", (2, 64)); nsv = gat("o_swa_v", (2, 64))
    nnk = gat("o_na_k", (8, 64)); nnv = gat("o_na_v", (8, 64))
    ndk = gat("o_diff_k", (4, 2, 64)); ndv = gat("o_diff_v", (4, 128))
    nst = np.concatenate([np.transpose(R[i]["o_state"], (1, 0, 2, 3)) for i in range(4)], 0)
    return (y_prompt.astype(np.float32), y_sample.astype(np.float32), nsk, nsv, nnk, nnv, ndk, ndv, nst.astype(np.float32))
```

```python
import numpy as np
import concourse.bass as bass
import concourse.mybir as mybir

F32 = mybir.dt.float32
BF16 = mybir.dt.bfloat16
AF = mybir.ActivationFunctionType
ALU = mybir.AluOpType


class _I:
    __slots__ = ("eng", "fn", "deps", "dsem", "signal", "semval", "idx", "total")


def _isz(dt):
    return 2 if dt == BF16 else 4


class Prog:
    ENGS = ("pe", "act", "dve", "pool", "sp")

    def __init__(self, nc):
        self.nc = nc
        self.ins = []
        self.trk = {}
        self.dma_cnt = {}
        self.out_dmas = []

    def _iv(self, ap):
        sp = str(ap.space)
        if "DRAM" in sp.upper():
            return None
        steps = ap.ap
        rs = steps[0][0]
        off = ap.offset
        p0 = off // rs if rs > 0 else 0
        f0 = off - p0 * rs
        lo = f0
        hi = f0
        for st, cnt in steps[1:]:
            ext = st * (cnt - 1)
            if ext < 0:
                lo += ext
            else:
                hi += ext
        isz = _isz(ap.dtype)
        return ((sp, ap.tensor.name), lo * isz, (hi + 1) * isz, p0, p0 + steps[0][1])

    def _rec(self, eng, fn, reads, writes, dsem=None, total=False):
        ins = _I()
        ins.eng = eng
        ins.fn = fn
        ins.dsem = dsem
        ins.signal = False
        ins.semval = 0
        ins.total = total
        ins.idx = len(self.ins)
        deps = set()
        acc = []
        for ap in reads:
            if ap is None or isinstance(ap, (int, float)):
                continue
            iv = self._iv(ap)
            if iv is not None:
                acc.append((iv, False))
        for ap in writes:
            iv = self._iv(ap)
            if iv is not None:
                acc.append((iv, True))
        for (key, lo, hi, plo, phi), isw in acc:
            if "PSUM" in key[0].upper():
                lo, hi, plo, phi, isw = 0, 2048, 0, 128, True
            recs = self.trk.setdefault(key, [])
            keep = []
            for r in recs:
                rlo, rhi, rplo, rphi, ridx, rw = r
                if ridx == ins.idx:
                    keep.append(r)
                    continue
                ov = (rlo < hi and lo < rhi and rplo < phi and plo < rphi)
                if ov and (rw or isw):
                    deps.add(ridx)
                if isw and ov and rlo >= lo and rhi <= hi and rplo >= plo and rphi <= phi:
                    continue
                if (not isw) and (not rw) and rlo == lo and rhi == hi and rplo == plo and rphi == phi \
                        and self.ins[ridx].eng == eng and self.ins[ridx].dsem is None and dsem is None:
                    continue
                keep.append(r)
            keep.append((lo, hi, plo, phi, ins.idx, isw))
            self.trk[key] = keep
        deps.discard(ins.idx)
        best = {}
        red = set()
        for d_ in deps:
            p_ = self.ins[d_]
            if p_.dsem is not None:
                red.add(d_)
            elif best.get(p_.eng, -1) < d_:
                best[p_.eng] = d_
        red.update(best.values())
        deps = red
        ins.deps = deps
        self.ins.append(ins)
        if dsem is not None:
            self.dma_cnt[dsem] = self.dma_cnt.get(dsem, 0) + 16
            ins.semval = self.dma_cnt[dsem]
        return ins

    def mm(self, out, lhsT, rhs, first):
        self._rec("pe", lambda e: e.matmul(out, lhsT, rhs, start=bool(first), stop=True,
                                           skip_group_check=True), [lhsT, rhs], [out])

    def tr(self, out, in_, ident):
        self._rec("pe", lambda e: e.transpose(out, in_, ident), [in_, ident], [out])

    def act(self, out, in_, func, bias=None, scale=None, eng="act"):
        kw = {}
        if bias is not None:
            kw["bias"] = bias
        if scale is not None:
            kw["scale"] = scale
        rd = [in_]
        if bias is not None and not isinstance(bias, (int, float)):
            rd.append(bias)
        if scale is not None and not isinstance(scale, (int, float)):
            rd.append(scale)
        self._rec(eng, lambda e: e.activation(out, in_, func, **kw), rd, [out])

    def tt(self, out, a, b, op, eng="dve"):
        self._rec(eng, lambda e: e.tensor_tensor(out, a, b, op), [a, b], [out])

    def ts(self, out, a, s1, op0, s2=None, op1=None, eng="dve"):
        rd = [a]
        if not isinstance(s1, (int, float)):
            rd.append(s1)
        if s2 is not None and not isinstance(s2, (int, float)):
            rd.append(s2)
        if op1 is None:
            self._rec(eng, lambda e: e.tensor_scalar(out, a, s1, None, op0), rd, [out])
        else:
            self._rec(eng, lambda e: e.tensor_scalar(out, a, s1, s2, op0, op1), rd, [out])

    def stt(self, out, a, s, b, op0, op1):
        rd = [a, b]
        if not isinstance(s, (int, float)):
            rd.append(s)
        self._rec("dve", lambda e: e.scalar_tensor_tensor(out, a, s, b, op0, op1), rd, [out])

    def scan(self, out, d0, d1, init):
        rd = [d0, d1]
        if not isinstance(init, (int, float)):
            rd.append(init)
        self._rec("dve", lambda e: e.tensor_tensor_scan(out, d0, d1, init, ALU.mult, ALU.add), rd, [out])

    def copy(self, out, in_, eng="dve"):
        if eng == "act":
            self._rec(eng, lambda e: e.copy(out, in_), [in_], [out])
        else:
            self._rec(eng, lambda e: e.tensor_copy(out, in_), [in_], [out])

    def memset(self, out, val, eng="dve"):
        self._rec(eng, lambda e: e.memset(out, val), [], [out])

    def recip(self, out, in_):
        self._rec("dve", lambda e: e.reciprocal(out, in_), [in_], [out])

    def dma(self, out, in_, sem, q="sp", total=False, **kw):
        ins = self._rec(q, lambda e: e.dma_start(out, in_, **kw), [in_], [out], dsem=sem, total=total)
        if "DRAM" in str(out.space).upper():
            self.out_dmas.append(ins)
        return ins

    def emit(self):
        nc = self.nc
        ins = self.ins
        for i in ins:
            for d in i.deps:
                p = ins[d]
                if p.dsem is not None:
                    continue
                if p.eng == i.eng and i.dsem is None:
                    if p.eng == "pe":
                        continue
                p.signal = True
        cnt = {e: 0 for e in self.ENGS}
        for i in ins:
            if i.dsem is None and i.signal:
                cnt[i.eng] += 1
                i.semval = cnt[i.eng]
        names = sorted(self.dma_cnt.keys())
        import contextlib
        with contextlib.ExitStack() as st:
            esem = {e: st.enter_context(nc.semaphore("s_" + e)) for e in self.ENGS}
            dsem = {n: st.enter_context(nc.semaphore("d_" + n)) for n in names}
            block = st.enter_context(nc.Block())
            per = {e: [i for i in ins if i.eng == e] for e in self.ENGS}
            final_waits = {}
            for i in self.out_dmas:
                final_waits[i.dsem] = self.dma_cnt[i.dsem]

            def gen(ename, last_waits=None):
                def body(e):
                    seen = {}
                    for i in per[ename]:
                        need = {}
                        for d in i.deps:
                            p = ins[d]
                            if p.dsem is not None:
                                s = ("d", p.dsem)
                                v = self.dma_cnt[p.dsem] if p.total else p.semval
                            else:
                                if p.eng == i.eng and i.dsem is None and p.eng == "pe":
                                    continue
                                if not p.signal:
                                    continue
                                s = ("e", p.eng)
                                v = p.semval
                            if need.get(s, 0) < v:
                                need[s] = v
                        for s, v in need.items():
                            if seen.get(s, 0) >= v:
                                continue
                            seen[s] = v
                            e.wait_ge(dsem[s[1]] if s[0] == "d" else esem[s[1]], v)
                        h = i.fn(e)
                        if i.dsem is not None:
                            h.then_inc(dsem[i.dsem], 16)
                        elif i.signal:
                            h.then_inc(esem[i.eng], 1)
                    if last_waits:
                        for n, v in last_waits.items():
                            e.wait_ge(dsem[n], v)
                return body

            block.tensor(gen("pe"))
            block.scalar(gen("act"))
            block.vector(gen("dve"))
            block.gpsimd(gen("pool"))
            block.sync(gen("sp", final_waits))
        return cnt

import contextlib
import math
from concourse.bass_utils import run_bass_kernel_spmd

D = 2048
T = 1024
HID = 5632
PW = 4864
O_SQ, O_SK, O_SV, O_NQ, O_NK, O_NV, O_DQ, O_DK, O_DV, O_LX, O_LG = 0, 512, 640, 768, 1280, 1792, 2304, 2816, 3328, 3840, 4352
EPS = 1e-6


_DBG = {"on": False, "stage": 99, "sub": 99}


def build_nc():
    nc = bass.Bass("TRN2", target_bir_lowering=False)
    SUB = _DBG["sub"]

    def dbig(name, shape):
        if _DBG["on"]:
            return nc.dram_tensor(name, list(shape), F32, kind="Internal").ap()
        return nc.dram_tensor(name, list(shape), F32, kind="ExternalInput").ap()

    def din(name, shape):
        return nc.dram_tensor(name, list(shape), F32, kind="ExternalInput").ap()

    def dout(name, shape):
        return nc.dram_tensor(name, list(shape), F32, kind="ExternalOutput").ap()

    x_d = din("x", [T, D]); cvec_d = din("cvec", [D]); flags_d = din("flags", [128, 2])
    c_swa_k = din("c_swa_k", [2, 256, 128]); c_swa_v = din("c_swa_v", [2, 256, 128])
    c_na_k = din("c_na_k", [2, 256, 512]); c_na_v = din("c_na_v", [2, 256, 512])
    c_diff_k = din("c_diff_k", [2, 256, 512]); c_diff_v = din("c_diff_v", [2, 256, 512])
    state_d = din("state", [2, 2, 512])
    w_mod = dbig("w_mod", [2, D, 6 * D]); b_mod = din("b_mod", [2, 6 * D])
    w_in = dbig("w_in", [2, D, PW]); w_out = dbig("w_out", [2, D, D])
    wg = dbig("wg", [2, D, HID]); wu = dbig("wu", [2, D, HID]); wd = dbig("wd", [2, HID, D])
    norm_mix = din("norm_mix", [2, D]); norm_ffn = din("norm_ffn", [2, D])
    qk_gain = din("qk_gain", [2, 6, 64]); swa_sink = din("swa_sink", [2, 8])
    rpbrev = din("rpbrev", [2, 8, 15, 127]); dlam = din("dlam", [2, 256]); dsub = din("dsub", [2, 128])
    conv_w = din("conv_w", [2, 4, 512]); conv_b = din("conv_b", [2, 512])
    lru_wa = din("lru_wa", [2, 2, 8, 64, 64]); lru_ba = din("lru_ba", [2, 2, 512])
    lru_wx = din("lru_wx", [2, 2, 8, 64, 64]); lru_bx = din("lru_bx", [2, 2, 512]); lru_L = din("lru_L", [2, 2, 512])
    cos_d = din("cos_t", [128, T]); sin_d = din("sin_t", [128, T])
    cmat_d = din("cmat", [128, 6, 128]); jmw_d = din("jmw", [64, 2, 128]); negc_d = din("negc", [128, 64]); sel_d = din("sel", [128, 2, 128]); band_d = din("band", [128, 384])

    y_d = dout("y", [T, D])
    o_swa_k = dout("o_swa_k", [2, T, 128]); o_swa_v = dout("o_swa_v", [2, T, 128])
    o_na_k = dout("o_na_k", [2, T, 512]); o_na_v = dout("o_na_v", [2, T, 512])
    o_diff_k = dout("o_diff_k", [2, T, 512]); o_diff_v = dout("o_diff_v", [2, T, 512])
    o_state = dout("o_state", [2, 4, 2, 512])

    st = contextlib.ExitStack()
    NW = 53000
    A = st.enter_context(nc.sbuf_tensor("arena", [128, NW], F32))
    PS = [st.enter_context(nc.psum_tensor("ps%d" % i, [128, 512], F32)) for i in range(8)]
    P = Prog(nc)
    top = [0]
    marks = []

    def al(n):
        o = top[0]
        top[0] += n
        assert top[0] <= NW, top[0]
        return o

    def f32v(o, n):
        return A[:, o:o + n]

    def bfv(o, nbf):
        return A[:, o:o + (nbf + 1) // 2].bitcast(BF16)

    rr = [0, 0]

    wide = [True]

    rrole = {"p": [0, (0, 1, 2, 3)], "m": [0, (4, 5)], "r": [0, (6,)], "t": [0, (7,)]}

    def rb(role=None):
        if role is not None:
            st_ = rrole[role]
            st_[0] += 1
            return PS[st_[1][st_[0] % len(st_[1])]]
        rr[0] += 1
        if wide[0]:
            return PS[rr[0] % 8]
        return PS[4 + rr[0] % 4]

    def ab():
        rr[1] += 1
        return PS[rr[1] % 4]

    XF = f32v(al(16 * T), 16 * T).rearrange("p (c t) -> p c t", c=16)
    HB = bfv(al(8 * T), 16 * T).rearrange("p (c t) -> p c t", c=16)
    RING = [al(2048) for _ in range(3)]
    wk = [0]

    def wslab(src, shape):
        s = wk[0] % 3
        wk[0] += 1
        n = 1
        for d_ in shape[1:]:
            n *= d_
        v = bfv(RING[s], 4096)[0:shape[0], 0:n]
        if len(shape) == 3:
            v = v.rearrange("p (a b) -> p a b", a=shape[1])
        if _DBG.get("tinyw") and len(shape) == 3:
            P.dma(v[:, 0:1, :], src[:, 0:1, :], "w%d" % s, q="pool")
        else:
            P.dma(v, src, "w%d" % s, q="pool")
        return v

    COS = f32v(al(T), T); SIN = f32v(al(T), T)
    P.dma(COS, cos_d, "c0", total=True); P.dma(SIN, sin_d, "c0", total=True)
    stg0 = 46000
    cm32 = stg0
    P.dma(f32v(cm32, 768).rearrange("p (k n) -> p k n", k=6), cmat_d, "c0", total=True)
    cmb = al(384)
    CM = bfv(cmb, 768).rearrange("p (k n) -> p k n", k=6)
    P.copy(CM, f32v(cm32, 768).rearrange("p (k n) -> p k n", k=6))
    IDENT, RMAT, BONES, ONESD, ONES128, ONES1 = [CM[:, k, :] for k in range(6)]
    j32 = stg0 + 768; P.dma(A[0:64, j32:j32 + 256].rearrange("p (k n) -> p k n", k=2), jmw_d, "c0", total=True)
    JMW = bfv(al(128), 256)[0:64, :].rearrange("p (k n) -> p k n", k=2); P.copy(JMW, A[0:64, j32:j32 + 256].rearrange("p (k n) -> p k n", k=2))
    s32 = stg0 + 1024; P.dma(f32v(s32, 256).rearrange("p (k n) -> p k n", k=2), sel_d, "c0", total=True)
    SELb = bfv(al(128), 256).rearrange("p (k n) -> p k n", k=2); P.copy(SELb, f32v(s32, 256).rearrange("p (k n) -> p k n", k=2))
    NEGC = f32v(al(64), 64); P.dma(NEGC, negc_d, "c0", total=True)
    ONEC = f32v(al(1), 1); P.memset(ONEC, 1.0)
    b32 = stg0 + 1280; P.dma(f32v(b32, 384), band_d, "c0", total=True)
    BAND = bfv(al(192), 384); P.copy(BAND, f32v(b32, 384))
    FL = f32v(al(2), 2); P.dma(FL, flags_d, "c0", total=True)
    FP_, FS_ = FL[:, 0:1], FL[:, 1:2]
    EPSC = f32v(al(1), 1); P.memset(EPSC, EPS)
    GN1 = f32v(al(32), 32).rearrange("p (l c) -> p l c", l=2)
    GN2 = f32v(al(32), 32).rearrange("p (l c) -> p l c", l=2)
    BMOD = f32v(al(192), 192).rearrange("p (l c) -> p l c", l=2)
    P.dma(GN1, norm_mix.rearrange("l (c p) -> p l c", p=128), "c0", total=True, allow_slow_non_contiguous=True)
    P.dma(GN2, norm_ffn.rearrange("l (c p) -> p l c", p=128), "c0", total=True, allow_slow_non_contiguous=True)
    P.dma(BMOD, b_mod.rearrange("l (c p) -> p l c", p=128), "c0", total=True, allow_slow_non_contiguous=True)
    QKG = f32v(al(12), 12).rearrange("p (l k) -> p l k", l=2)
    for hh in range(2):
        P.dma(A[hh * 64:(hh + 1) * 64, QKG.offset % NW:QKG.offset % NW + 12].rearrange("p (l k) -> p l k", l=2),
              qk_gain.rearrange("l k d -> d l k"), "c0", total=True, allow_slow_non_contiguous=True)
    SINK = f32v(al(16), 16).rearrange("p (l h) -> p l h", l=2)
    P.dma(SINK, bass.AP(swa_sink.tensor, 0, [[0, 128], [8, 2], [1, 8]]), "c0", total=True)
    ESINK = f32v(al(16), 16).rearrange("p (l h) -> p l h", l=2)
    P.act(ESINK, SINK, AF.Exp)
    DLAM = f32v(stg0 + 1664, 512).rearrange("p (l k) -> p l k", l=2)
    P.dma(DLAM, bass.AP(dlam.tensor, 0, [[0, 128], [256, 2], [1, 256]]), "c0", total=True)
    DSUB = f32v(al(2), 2)
    P.dma(DSUB, dsub.rearrange("l p -> p l"), "c0", total=True, allow_slow_non_contiguous=True)
    CW = f32v(al(32), 32).rearrange("p (l k c) -> p l k c", l=2, k=4)
    P.dma(CW, conv_w.rearrange("l k (c p) -> p l k c", p=128), "c0", total=True, allow_slow_non_contiguous=True)
    CB = f32v(al(8), 8).rearrange("p (l c) -> p l c", l=2)
    P.dma(CB, conv_b.rearrange("l (c p) -> p l c", p=128), "c0", total=True, allow_slow_non_contiguous=True)
    LBA = f32v(al(16), 16).rearrange("p (l d c) -> p l d c", l=2, d=2)
    LBX = f32v(al(16), 16).rearrange("p (l d c) -> p l d c", l=2, d=2)
    LL = f32v(al(16), 16).rearrange("p (l d c) -> p l d c", l=2, d=2)
    P.dma(LBA, lru_ba.rearrange("l d (c p) -> p l d c", p=128), "c0", total=True, allow_slow_non_contiguous=True)
    P.dma(LBX, lru_bx.rearrange("l d (c p) -> p l d c", p=128), "c0", total=True, allow_slow_non_contiguous=True)
    P.dma(LL, lru_L.rearrange("l d (c p) -> p l d c", p=128), "c0", total=True, allow_slow_non_contiguous=True)
    H0S = f32v(al(16), 16).rearrange("p (l d c) -> p l d c", l=2, d=2)
    P.dma(H0S, state_d.rearrange("l d (c p) -> p l d c", p=128), "c0", total=True, allow_slow_non_contiguous=True)
    CL = f32v(al(16), 16).rearrange("p (l d c) -> p l d c", l=2, d=2)
    P.act(CL, LL, AF.Exp, scale=-1.0)
    P.act(CL, CL, AF.Ln, bias=ONEC)
    P.ts(CL, CL, -8.0, ALU.mult)
    LAMV = f32v(al(8), 8)
    dl_t = f32v(stg0 + 2176, 128)
    for l in range(2):
        li = 0.8 - 0.6 * math.exp(-0.3 * l)
        sacc = LAMV[:, 4 * l:4 * l + 2]
        P.tt(dl_t.rearrange("p (a d) -> p a d", a=2), DLAM[:, l, :].rearrange("p (a b d) -> p a b d", a=2, b=2)[:, :, 0, :],
             DLAM[:, l, :].rearrange("p (a b d) -> p a b d", a=2, b=2)[:, :, 1, :], ALU.mult)
        P._rec("dve", (lambda o_, i_: (lambda e: e.reduce_sum(o_, i_, axis=mybir.AxisListType.X)))(sacc, dl_t.rearrange("p (a d) -> p a d", a=2)),
               [dl_t], [sacc])
        P.act(sacc, sacc, AF.Exp)
        P.tt(LAMV[:, 4 * l + 2:4 * l + 3], sacc[:, 0:1], sacc[:, 1:2], ALU.subtract)
        P.ts(LAMV[:, 4 * l + 3:4 * l + 4], LAMV[:, 4 * l + 2:4 * l + 3], li, ALU.add, -1.0, ALU.mult)
    DSC = f32v(al(4), 4).rearrange("p (l s) -> p l s", l=2)
    for l in range(2):
        li = 0.8 - 0.6 * math.exp(-0.3 * l)
        P.ts(DSC[:, l, 0:1], DSUB[:, l:l + 1], 1.0 - li, ALU.mult, FP_, ALU.mult)
        P.ts(DSC[:, l, 1:2], DSUB[:, l:l + 1], 1.0 - li, ALU.mult, FS_, ALU.mult)
    MODV = f32v(al(192), 192).rearrange("p (l c) -> p l c", l=2)
    AB = f32v(al(128), 128).rearrange("p (l k c) -> p l k c", l=2, k=4)
    CV = f32v(al(16), 16)
    P.dma(CV, cvec_d.rearrange("(c p) -> p c", p=128), "c0", total=True, allow_slow_non_contiguous=True)
    SCV = bfv(al(8), 16)
    P.act(SCV, CV, AF.Silu)
    static_top = top[0]

    m0 = top[0]
    xs = [al(2048), al(2048)]
    xh = bfv(al(1024), 2048); xl = bfv(al(1024), 2048)
    for i in range(8):
        xt = f32v(xs[i % 2], 2048)
        P.dma(xt, x_d[i * 128:(i + 1) * 128, :], "x%d" % (i % 2))
        P.copy(xh, xt, eng="act")
        P.tt(xl, xt, xh, ALU.subtract)
        for g in range(4):
            ps = rb()
            for j in range(4):
                c = g * 4 + j
                P.mm(ps[:, j * 128:(j + 1) * 128], xh[:, c * 128:(c + 1) * 128], IDENT, j == 0)
                P.mm(ps[:, j * 128:(j + 1) * 128], xl[:, c * 128:(c + 1) * 128], IDENT, False)
            P.copy(XF[:, g * 4:(g + 1) * 4, i * 128:(i + 1) * 128], ps[:, :].rearrange("p (j n) -> p j n", j=4),
                   eng=("act" if g % 2 else "dve"))
    top[0] = m0

    modq = [(l_, s_) for l_ in range(2) for s_ in range(48)]
    modpos = [0]

    def mod_step(n=1):
        for _ in range(n):
            if modpos[0] >= len(modq):
                return
            l_, s_ = modq[modpos[0]]
            modpos[0] += 1
            W = wslab(w_mod[l_].rearrange("(kc p) n -> p kc n", p=128)[:, :, s_ * 256:(s_ + 1) * 256], [128, 16, 256])
            ps = rb("m")
            for j in range(2):
                for kc in range(16):
                    P.mm(ps[:, j:j + 1], W[:, kc, j * 128:(j + 1) * 128], SCV[:, kc:kc + 1], (j == 0 and kc == 0))
            P.tt(MODV[:, l_, 2 * s_:2 * s_ + 2], ps[:, 0:2], BMOD[:, l_, 2 * s_:2 * s_ + 2], ALU.add)

    def mod_require(l_, upto):
        while modpos[0] < l_ * 48 + upto:
            mod_step(1)

    def mod_ab(l_, which):
        if which == 0:
            mod_require(l_, 16)
            P.stt(AB[:, l_, 0, :], MODV[:, l_, 16:32], 1.0, GN1[:, l_, :], ALU.add, ALU.mult)
            P.copy(AB[:, l_, 1, :], MODV[:, l_, 0:16])
        else:
            mod_require(l_, 40)
            P.stt(AB[:, l_, 2, :], MODV[:, l_, 64:80], 1.0, GN2[:, l_, :], ALU.add, ALU.mult)
            P.copy(AB[:, l_, 3, :], MODV[:, l_, 48:64])

    def modnorm(l, which):
        m = top[0]
        sqb = [bfv(al(256), 512) for _ in range(2)]
        RS = f32v(al(512), 512)
        tmp = [f32v(al(512), 512) for _ in range(2)]
        for t in range(2):
            ts_ = slice(t * 512, (t + 1) * 512)
            ms = rb()
            for c in range(16):
                P.act(sqb[c % 2], XF[:, c, ts_], AF.Square)
                P.mm(ms[:, :], ONESD, sqb[c % 2], c == 0)
            P.act(RS, ms[:, :], AF.Sqrt, bias=EPSC)
            P.recip(RS, RS)
            for c in range(16):
                P.tt(tmp[c % 2], XF[:, c, ts_], RS, ALU.mult)
                P.act(HB[:, c, ts_], tmp[c % 2], AF.Identity, bias=AB[:, l, 2 * which + 1, c:c + 1], scale=AB[:, l, 2 * which, c:c + 1])
        top[0] = m

    def proj_fm(l, col0, ncols, consume):
        for s0 in range(0, ncols, 256):
            nn = min(256, ncols - s0)
            W = wslab(w_in[l].rearrange("(kc p) n -> p kc n", p=128)[:, :, col0 + s0:col0 + s0 + nn], [128, 16, nn])
            for j in range(nn // 128):
                for t in range(2):
                    ps = rb("p")
                    for kc in range(16):
                        P.mm(ps[:, :], W[:, kc, j * 128:(j + 1) * 128], HB[:, kc, t * 512:(t + 1) * 512], kc == 0)
                    consume((s0 + j * 128) // 128, t, ps)
            mod_step()

    qpend = []
    qcnt = [0]

    def qk_advance():
        for u in list(qpend):
            u.pop(0)()
            if not u:
                qpend.remove(u)

    def qk_flush():
        while qpend:
            qk_advance()

    def qk_unit(l, ps, t, gidx, dst_n, dst_r, kout, scr):
        ts_ = slice(t * 512, (t + 1) * 512)
        n_ = qcnt[0]
        qcnt[0] += 1
        sqb = scr["sqb"][n_ % 2]; QN = scr["QN"][n_ % 2]; lo = scr["lo"][n_ % 2]
        sd, t1, t2 = scr["sd"], scr["t1"], scr["t2"]

        def c1():
            P.act(sqb, ps[:, :], AF.Square)
            ms = rb("m")
            P.mm(ms[:, :], BONES, sqb, True)
            P.act(sd, ms[:, :], AF.Sqrt, bias=EPSC)
            P.recip(sd, sd)
            P.stt(QN, ps[:, :], QKG[:, l, gidx:gidx + 1], sd, ALU.mult, ALU.mult)
            P.copy(dst_n[:, ts_], QN, eng="act")

        def c2():
            if dst_r is not None:
                rq = rb("r")
                P.mm(rq[:, :], RMAT, dst_n[:, ts_], True)
                P.tt(t1, QN, COS[:, ts_], ALU.mult)
                P.tt(t2, rq[:, :], SIN[:, ts_], ALU.mult)
                P.tt(dst_r[:, ts_], t1, t2, ALU.add)
            if kout is not None:
                P.tt(lo, QN, dst_n[:, ts_], ALU.subtract)

        def c3():
            if kout is not None:
                od, c0, stg = kout
                tp = rb("t")
                for j in range(4):
                    P.mm(tp[:, j * 128:(j + 1) * 128], dst_n[:, t * 512 + j * 128:t * 512 + (j + 1) * 128], IDENT, j == 0)
                    P.mm(tp[:, j * 128:(j + 1) * 128], lo[:, j * 128:(j + 1) * 128], IDENT, False)
                P.copy(stg, tp[:, :], eng="act")
                P.dma(od[l].rearrange("(j p) f -> p j f", p=128)[:, t * 4:(t + 1) * 4, c0:c0 + 128],
                      stg.rearrange("p (j f) -> p j f", j=4), "so%d" % t)
        qk_advance()
        qpend.append([c1, c2, c3])

    def proj_v(l, col0, ncols, Vb, od, stgs):
        for s0 in range(0, ncols, 256):
            nn = min(256, ncols - s0)
            W = wslab(w_in[l].rearrange("(kc p) n -> p kc n", p=128)[:, :, col0 + s0:col0 + s0 + nn], [128, 16, nn])
            for i in range(8):
                ps = rb()
                for kc in range(16):
                    P.mm(ps[:, 0:nn], HB[:, kc, i * 128:(i + 1) * 128], W[:, kc, :], kc == 0)
                P.copy(Vb[:, i, s0:s0 + nn], ps[:, 0:nn], eng="act")
                sg = stgs[i % 2]
                P.copy(sg[:, 0:nn], ps[:, 0:nn])
                P.dma(od[l, i * 128:(i + 1) * 128, s0:s0 + nn], sg[:, 0:nn], "sv%d" % (i % 2))
            mod_step()

    def attend(qfn, chunks, dv, finish, ptb, base=0):
        wide[0] = False
        jobs = []
        for ch in chunks:
            for t in range(2):
                lo = max(ch["qlo"], 512 * t); hi = min(ch["qhi"], 512 * (t + 1))
                if lo < hi:
                    jobs.append((ch, t, lo, hi))
        first = [True, True]

        def p1(k):
            ch, t, lo, hi = jobs[k]
            n = hi - lo
            a0 = lo - 512 * t
            sp = rb()
            if ch.get("parts") is not None:
                for ip, pr in enumerate(ch["parts"]):
                    P.mm(sp[:, pr["qlo"] - 512 * t:pr["qhi"] - 512 * t], pr["kT"], qfn(pr["qlo"], pr["qhi"]), ip == 0)
            else:
                P.mm(sp[:, a0:a0 + n], ch["kT"], qfn(lo, hi), True)
            if ch.get("bias") is not None:
                for sel, fn_ in ch["bias"]:
                    P.mm(sp[:, a0:a0 + n], sel, fn_(lo, hi), False)
            pt = ptb[k % len(ptb)]
            P.act(pt[:, 0:n], sp[:, a0:a0 + n], AF.Exp, scale=0.125)
            if ch.get("mask") is not None:
                mk = ch["mask"]
                P.tt(pt[:, 0:n], pt[:, 0:n], mk[:, lo - ch["qlo"]:hi - ch["qlo"]], ALU.mult)
            if ch.get("zero") is not None:
                p0, c0, c1 = ch["zero"]
                z0 = max(c0, lo); z1 = min(c1, hi)
                if z0 < z1:
                    P.memset(pt[p0:p0 + 64, z0 - lo:z1 - lo], 0.0)

        def p2(k):
            ch, t, lo, hi = jobs[k]
            n = hi - lo
            a0 = lo - 512 * t
            pt = ptb[k % len(ptb)]
            if ch.get("parts") is not None:
                for pr in ch["parts"]:
                    b0 = pr["qlo"] - 512 * t; b1 = pr["qhi"] - 512 * t
                    P.mm(PS[t][0:dv, b0:b1], pr["V"], pt[:, pr["qlo"] - lo:pr["qhi"] - lo], first[t])
                    P.mm(PS[2 + t][0:dv, b0:b1], ONES1[:, 0:dv], pt[:, pr["qlo"] - lo:pr["qhi"] - lo], first[t])
                    first[t] = False
                return
            P.mm(PS[t][0:dv, a0:a0 + n], ch["V"], pt[:, 0:n], first[t])
            P.mm(PS[2 + t][0:dv, a0:a0 + n], ONES1[:, 0:dv], pt[:, 0:n], first[t])
            first[t] = False

        jobs.sort(key=lambda jb: jb[1])
        nj = len(jobs)
        LA = 2
        last_of = {}
        for k, jb in enumerate(jobs):
            last_of[jb[1]] = k
        for k in range(min(LA, nj)):
            p1(k)
        for k in range(nj):
            if k + LA < nj:
                p1(k + LA)
            p2(k)
            if last_of[jobs[k][1]] == k:
                finish(jobs[k][1], PS[jobs[k][1]], PS[2 + jobs[k][1]])
        wide[0] = True

    def wout_group(l, tiles, K, row0):
        nk = len(tiles)
        cols = 4096 // nk
        mod_require(l, 24)
        for c0 in range(0, D, cols):
            W = wslab(w_out[l, row0:row0 + nk * K, c0:c0 + cols].rearrange("(h p) n -> p h n", p=K), [K, nk, cols])
            for j in range(cols // 128):
                c = (c0 // 128) + j
                for t in range(2):
                    acc = ab()
                    for i in range(nk):
                        P.mm(acc[:, :], W[:, i, j * 128:(j + 1) * 128], tiles[i][:, t * 512:(t + 1) * 512], i == 0)
                    P.stt(XF[:, c, t * 512:(t + 1) * 512], acc[:, :], MODV[:, l, 32 + c:33 + c], XF[:, c, t * 512:(t + 1) * 512], ALU.mult, ALU.add)
            mod_step()

    STAGE = _DBG["stage"]
    for l in range(2 if STAGE > 0 else 0):
        mod_ab(l, 0)
        modnorm(l, 0)
        if STAGE == 1:
            continue
        lay_mark = top[0]
        m = top[0]
        LXP = f32v(al(4 * 1036), 4 * 1036).rearrange("p (c s w) -> p c s w", c=4, s=4)
        P.memset(LXP, 0.0)
        Gb = bfv(al(2048), 4096).rearrange("p (c t) -> p c t", c=4)
        MIXL = bfv(al(2048), 4096).rearrange("p (c t) -> p c t", c=4)
        STB = f32v(al(32), 32).rearrange("p (s d c) -> p s d c", s=4, d=2)

        def lx_consume(c, t, ps):
            P.copy(LXP[:, c, 2 * t:2 * t + 2, 2:258], ps[:, :].rearrange("p (s w) -> p s w", s=2), eng="act")
        proj_fm(l, O_LX, 512, lx_consume)

        def lg_consume(c, t, ps):
            P.act(Gb[:, c, t * 512:(t + 1) * 512], ps[:, :], AF.Gelu_apprx_tanh)
        proj_fm(l, O_LG, 512, lg_consume)
        U = f32v(al(1024), 1024); UB = bfv(al(512), 1024)
        Rb = f32v(al(1024), 1024); Ib = f32v(al(1024), 1024); Ab = f32v(al(1024), 1024)
        Hb = [f32v(al(1024), 1024), f32v(al(1024), 1024)]
        WBD = bfv(al(256), 512).rearrange("p (k n) -> p k n", k=4)
        wst = f32v(al(512), 512).rearrange("p (k n) -> p k n", k=4)
        for c in range(4):
            P.ts(LXP[:, c, 1:4, 0:2], LXP[:, c, 0:3, 256:258], FS_, ALU.mult)
            P.ts(LXP[:, c, 0:3, 258:259], LXP[:, c, 1:4, 2:3], FS_, ALU.mult)
            U4 = U.rearrange("p (s w) -> p s w", s=4)
            P.act(U4, LXP[:, c, :, 0:256], AF.Identity, bias=CB[:, l, c:c + 1], scale=CW[:, l, 0, c:c + 1])
            for k in range(1, 4):
                P.stt(U4, LXP[:, c, :, k:k + 256], CW[:, l, k, c:c + 1], U4, ALU.mult, ALU.add)
            P.copy(UB, U, eng="act")
            P.memset(wst, 0.0)
            for d_ in range(2):
                for g_, wsrc in enumerate((lru_wa, lru_wx)):
                    for b_ in range(2):
                        P.dma(A[b_ * 64:(b_ + 1) * 64, wst.offset % NW + (d_ * 2 + g_) * 128 + b_ * 64: wst.offset % NW + (d_ * 2 + g_) * 128 + b_ * 64 + 64],
                              wsrc[l, d_, 2 * c + b_], "lw%d" % (d_ * 4 + g_ * 2 + b_))
            P.copy(WBD, wst)
            for d_ in range(2):
                for t in range(2):
                    ts_ = slice(t * 512, (t + 1) * 512)
                    pr = rb()
                    P.mm(pr[:, :], WBD[:, d_ * 2, :], UB[:, ts_], True)
                    P.act(Rb[:, ts_], pr[:, :], AF.Sigmoid, bias=LBA[:, l, d_, c:c + 1])
                    pi = rb()
                    P.mm(pi[:, :], WBD[:, d_ * 2 + 1, :], UB[:, ts_], True)
                    P.act(Ib[:, ts_], pi[:, :], AF.Sigmoid, bias=LBX[:, l, d_, c:c + 1])
                P.act(Ab, Rb, AF.Exp, scale=CL[:, l, d_, c:c + 1])
                P.act(Rb, Ab, AF.Square)
                P.ts(Rb, Rb, -1.0, ALU.mult, 1.0, ALU.add)
                P.act(Rb, Rb, AF.Sqrt)
                P.tt(Ib, Ib, U, ALU.mult)
                P.tt(Ib, Ib, Rb, ALU.mult)
                if d_ == 0:
                    P.ts(Ab[:, 256:1024:256], Ab[:, 256:1024:256], FS_, ALU.mult)
                    P.scan(Hb[0], Ab, Ib, H0S[:, l, 0, c:c + 1])
                else:
                    P.ts(Ab[:, 255:1023:256], Ab[:, 255:1023:256], FS_, ALU.mult)
                    P.scan(Hb[1][:, ::-1], Ab[:, ::-1], Ib[:, ::-1], H0S[:, l, 1, c:c + 1])
            P.copy(STB[:, :, 0, c], Hb[0][:, 255:1024:256])
            P.copy(STB[:, :, 1, c], Hb[1][:, 0:1024:256])
            P.tt(Hb[0], Hb[0], Hb[1], ALU.add)
            P.tt(MIXL[:, c, :], Hb[0], Gb[:, c, :], ALU.mult)
        P.dma(o_state[l].rearrange("s d (c p) -> p s d c", p=128), STB, "sst", allow_slow_non_contiguous=True)
        wout_group(l, [MIXL[:, c, :] for c in range(4)], 128, 1536)
        top[0] = m
        if STAGE == 2:
            continue

        def attn_mixer(kind):
            m = top[0]
            nq = 4
            nk = 1 if kind == "swa" else 4
            oq = {"swa": O_SQ, "na": O_NQ, "diff": O_DQ}[kind]
            ok = {"swa": O_SK, "na": O_NK, "diff": O_DK}[kind]
            ov = {"swa": O_SV, "na": O_NV, "diff": O_DV}[kind]
            nvc = 128 if kind == "swa" else 512
            odk = {"swa": o_swa_k, "na": o_na_k, "diff": o_diff_k}[kind]
            odv = {"swa": o_swa_v, "na": o_na_v, "diff": o_diff_v}[kind]
            ck = {"swa": c_swa_k, "na": c_na_k, "diff": c_diff_k}[kind]
            cv = {"swa": c_swa_v, "na": c_na_v, "diff": c_diff_v}[kind]
            gi = {"swa": 0, "na": 2, "diff": 4}[kind]
            rope = kind != "na"
            QN = [bfv(al(512), 1024) for _ in range(nq)]
            KN = [bfv(al(512), 1024) for _ in range(nk)]
            QR = [bfv(al(512), 1024) for _ in range(nq)] if rope else QN
            KR = [bfv(al(512), 1024) for _ in range(nk)] if rope else KN
            Vb = bfv(al(4 * nvc), 8 * nvc).rearrange("p (i n) -> p i n", i=8)
            CKF = bfv(al(128 * nk), 256 * nk).rearrange("p (c n) -> p c n", c=nk)
            CVb = bfv(al(nvc), 2 * nvc).rearrange("p (i n) -> p i n", i=2)
            scr = dict(sqb=[bfv(al(256), 512)], sd=f32v(al(512), 512), QN=[f32v(al(512), 512)], t1=f32v(al(512), 512), lo=[bfv(al(256), 512)])
            stg = [f32v(al(512), 512), f32v(al(512), 512)]
            ptb = [bfv(al(256), 512) for _ in range(3)]
            RI = f32v(al(512), 512)
            ex0 = top[0]
            scr["sqb"].append(bfv(al(256), 512)); scr["QN"].append(f32v(al(512), 512)); scr["lo"].append(bfv(al(256), 512)); scr["t2"] = f32v(al(512), 512)
            top[0] = ex0
            proj_fm(l, oq, 512, lambda c, t, ps: qk_unit(l, ps, t, gi, QN[c], QR[c] if rope else None, None, scr))
            if SUB == 1:
                qk_flush()
                top[0] = m
                return
            proj_fm(l, ok, 128 * nk, lambda c, t, ps: qk_unit(l, ps, t, gi + 1, KN[c], KR[c] if rope else None, (odk, c * 128, stg[t]), scr))
            qk_flush()
            if SUB == 2:
                top[0] = m
                return
            proj_v(l, ov, nvc, Vb, odv, stg)
            if SUB == 3:
                top[0] = m
                return
            for i in range(2):
                sg = stg[i]
                P.dma(sg[:, 0:nvc], cv[l, i * 128:(i + 1) * 128, :], "cl%d" % i)
                P.copy(CVb[:, i, :], sg[:, 0:nvc])
            ktm = bfv(al(256), 512)
            for i in range(2):
                sg = stg[i]
                P.dma(sg[:, 0:128 * nk], ck[l, i * 128:(i + 1) * 128, :], "cl%d" % i)
                P.copy(ktm[:, 0:128 * nk], sg[:, 0:128 * nk])
                tp = rb()
                for c in range(nk):
                    P.mm(tp[:, c * 128:(c + 1) * 128], ktm[:, c * 128:(c + 1) * 128], IDENT, c == 0)
                P.copy(CKF[:, :, i * 128:(i + 1) * 128], tp[:, 0:128 * nk].rearrange("p (c n) -> p c n", c=nk), eng="act")

            if SUB == 4:
                top[0] = m
                return
            if kind in ("swa", "na"):
                MT = [bfv(al(512), 1024) for _ in range(8)]
                TPR = None
                if kind == "na":
                    HK = f32v(al(960), 960)
                    HKb = bfv(al(480), 960)
                    TPR = bfv(al(480), 960)
                for h in range(8):
                    if kind == "swa":
                        qc, base = h % 4, (h // 4) * 64
                        kc_, kbase = 0, (h // 4) * 64
                        vsl = slice((h // 4) * 64, (h // 4) * 64 + 64)
                    else:
                        qc, base = h // 2, (h % 2) * 64
                        kc_, kbase = h // 2, (h % 2) * 64
                        vsl = slice(h * 64, h * 64 + 64)
                    bs = slice(base, base + 64)
                    es = ESINK[0:64, l, h:h + 1] if kind == "swa" else None
                    if kind == "na":
                        P.dma(HK[0:64, :].rearrange("p (e n) -> p e n", e=15),
                              bass.AP(rpbrev.tensor, (l * 8 + h) * 15 * 127, [[1, 64], [127, 15], [1, 64]]), "hk")
                        P.copy(HKb[0:64, :], HK[0:64, :])
                        for e0, en in ((0, 8), (8, 7)):
                            tq = rb()
                            P.mm(tq[:, 0:en * 64], JMW[:, h % 2, :], HKb[0:64, e0 * 64:(e0 + en) * 64], True)
                            ng = NEGC[bs, :]
                            P.stt(TPR[bs, e0 * 64:(e0 + en) * 64].rearrange("p (e n) -> p e n", e=en),
                                  tq[bs, 0:en * 64].rearrange("p (e n) -> p e n", e=en), 8.0,
                                  bass.AP(ng.tensor, ng.offset, [list(ng.ap[0]), [0, en], [1, 64]]), ALU.mult, ALU.add)
                    chunks = []
                    for t_ in range(2):
                        for kc in range(2):
                            parts = []
                            for s in (2 * t_, 2 * t_ + 1):
                                k0 = s * 256 + kc * 128
                                parts.append(dict(kT=KN[kc_][bs, k0:k0 + 128], V=Vb[:, 2 * s + kc, vsl], qlo=s * 256, qhi=s * 256 + 256))
                            chunks.append(dict(parts=parts, qlo=t_ * 512, qhi=t_ * 512 + 512))

                    def fin_p(t, o, d, h=h, es=es):
                        if es is not None:
                            P.ts(RI[0:64, :], d[0:64, :], es, ALU.add)
                            P.recip(RI[0:64, :], RI[0:64, :])
                        else:
                            P.recip(RI[0:64, :], d[0:64, :])
                        P.stt(MT[h][0:64, t * 512:(t + 1) * 512], o[0:64, :], FP_[0:64, :], RI[0:64, :], ALU.mult, ALU.mult)
                    if SUB == 5:
                        continue
                    attend(lambda lo, hi, qc=qc, bs=bs: QN[qc][bs, lo:hi], chunks, 64, fin_p, ptb)
                    if SUB == 6:
                        continue
                    chunks = []
                    for i in range(2):
                        chunks.append(dict(kT=CKF[bs, kc_, i * 128:(i + 1) * 128], V=CVb[:, i, vsl], qlo=0, qhi=1024))
                    if kind == "swa":
                        for j in range(8):
                            qlo = max(0, j - 1) * 128; qhi = min(8, j + 2) * 128
                            mk = BAND[:, (128 if j == 0 else 0):(128 if j == 0 else 0) + (qhi - qlo)]
                            chunks.append(dict(kT=KR[0][bs, j * 128:(j + 1) * 128], V=Vb[:, j, vsl], qlo=qlo, qhi=qhi, mask=mk))
                    else:
                        for j in range(8):
                            if j <= 3:
                                r0, r1 = 0, 2 * j + 5
                                zero = (0, (2 * j + 5) * 64, (2 * j + 6) * 64)
                            else:
                                r0, r1 = 2 * j - 3, 15
                                zero = (64, (2 * j - 3) * 64, (2 * j - 2) * 64)
                            bias = []
                            for i in range(2):
                                kr = 2 * j + i
                                bias.append((SELb[bs, i, :], (lambda lo, hi, kr=kr, bs=bs: TPR[bs, (lo // 64 - kr + 7) * 64:(hi // 64 - kr + 7) * 64])))
                            chunks.append(dict(kT=KR[kc_][bs, j * 128:(j + 1) * 128], V=Vb[:, j, vsl], qlo=r0 * 64, qhi=(r1 + 1) * 64,
                                               bias=bias, zero=zero))

                    def fin_s(t, o, d, h=h, es=es):
                        if es is not None:
                            P.ts(RI[0:64, :], d[0:64, :], es, ALU.add)
                            P.recip(RI[0:64, :], RI[0:64, :])
                        else:
                            P.recip(RI[0:64, :], d[0:64, :])
                        P.stt(scr["t1"][0:64, :], o[0:64, :], FS_[0:64, :], RI[0:64, :], ALU.mult, ALU.mult)
                        P.tt(MT[h][0:64, t * 512:(t + 1) * 512], MT[h][0:64, t * 512:(t + 1) * 512], scr["t1"][0:64, :], ALU.add)
                    attend(lambda lo, hi, qc=qc, bs=bs: QR[qc][bs, lo:hi], chunks, 64, fin_s, ptb)
                if SUB in (5, 6, 7):
                    top[0] = m
                    return
                row0 = 0 if kind == "swa" else 512
                if kind == "swa":
                    order = []
                    for h0 in range(8):
                        order.append(MT[h0])
                    wout_group(l, [MT[hh][0:64, :] for hh in range(8)], 64, row0)
                else:
                    wout_group(l, [MT[hh][0:64, :] for hh in range(8)], 64, row0)
            else:
                MD = QN
                D1 = f32v(al(1024), 1024); D2 = f32v(al(1024), 1024)
                for h in range(4):
                    for style in range(2):
                        QQ, KK = (QN, KN) if style == 0 else (QR, KR)
                        for i2 in range(2):
                            bs = slice(i2 * 64, i2 * 64 + 64)
                            Dd = D1 if i2 == 0 else D2
                            chunks = []
                            if style == 0:
                                for t_ in range(2):
                                    for kc in range(2):
                                        parts = []
                                        for s in (2 * t_, 2 * t_ + 1):
                                            k0 = s * 256 + kc * 128
                                            parts.append(dict(kT=KK[h][bs, k0:k0 + 128], V=Vb[:, 2 * s + kc, h * 128:(h + 1) * 128], qlo=s * 256, qhi=s * 256 + 256))
                                        chunks.append(dict(parts=parts, qlo=t_ * 512, qhi=t_ * 512 + 512))
                            else:
                                for j in range(8):
                                    chunks.append(dict(kT=KK[h][bs, j * 128:(j + 1) * 128], V=Vb[:, j, h * 128:(h + 1) * 128], qlo=0, qhi=1024))
                                for i in range(2):
                                    chunks.append(dict(kT=CKF[bs, h, i * 128:(i + 1) * 128], V=CVb[:, i, h * 128:(h + 1) * 128], qlo=0, qhi=1024))

                            def fin_d(t, o, d, Dd=Dd):
                                P.recip(RI, d[:, :])
                                P.tt(Dd[:, t * 512:(t + 1) * 512], o[:, :], RI, ALU.mult)
                            attend(lambda lo, hi, QQ=QQ, h=h, bs=bs: QQ[h][bs, lo:hi], chunks, 128, fin_d, ptb)
                        P.stt(D1, D2, LAMV[:, 4 * l + 3:4 * l + 4], D1, ALU.mult, ALU.add)
                        for t in range(2):
                            ts_ = slice(t * 512, (t + 1) * 512)
                            P.act(scr["sqb"][0], D1[:, ts_], AF.Square)
                            ms = rb()
                            P.mm(ms[:, :], ONES128, scr["sqb"][0], True)
                            P.act(scr["sd"], ms[:, :], AF.Sqrt, bias=EPSC)
                            P.recip(scr["sd"], scr["sd"])
                            if style == 0:
                                P.stt(MD[h][:, ts_], D1[:, ts_], DSC[:, l, 0:1], scr["sd"], ALU.mult, ALU.mult)
                            else:
                                P.stt(scr["t1"], D1[:, ts_], DSC[:, l, 1:2], scr["sd"], ALU.mult, ALU.mult)
                                P.tt(MD[h][:, ts_], MD[h][:, ts_], scr["t1"], ALU.add)
                wout_group(l, MD, 128, 1024)
            top[0] = m

        attn_mixer("swa")
        if STAGE == 3:
            continue
        attn_mixer("na")
        if STAGE == 4:
            continue
        attn_mixer("diff")
        if STAGE == 5:
            continue

        mod_ab(l, 1)
        modnorm(l, 1)
        if SUB == 11 or (SUB == 13 and l == 1):
            continue
        m = top[0]
        ACT_ = bfv(al(2048), 4096).rearrange("p (c t) -> p c t", c=4)
        sg_ = [f32v(al(512), 512) for _ in range(2)]
        for g in range(11):
            for half in range(2):
                c0 = g * 512 + half * 256
                Wg_ = wslab(wg[l].rearrange("(kc p) n -> p kc n", p=128)[:, :, c0:c0 + 256], [128, 16, 256])
                Wu_ = wslab(wu[l].rearrange("(kc p) n -> p kc n", p=128)[:, :, c0:c0 + 256], [128, 16, 256])
                pgs = {}
                for j in range(2):
                    for t in range(2):
                        ts_ = slice(t * 512, (t + 1) * 512)
                        pg = rb()
                        for kc in range(16):
                            P.mm(pg[:, :], Wg_[:, kc, j * 128:(j + 1) * 128], HB[:, kc, ts_], kc == 0)
                        pgs[(j, t)] = pg
                for j in range(2):
                    for t in range(2):
                        ts_ = slice(t * 512, (t + 1) * 512)
                        pg = pgs[(j, t)]
                        pu = rb()
                        for kc in range(16):
                            P.mm(pu[:, :], Wu_[:, kc, j * 128:(j + 1) * 128], HB[:, kc, ts_], kc == 0)
                        s_ = sg_[(j * 2 + t) % 2]
                        P.act(s_, pg[:, :], AF.Silu)
                        P.tt(ACT_[:, half * 2 + j, ts_], s_, pu[:, :], ALU.mult)
                mod_step(2)
            for c0 in range(0, D, 1024):
                if SUB == 12:
                    continue
                mod_require(l, 48)
                W = wslab(wd[l, g * 512:(g + 1) * 512, c0:c0 + 1024].rearrange("(h p) n -> p h n", p=128), [128, 4, 1024])
                for j in range(8):
                    c = c0 // 128 + j
                    for t in range(2):
                        ts_ = slice(t * 512, (t + 1) * 512)
                        acc = ab()
                        for i in range(4):
                            P.mm(acc[:, :], W[:, i, j * 128:(j + 1) * 128], ACT_[:, i, ts_], i == 0)
                        P.stt(XF[:, c, ts_], acc[:, :], MODV[:, l, 80 + c:81 + c], XF[:, c, ts_], ALU.mult, ALU.add)
                mod_step()
        top[0] = m

    ys = [al(2048), al(2048)]
    yh = bfv(al(1024), 2048).rearrange("p (c n) -> p c n", c=16)
    yl = bfv(al(1024), 2048).rearrange("p (c n) -> p c n", c=16)
    for i in range(8):
        tsl = slice(i * 128, (i + 1) * 128)
        yt = f32v(ys[i % 2], 2048)
        P.copy(yh, XF[:, :, tsl], eng="act")
        P.tt(yl, XF[:, :, tsl], yh, ALU.subtract)
        for g in range(4):
            ps = rb()
            for j in range(4):
                c = g * 4 + j
                P.mm(ps[:, j * 128:(j + 1) * 128], yh[:, c, :], IDENT, j == 0)
                P.mm(ps[:, j * 128:(j + 1) * 128], yl[:, c, :], IDENT, False)
            P.copy(yt[:, g * 512:(g + 1) * 512], ps[:, :], eng=("act" if g % 2 else "dve"))
        P.dma(y_d[tsl, :], yt, "y%d" % (i % 2))
    cnt = P.emit()
    st.close()
    return nc, cnt, len(P.ins)


def _consts():
    t = np.arange(T)
    pos = np.stack([t // 64, t % 64], 0).astype(np.float32)
    inv = (10000.0 ** (-np.arange(16, dtype=np.float32) / 16)).astype(np.float32)
    cos = np.zeros((128, T), np.float32); sin = np.zeros((128, T), np.float32)
    for p in range(128):
        f = p % 64
        ang = (pos[f // 32] * inv[f % 16]).astype(np.float32)
        cos[p] = np.cos(ang); sin[p] = np.sin(ang)
    cm = np.zeros((128, 6, 128), np.float32)
    cm[:, 0, :] = np.eye(128)
    for m in range(128):
        f = m % 64; a = f // 32; half = (f % 32) // 16; j = f % 16; hb = (m // 64) * 64
        if half == 0:
            k = hb + a * 32 + 16 + j; cm[k, 1, m] = -1.0
        else:
            k = hb + a * 32 + j; cm[k, 1, m] = 1.0
    for k in range(128):
        cm[k, 2, (k // 64) * 64:(k // 64) * 64 + 64] = 1.0 / 64
    cm[:, 3, :] = 1.0 / 2048
    cm[:, 4, :] = 1.0 / 128
    cm[:, 5, :] = 1.0
    jmw = np.zeros((64, 2, 128), np.float32)
    for k in range(64):
        jmw[k, 0, 63 - k] = 1.0
        jmw[k, 1, 64 + 63 - k] = 1.0
    sel = np.zeros((128, 2, 128), np.float32)
    for p in range(128):
        for i in range(2):
            sel[p, i, i * 64 + (p % 64)] = 1.0
    cq = np.arange(64)
    cstart = np.clip(cq - 8, 0, 48)
    ok = (cq[None, :] >= cstart[:, None]) & (cq[None, :] < cstart[:, None] + 16)
    negc1 = np.where(ok.T, 0.0, -1e30).astype(np.float32)
    negc = np.concatenate([negc1, negc1], 0)
    b = np.arange(128)[:, None]; a = np.arange(128)[None, :]
    band = np.concatenate([(b <= a), np.ones((128, 128), bool), (a <= b)], 1).astype(np.float32)
    return dict(cos_t=cos, sin_t=sin, cmat=cm, jmw=jmw, sel=sel, negc=negc, band=band)


_CACHE = {}


def kernel(x_prompt, x_sample, cache_swa_k, cache_swa_v, cache_na_k, cache_na_v, cache_diff_k, cache_diff_v,
           state_lru, c, c_ctx, norm_mix, norm_ffn, w_mod, b_mod, w_in, w_out, qk_gain, swa_sink, na_rpb,
           diff_lambda, diff_subln, conv_w, conv_b, lru_wa, lru_ba, lru_wx, lru_bx, lru_L,
           w_ffn_gate, w_ffn_up, w_ffn_down):
    f = lambda a: np.ascontiguousarray(np.asarray(a, dtype=np.float32))
    if "nc" not in _CACHE:
        _CACHE["nc"] = build_nc()[0]
    nc = _CACHE["nc"]
    perm = np.arange(PW)
    qperm = []
    for cc in range(4):
        qperm += list(range(cc * 64, cc * 64 + 64)) + list(range((4 + cc) * 64, (4 + cc) * 64 + 64))
    perm[0:512] = np.array(qperm)
    w_in_p = f(np.asarray(w_in)[:, :, perm])
    rp = np.asarray(na_rpb, np.float32)
    ppad = np.zeros((2, 8, 15, 127), np.float32)
    ppad[..., 48:79] = rp
    prev = ppad[..., ::-1]
    rpbrev = f(prev[:, :, ::-1, :])
    shared = dict(w_mod=f(w_mod), b_mod=f(b_mod), w_in=w_in_p, w_out=f(w_out), wg=f(w_ffn_gate), wu=f(w_ffn_up),
                  wd=f(w_ffn_down), norm_mix=f(norm_mix), norm_ffn=f(norm_ffn), qk_gain=f(np.asarray(qk_gain).reshape(2, 6, 64)),
                  swa_sink=f(swa_sink), rpbrev=rpbrev, dlam=f(np.asarray(diff_lambda).reshape(2, 256)), dsub=f(diff_subln),
                  conv_w=f(conv_w), conv_b=f(conv_b), lru_wa=f(lru_wa), lru_ba=f(lru_ba), lru_wx=f(lru_wx),
                  lru_bx=f(lru_bx), lru_L=f(lru_L))
    shared.update(_consts())
    xp = np.asarray(x_prompt, np.float32); xs = np.asarray(x_sample, np.float32)
    in_maps = []
    for i in range(8):
        m = dict(shared)
        if i < 4:
            m["x"] = f(xp[4 * i:4 * i + 4].reshape(T, D)); m["cvec"] = f(c_ctx)
            m["flags"] = f(np.tile(np.array([[1.0, 0.0]], np.float32), (128, 1)))
            m["c_swa_k"] = np.zeros((2, 256, 128), np.float32); m["c_swa_v"] = np.zeros((2, 256, 128), np.float32)
            m["c_na_k"] = np.zeros((2, 256, 512), np.float32); m["c_na_v"] = np.zeros((2, 256, 512), np.float32)
            m["c_diff_k"] = np.zeros((2, 256, 512), np.float32); m["c_diff_v"] = np.zeros((2, 256, 512), np.float32)
            m["state"] = np.zeros((2, 2, 512), np.float32)
        else:
            b = i - 4
            m["x"] = f(xs[b]); m["cvec"] = f(np.asarray(c)[b])
            m["flags"] = f(np.tile(np.array([[0.0, 1.0]], np.float32), (128, 1)))
            m["c_swa_k"] = f(np.asarray(cache_swa_k)[b].reshape(2, 256, 128)); m["c_swa_v"] = f(np.asarray(cache_swa_v)[b].reshape(2, 256, 128))
            m["c_na_k"] = f(np.asarray(cache_na_k)[b].reshape(2, 256, 512)); m["c_na_v"] = f(np.asarray(cache_na_v)[b].reshape(2, 256, 512))
            m["c_diff_k"] = f(np.asarray(cache_diff_k)[b].reshape(2, 256, 512)); m["c_diff_v"] = f(np.asarray(cache_diff_v)[b].reshape(2, 256, 512))
            m["state"] = f(np.asarray(state_lru)[b])
        in_maps.append(m)
    res = run_bass_kernel_spmd(nc, in_maps, core_ids=list(range(8)))
    R = res.results
    y_prompt = np.concatenate([R[i]["y"].reshape(4, 256, D) for i in range(4)], 0)
    y_sample = np.stack([R[4 + i]["y"] for i in range(4)], 0)

    def gat(name, shp):
        return np.concatenate([np.transpose(R[i][name].reshape((2, 4, 256) + shp), (1, 0, 2) + tuple(range(3, 3 + len(shp)))) for i in range(4)], 0)
    nsk = gat("o_swa_k", (2, 64)); nsv = gat("o_swa_v", (2, 64))
    nnk = gat("o_na_k", (8, 64)); nnv = gat("o_na_v", (8, 64))
    ndk = gat("o_diff_k", (4, 2, 64)); ndv = gat("o_diff_v", (4, 128))
    nst = np.concatenate([np.transpose(R[i]["o_state"], (1, 0, 2, 3)) for i in range(4)], 0)
    return (y_prompt.astype(np.float32), y_sample.astype(np.float32), nsk, nsv, nnk, nnv, ndk, ndv, nst.astype(np.float32))
```

```python
import numpy as np
import concourse.bass as bass
import concourse.mybir as mybir

F32 = mybir.dt.float32
BF16 = mybir.dt.bfloat16
AF = mybir.ActivationFunctionType
ALU = mybir.AluOpType


class _I:
    __slots__ = ("eng", "fn", "deps", "dsem", "signal", "semval", "idx", "total")


def _isz(dt):
    return 2 if dt == BF16 else 4


class Prog:
    ENGS = ("pe", "act", "dve", "pool", "sp")

    def __init__(self, nc):
        self.nc = nc
        self.ins = []
        self.trk = {}
        self.dma_cnt = {}
        self.out_dmas = []

    def _iv(self, ap):
        sp = str(ap.space)
        if "DRAM" in sp.upper():
            return None
        steps = ap.ap
        rs = steps[0][0]
        off = ap.offset
        p0 = off // rs if rs > 0 else 0
        f0 = off - p0 * rs
        lo = f0
        hi = f0
        for st, cnt in steps[1:]:
            ext = st * (cnt - 1)
            if ext < 0:
                lo += ext
            else:
                hi += ext
        isz = _isz(ap.dtype)
        return ((sp, ap.tensor.name), lo * isz, (hi + 1) * isz, p0, p0 + steps[0][1])

    def _rec(self, eng, fn, reads, writes, dsem=None, total=False):
        ins = _I()
        ins.eng = eng
        ins.fn = fn
        ins.dsem = dsem
        ins.signal = False
        ins.semval = 0
        ins.total = total
        ins.idx = len(self.ins)
        deps = set()
        acc = []
        for ap in reads:
            if ap is None or isinstance(ap, (int, float)):
                continue
            iv = self._iv(ap)
            if iv is not None:
                acc.append((iv, False))
        for ap in writes:
            iv = self._iv(ap)
            if iv is not None:
                acc.append((iv, True))
        for (key, lo, hi, plo, phi), isw in acc:
            if "PSUM" in key[0].upper():
                lo, hi, plo, phi, isw = 0, 2048, 0, 128, True
            recs = self.trk.setdefault(key, [])
            keep = []
            for r in recs:
                rlo, rhi, rplo, rphi, ridx, rw = r
                if ridx == ins.idx:
                    keep.append(r)
                    continue
                ov = (rlo < hi and lo < rhi and rplo < phi and plo < rphi)
                if ov and (rw or isw):
                    deps.add(ridx)
                if isw and ov and rlo >= lo and rhi <= hi and rplo >= plo and rphi <= phi:
                    continue
                if (not isw) and (not rw) and rlo == lo and rhi == hi and rplo == plo and rphi == phi \
                        and self.ins[ridx].eng == eng and self.ins[ridx].dsem is None and dsem is None:
                    continue
                keep.append(r)
            keep.append((lo, hi, plo, phi, ins.idx, isw))
            self.trk[key] = keep
        deps.discard(ins.idx)
        best = {}
        red = set()
        for d_ in deps:
            p_ = self.ins[d_]
            if p_.dsem is not None:
                red.add(d_)
            elif best.get(p_.eng, -1) < d_:
                best[p_.eng] = d_
        red.update(best.values())
        deps = red
        ins.deps = deps
        self.ins.append(ins)
        if dsem is not None:
            self.dma_cnt[dsem] = self.dma_cnt.get(dsem, 0) + 16
            ins.semval = self.dma_cnt[dsem]
        return ins

    def mm(self, out, lhsT, rhs, first):
        self._rec("pe", lambda e: e.matmul(out, lhsT, rhs, start=bool(first), stop=True,
                                           skip_group_check=True), [lhsT, rhs], [out])

    def tr(self, out, in_, ident):
        self._rec("pe", lambda e: e.transpose(out, in_, ident), [in_, ident], [out])

    def act(self, out, in_, func, bias=None, scale=None, eng="act"):
        kw = {}
        if bias is not None:
            kw["bias"] = bias
        if scale is not None:
            kw["scale"] = scale
        rd = [in_]
        if bias is not None and not isinstance(bias, (int, float)):
            rd.append(bias)
        if scale is not None and not isinstance(scale, (int, float)):
            rd.append(scale)
        self._rec(eng, lambda e: e.activation(out, in_, func, **kw), rd, [out])

    def tt(self, out, a, b, op, eng="dve"):
        self._rec(eng, lambda e: e.tensor_tensor(out, a, b, op), [a, b], [out])

    def ts(self, out, a, s1, op0, s2=None, op1=None, eng="dve"):
        rd = [a]
        if not isinstance(s1, (int, float)):
            rd.append(s1)
        if s2 is not None and not isinstance(s2, (int, float)):
            rd.append(s2)
        if op1 is None:
            self._rec(eng, lambda e: e.tensor_scalar(out, a, s1, None, op0), rd, [out])
        else:
            self._rec(eng, lambda e: e.tensor_scalar(out, a, s1, s2, op0, op1), rd, [out])

    def stt(self, out, a, s, b, op0, op1):
        rd = [a, b]
        if not isinstance(s, (int, float)):
            rd.append(s)
        self._rec("dve", lambda e: e.scalar_tensor_tensor(out, a, s, b, op0, op1), rd, [out])

    def scan(self, out, d0, d1, init):
        rd = [d0, d1]
        if not isinstance(init, (int, float)):
            rd.append(init)
        self._rec("dve", lambda e: e.tensor_tensor_scan(out, d0, d1, init, ALU.mult, ALU.add), rd, [out])

    def copy(self, out, in_, eng="dve"):
        if eng == "act":
            self._rec(eng, lambda e: e.copy(out, in_), [in_], [out])
        else:
            self._rec(eng, lambda e: e.tensor_copy(out, in_), [in_], [out])

    def memset(self, out, val, eng="dve"):
        self._rec(eng, lambda e: e.memset(out, val), [], [out])

    def recip(self, out, in_):
        self._rec("dve", lambda e: e.reciprocal(out, in_), [in_], [out])

    def dma(self, out, in_, sem, q="sp", total=False, **kw):
        ins = self._rec(q, lambda e: e.dma_start(out, in_, **kw), [in_], [out], dsem=sem, total=total)
        if "DRAM" in str(out.space).upper():
            self.out_dmas.append(ins)
        return ins

    def emit(self):
        nc = self.nc
        ins = self.ins
        for i in ins:
            for d in i.deps:
                p = ins[d]
                if p.dsem is not None:
                    continue
                if p.eng == i.eng and i.dsem is None:
                    if p.eng == "pe":
                        continue
                p.signal = True
        cnt = {e: 0 for e in self.ENGS}
        for i in ins:
            if i.dsem is None and i.signal:
                cnt[i.eng] += 1
                i.semval = cnt[i.eng]
        names = sorted(self.dma_cnt.keys())
        import contextlib
        with contextlib.ExitStack() as st:
            esem = {e: st.enter_context(nc.semaphore("s_" + e)) for e in self.ENGS}
            dsem = {n: st.enter_context(nc.semaphore("d_" + n)) for n in names}
            block = st.enter_context(nc.Block())
            per = {e: [i for i in ins if i.eng == e] for e in self.ENGS}
            final_waits = {}
            for i in self.out_dmas:
                final_waits[i.dsem] = self.dma_cnt[i.dsem]

            def gen(ename, last_waits=None):
                def body(e):
                    seen = {}
                    for i in per[ename]:
                        need = {}
                        for d in i.deps:
                            p = ins[d]
                            if p.dsem is not None:
                                s = ("d", p.dsem)
                                v = self.dma_cnt[p.dsem] if p.total else p.semval
                            else:
                                if p.eng == i.eng and i.dsem is None and p.eng == "pe":
                                    continue
                                if not p.signal:
                                    continue
                                s = ("e", p.eng)
                                v = p.semval
                            if need.get(s, 0) < v:
                                need[s] = v
                        for s, v in need.items():
                            if seen.get(s, 0) >= v:
                                continue
                            seen[s] = v
                            e.wait_ge(dsem[s[1]] if s[0] == "d" else esem[s[1]], v)
                        h = i.fn(e)
                        if i.dsem is not None:
                            h.then_inc(dsem[i.dsem], 16)
                        elif i.signal:
                            h.then_inc(esem[i.eng], 1)
                    if last_waits:
                        for n, v in last_waits.items():
                            e.wait_ge(dsem[n], v)
                return body

            block.tensor(gen("pe"))
            block.scalar(gen("act"))
            block.vector(gen("dve"))
            block.gpsimd(gen("pool"))
            block.sync(gen("sp", final_waits))
        return cnt

import contextlib
import math
from concourse.bass_utils import run_bass_kernel_spmd

D = 2048
T = 1024
HID = 5632
PW = 4864
O_SQ, O_SK, O_SV, O_NQ, O_NK, O_NV, O_DQ, O_DK, O_DV, O_LX, O_LG = 0, 512, 640, 768, 1280, 1792, 2304, 2816, 3328, 3840, 4352
EPS = 1e-6


_DBG = {"on": False, "stage": 99, "sub": 99}


def build_nc():
    nc = bass.Bass("TRN2", target_bir_lowering=False)
    SUB = _DBG["sub"]

    def dbig(name, shape):
        if _DBG["on"]:
            return nc.dram_tensor(name, list(shape), F32, kind="Internal").ap()
        return nc.dram_tensor(name, list(shape), F32, kind="ExternalInput").ap()

    def din(name, shape):
        return nc.dram_tensor(name, list(shape), F32, kind="ExternalInput").ap()

    def dout(name, shape):
        return nc.dram_tensor(name, list(shape), F32, kind="ExternalOutput").ap()

    x_d = din("x", [T, D]); cvec_d = din("cvec", [D]); flags_d = din("flags", [128, 2])
    c_swa_k = din("c_swa_k", [2, 256, 128]); c_swa_v = din("c_swa_v", [2, 256, 128])
    c_na_k = din("c_na_k", [2, 256, 512]); c_na_v = din("c_na_v", [2, 256, 512])
    c_diff_k = din("c_diff_k", [2, 256, 512]); c_diff_v = din("c_diff_v", [2, 256, 512])
    state_d = din("state", [2, 2, 512])
    w_mod = dbig("w_mod", [2, D, 6 * D]); b_mod = din("b_mod", [2, 6 * D])
    w_in = dbig("w_in", [2, D, PW]); w_out = dbig("w_out", [2, D, D])
    wg = dbig("wg", [2, D, HID]); wu = dbig("wu", [2, D, HID]); wd = dbig("wd", [2, HID, D])
    norm_mix = din("norm_mix", [2, D]); norm_ffn = din("norm_ffn", [2, D])
    qk_gain = din("qk_gain", [2, 6, 64]); swa_sink = din("swa_sink", [2, 8])
    rpbrev = din("rpbrev", [2, 8, 15, 127]); dlam = din("dlam", [2, 256]); dsub = din("dsub", [2, 128])
    conv_w = din("conv_w", [2, 4, 512]); conv_b = din("conv_b", [2, 512])
    lru_wa = din("lru_wa", [2, 2, 8, 64, 64]); lru_ba = din("lru_ba", [2, 2, 512])
    lru_wx = din("lru_wx", [2, 2, 8, 64, 64]); lru_bx = din("lru_bx", [2, 2, 512]); lru_L = din("lru_L", [2, 2, 512])
    cos_d = din("cos_t", [128, T]); sin_d = din("sin_t", [128, T])
    cmat_d = din("cmat", [128, 6, 128]); jmw_d = din("jmw", [64, 2, 128]); negc_d = din("negc", [128, 64]); sel_d = din("sel", [128, 2, 128]); band_d = din("band", [128, 384])

    y_d = dout("y", [T, D])
    o_swa_k = dout("o_swa_k", [2, T, 128]); o_swa_v = dout("o_swa_v", [2, T, 128])
    o_na_k = dout("o_na_k", [2, T, 512]); o_na_v = dout("o_na_v", [2, T, 512])
    o_diff_k = dout("o_diff_k", [2, T, 512]); o_diff_v = dout("o_diff_v", [2, T, 512])
    o_state = dout("o_state", [2, 4, 2, 512])

    st = contextlib.ExitStack()
    NW = 53000
    A = st.enter_context(nc.sbuf_tensor("arena", [128, NW], F32))
    PS = [st.enter_context(nc.psum_tensor("ps%d" % i, [128, 512], F32)) for i in range(8)]
    P = Prog(nc)
    top = [0]
    marks = []

    def al(n):
        o = top[0]
        top[0] += n
        assert top[0] <= NW, top[0]
        return o

    def f32v(o, n):
        return A[:, o:o + n]

    def bfv(o, nbf):
        return A[:, o:o + (nbf + 1) // 2].bitcast(BF16)

    rr = [0, 0]

    wide = [True]

    rrole = {"p": [0, (0, 1, 2, 3)], "m": [0, (4, 5)], "r": [0, (6,)], "t": [0, (7,)]}

    def rb(role=None):
        if role is not None:
            st_ = rrole[role]
            st_[0] += 1
            return PS[st_[1][st_[0] % len(st_[1])]]
        rr[0] += 1
        if wide[0]:
            return PS[rr[0] % 8]
        return PS[4 + rr[0] % 4]

    def ab():
        rr[1] += 1
        return PS[rr[1] % 4]

    XF = f32v(al(16 * T), 16 * T).rearrange("p (c t) -> p c t", c=16)
    HB = bfv(al(8 * T), 16 * T).rearrange("p (c t) -> p c t", c=16)
    RING = [al(2048) for _ in range(3)]
    wk = [0]

    def wslab(src, shape):
        s = wk[0] % 3
        wk[0] += 1
        n = 1
        for d_ in shape[1:]:
            n *= d_
        v = bfv(RING[s], 4096)[0:shape[0], 0:n]
        if len(shape) == 3:
            v = v.rearrange("p (a b) -> p a b", a=shape[1])
        if _DBG.get("tinyw") and len(shape) == 3:
            P.dma(v[:, 0:1, :], src[:, 0:1, :], "w%d" % s, q="pool")
        else:
            P.dma(v, src, "w%d" % s, q="pool")
        return v

    COS = f32v(al(T), T); SIN = f32v(al(T), T)
    P.dma(COS, cos_d, "c0", total=True); P.dma(SIN, sin_d, "c0", total=True)
    stg0 = 46000
    cm32 = stg0
    P.dma(f32v(cm32, 768).rearrange("p (k n) -> p k n", k=6), cmat_d, "c0", total=True)
    cmb = al(384)
    CM = bfv(cmb, 768).rearrange("p (k n) -> p k n", k=6)
    P.copy(CM, f32v(cm32, 768).rearrange("p (k n) -> p k n", k=6))
    IDENT, RMAT, BONES, ONESD, ONES128, ONES1 = [CM[:, k, :] for k in range(6)]
    j32 = stg0 + 768; P.dma(A[0:64, j32:j32 + 256].rearrange("p (k n) -> p k n", k=2), jmw_d, "c0", total=True)
    JMW = bfv(al(128), 256)[0:64, :].rearrange("p (k n) -> p k n", k=2); P.copy(JMW, A[0:64, j32:j32 + 256].rearrange("p (k n) -> p k n", k=2))
    s32 = stg0 + 1024; P.dma(f32v(s32, 256).rearrange("p (k n) -> p k n", k=2), sel_d, "c0", total=True)
    SELb = bfv(al(128), 256).rearrange("p (k n) -> p k n", k=2); P.copy(SELb, f32v(s32, 256).rearrange("p (k n) -> p k n", k=2))
    NEGC = f32v(al(64), 64); P.dma(NEGC, negc_d, "c0", total=True)
    ONEC = f32v(al(1), 1); P.memset(ONEC, 1.0)
    b32 = stg0 + 1280; P.dma(f32v(b32, 384), band_d, "c0", total=True)
    BAND = bfv(al(192), 384); P.copy(BAND, f32v(b32, 384))
    FL = f32v(al(2), 2); P.dma(FL, flags_d, "c0", total=True)
    FP_, FS_ = FL[:, 0:1], FL[:, 1:2]
    EPSC = f32v(al(1), 1); P.memset(EPSC, EPS)
    GN1 = f32v(al(32), 32).rearrange("p (l c) -> p l c", l=2)
    GN2 = f32v(al(32), 32).rearrange("p (l c) -> p l c", l=2)
    BMOD = f32v(al(192), 192).rearrange("p (l c) -> p l c", l=2)
    P.dma(GN1, norm_mix.rearrange("l (c p) -> p l c", p=128), "c0", total=True, allow_slow_non_contiguous=True)
    P.dma(GN2, norm_ffn.rearrange("l (c p) -> p l c", p=128), "c0", total=True, allow_slow_non_contiguous=True)
    P.dma(BMOD, b_mod.rearrange("l (c p) -> p l c", p=128), "c0", total=True, allow_slow_non_contiguous=True)
    QKG = f32v(al(12), 12).rearrange("p (l k) -> p l k", l=2)
    for hh in range(2):
        P.dma(A[hh * 64:(hh + 1) * 64, QKG.offset % NW:QKG.offset % NW + 12].rearrange("p (l k) -> p l k", l=2),
              qk_gain.rearrange("l k d -> d l k"), "c0", total=True, allow_slow_non_contiguous=True)
    SINK = f32v(al(16), 16).rearrange("p (l h) -> p l h", l=2)
    P.dma(SINK, bass.AP(swa_sink.tensor, 0, [[0, 128], [8, 2], [1, 8]]), "c0", total=True)
    ESINK = f32v(al(16), 16).rearrange("p (l h) -> p l h", l=2)
    P.act(ESINK, SINK, AF.Exp)
    DLAM = f32v(stg0 + 1664, 512).rearrange("p (l k) -> p l k", l=2)
    P.dma(DLAM, bass.AP(dlam.tensor, 0, [[0, 128], [256, 2], [1, 256]]), "c0", total=True)
    DSUB = f32v(al(2), 2)
    P.dma(DSUB, dsub.rearrange("l p -> p l"), "c0", total=True, allow_slow_non_contiguous=True)
    CW = f32v(al(32), 32).rearrange("p (l k c) -> p l k c", l=2, k=4)
    P.dma(CW, conv_w.rearrange("l k (c p) -> p l k c", p=128), "c0", total=True, allow_slow_non_contiguous=True)
    CB = f32v(al(8), 8).rearrange("p (l c) -> p l c", l=2)
    P.dma(CB, conv_b.rearrange("l (c p) -> p l c", p=128), "c0", total=True, allow_slow_non_contiguous=True)
    LBA = f32v(al(16), 16).rearrange("p (l d c) -> p l d c", l=2, d=2)
    LBX = f32v(al(16), 16).rearrange("p (l d c) -> p l d c", l=2, d=2)
    LL = f32v(al(16), 16).rearrange("p (l d c) -> p l d c", l=2, d=2)
    P.dma(LBA, lru_ba.rearrange("l d (c p) -> p l d c", p=128), "c0", total=True, allow_slow_non_contiguous=True)
    P.dma(LBX, lru_bx.rearrange("l d (c p) -> p l d c", p=128), "c0", total=True, allow_slow_non_contiguous=True)
    P.dma(LL, lru_L.rearrange("l d (c p) -> p l d c", p=128), "c0", total=True, allow_slow_non_contiguous=True)
    H0S = f32v(al(16), 16).rearrange("p (l d c) -> p l d c", l=2, d=2)
    P.dma(H0S, state_d.rearrange("l d (c p) -> p l d c", p=128), "c0", total=True, allow_slow_non_contiguous=True)
    CL = f32v(al(16), 16).rearrange("p (l d c) -> p l d c", l=2, d=2)
    P.act(CL, LL, AF.Exp, scale=-1.0)
    P.act(CL, CL, AF.Ln, bias=ONEC)
    P.ts(CL, CL, -8.0, ALU.mult)
    LAMV = f32v(al(8), 8)
    dl_t = f32v(stg0 + 2176, 128)
    for l in range(2):
        li = 0.8 - 0.6 * math.exp(-0.3 * l)
        sacc = LAMV[:, 4 * l:4 * l + 2]
        P.tt(dl_t.rearrange("p (a d) -> p a d", a=2), DLAM[:, l, :].rearrange("p (a b d) -> p a b d", a=2, b=2)[:, :, 0, :],
             DLAM[:, l, :].rearrange("p (a b d) -> p a b d", a=2, b=2)[:, :, 1, :], ALU.mult)
        P._rec("dve", (lambda o_, i_: (lambda e: e.reduce_sum(o_, i_, axis=mybir.AxisListType.X)))(sacc, dl_t.rearrange("p (a d) -> p a d", a=2)),
               [dl_t], [sacc])
        P.act(sacc, sacc, AF.Exp)
        P.tt(LAMV[:, 4 * l + 2:4 * l + 3], sacc[:, 0:1], sacc[:, 1:2], ALU.subtract)
        P.ts(LAMV[:, 4 * l + 3:4 * l + 4], LAMV[:, 4 * l + 2:4 * l + 3], li, ALU.add, -1.0, ALU.mult)
    DSC = f32v(al(4), 4).rearrange("p (l s) -> p l s", l=2)
    for l in range(2):
        li = 0.8 - 0.6 * math.exp(-0.3 * l)
        P.ts(DSC[:, l, 0:1], DSUB[:, l:l + 1], 1.0 - li, ALU.mult, FP_, ALU.mult)
        P.ts(DSC[:, l, 1:2], DSUB[:, l:l + 1], 1.0 - li, ALU.mult, FS_, ALU.mult)
    MODV = f32v(al(192), 192).rearrange("p (l c) -> p l c", l=2)
    AB = f32v(al(128), 128).rearrange("p (l k c) -> p l k c", l=2, k=4)
    CV = f32v(al(16), 16)
    P.dma(CV, cvec_d.rearrange("(c p) -> p c", p=128), "c0", total=True, allow_slow_non_contiguous=True)
    SCV = bfv(al(8), 16)
    P.act(SCV, CV, AF.Silu)
    static_top = top[0]

    m0 = top[0]
    xs = [al(2048), al(2048)]
    xh = bfv(al(1024), 2048); xl = bfv(al(1024), 2048)
    for i in range(8):
        xt = f32v(xs[i % 2], 2048)
        P.dma(xt, x_d[i * 128:(i + 1) * 128, :], "x%d" % (i % 2))
        P.copy(xh, xt, eng="act")
        P.tt(xl, xt, xh, ALU.subtract)
        for g in range(4):
            ps = rb()
            for j in range(4):
                c = g * 4 + j
                P.mm(ps[:, j * 128:(j + 1) * 128], xh[:, c * 128:(c + 1) * 128], IDENT, j == 0)
                P.mm(ps[:, j * 128:(j + 1) * 128], xl[:, c * 128:(c + 1) * 128], IDENT, False)
            P.copy(XF[:, g * 4:(g + 1) * 4, i * 128:(i + 1) * 128], ps[:, :].rearrange("p (j n) -> p j n", j=4),
                   eng=("act" if g % 2 else "dve"))
    top[0] = m0

    modq = [(l_, s_) for l_ in range(2) for s_ in range(48)]
    modpos = [0]

    def mod_step(n=1):
        for _ in range(n):
            if modpos[0] >= len(modq):
                return
            l_, s_ = modq[modpos[0]]
            modpos[0] += 1
            W = wslab(w_mod[l_].rearrange("(kc p) n -> p kc n", p=128)[:, :, s_ * 256:(s_ + 1) * 256], [128, 16, 256])
            ps = rb("m")
            for j in range(2):
                for kc in range(16):
                    P.mm(ps[:, j:j + 1], W[:, kc, j * 128:(j + 1) * 128], SCV[:, kc:kc + 1], (j == 0 and kc == 0))
            P.tt(MODV[:, l_, 2 * s_:2 * s_ + 2], ps[:, 0:2], BMOD[:, l_, 2 * s_:2 * s_ + 2], ALU.add)

    def mod_require(l_, upto):
        while modpos[0] < l_ * 48 + upto:
            mod_step(1)

    def mod_ab(l_, which):
        if which == 0:
            mod_require(l_, 16)
            P.stt(AB[:, l_, 0, :], MODV[:, l_, 16:32], 1.0, GN1[:, l_, :], ALU.add, ALU.mult)
            P.copy(AB[:, l_, 1, :], MODV[:, l_, 0:16])
        else:
            mod_require(l_, 40)
            P.stt(AB[:, l_, 2, :], MODV[:, l_, 64:80], 1.0, GN2[:, l_, :], ALU.add, ALU.mult)
            P.copy(AB[:, l_, 3, :], MODV[:, l_, 48:64])

    def modnorm(l, which):
        m = top[0]
        sqb = [bfv(al(256), 512) for _ in range(2)]
        RS = f32v(al(512), 512)
        tmp = [f32v(al(512), 512) for _ in range(2)]
        for t in range(2):
            ts_ = slice(t * 512, (t + 1) * 512)
            ms = rb()
            for c in range(16):
                P.act(sqb[c % 2], XF[:, c, ts_], AF.Square)
                P.mm(ms[:, :], ONESD, sqb[c % 2], c == 0)
            P.act(RS, ms[:, :], AF.Sqrt, bias=EPSC)
            P.recip(RS, RS)
            for c in range(16):
                P.tt(tmp[c % 2], XF[:, c, ts_], RS, ALU.mult)
                P.act(HB[:, c, ts_], tmp[c % 2], AF.Identity, bias=AB[:, l, 2 * which + 1, c:c + 1], scale=AB[:, l, 2 * which, c:c + 1])
        top[0] = m

    def proj_fm(l, col0, ncols, consume):
        for s0 in range(0, ncols, 256):
            nn = min(256, ncols - s0)
            W = wslab(w_in[l].rearrange("(kc p) n -> p kc n", p=128)[:, :, col0 + s0:col0 + s0 + nn], [128, 16, nn])
            for j in range(nn // 128):
                for t in range(2):
                    ps = rb("p")
                    for kc in range(16):
                        P.mm(ps[:, :], W[:, kc, j * 128:(j + 1) * 128], HB[:, kc, t * 512:(t + 1) * 512], kc == 0)
                    consume((s0 + j * 128) // 128, t, ps)
            mod_step()

    qpend = []
    qcnt = [0]

    def qk_advance():
        for u in list(qpend):
            u.pop(0)()
            if not u:
                qpend.remove(u)

    def qk_flush():
        while qpend:
            qk_advance()

    def qk_unit(l, ps, t, gidx, dst_n, dst_r, kout, scr):
        ts_ = slice(t * 512, (t + 1) * 512)
        n_ = qcnt[0]
        qcnt[0] += 1
        sqb = scr["sqb"][n_ % 2]; QN = scr["QN"][n_ % 2]; lo = scr["lo"][n_ % 2]
        sd, t1, t2 = scr["sd"], scr["t1"], scr["t2"]

        def c1():
            P.act(sqb, ps[:, :], AF.Square)
            ms = rb("m")
            P.mm(ms[:, :], BONES, sqb, True)
            P.act(sd, ms[:, :], AF.Sqrt, bias=EPSC)
            P.recip(sd, sd)
            P.stt(QN, ps[:, :], QKG[:, l, gidx:gidx + 1], sd, ALU.mult, ALU.mult)
            P.copy(dst_n[:, ts_], QN, eng="act")

        def c2():
            if dst_r is not None:
                rq = rb("r")
                P.mm(rq[:, :], RMAT, dst_n[:, ts_], True)
                P.tt(t1, QN, COS[:, ts_], ALU.mult)
                P.tt(t2, rq[:, :], SIN[:, ts_], ALU.mult)
                P.tt(dst_r[:, ts_], t1, t2, ALU.add)
            if kout is not None:
                P.tt(lo, QN, dst_n[:, ts_], ALU.subtract)

        def c3():
            if kout is not None:
                od, c0, stg = kout
                tp = rb("t")
                for j in range(4):
                    P.mm(tp[:, j * 128:(j + 1) * 128], dst_n[:, t * 512 + j * 128:t * 512 + (j + 1) * 128], IDENT, j == 0)
                    P.mm(tp[:, j * 128:(j + 1) * 128], lo[:, j * 128:(j + 1) * 128], IDENT, False)
                P.copy(stg, tp[:, :], eng="act")
                P.dma(od[l].rearrange("(j p) f -> p j f", p=128)[:, t * 4:(t + 1) * 4, c0:c0 + 128],
                      stg.rearrange("p (j f) -> p j f", j=4), "so%d" % t)
        qk_advance()
        qpend.append([c1, c2, c3])

    def proj_v(l, col0, ncols, Vb, od, stgs):
        for s0 in range(0, ncols, 256):
            nn = min(256, ncols - s0)
            W = wslab(w_in[l].rearrange("(kc p) n -> p kc n", p=128)[:, :, col0 + s0:col0 + s0 + nn], [128, 16, nn])
            for i in range(8):
                ps = rb()
                for kc in range(16):
                    P.mm(ps[:, 0:nn], HB[:, kc, i * 128:(i + 1) * 128], W[:, kc, :], kc == 0)
                P.copy(Vb[:, i, s0:s0 + nn], ps[:, 0:nn], eng="act")
                sg = stgs[i % 2]
                P.copy(sg[:, 0:nn], ps[:, 0:nn])
                P.dma(od[l, i * 128:(i + 1) * 128, s0:s0 + nn], sg[:, 0:nn], "sv%d" % (i % 2))
            mod_step()

    def attend(qfn, chunks, dv, finish, ptb, base=0):
        wide[0] = False
        jobs = []
        for ch in chunks:
            for t in range(2):
                lo = max(ch["qlo"], 512 * t); hi = min(ch["qhi"], 512 * (t + 1))
                if lo < hi:
                    jobs.append((ch, t, lo, hi))
        first = [True, True]

        def p1(k):
            ch, t, lo, hi = jobs[k]
            n = hi - lo
            a0 = lo - 512 * t
            sp = rb()
            if ch.get("parts") is not None:
                for ip, pr in enumerate(ch["parts"]):
                    P.mm(sp[:, pr["qlo"] - 512 * t:pr["qhi"] - 512 * t], pr["kT"], qfn(pr["qlo"], pr["qhi"]), ip == 0)
            else:
                P.mm(sp[:, a0:a0 + n], ch["kT"], qfn(lo, hi), True)
            if ch.get("bias") is not None:
                for sel, fn_ in ch["bias"]:
                    P.mm(sp[:, a0:a0 + n], sel, fn_(lo, hi), False)
            pt = ptb[k % len(ptb)]
            P.act(pt[:, 0:n], sp[:, a0:a0 + n], AF.Exp, scale=0.125)
            if ch.get("mask") is not None:
                mk = ch["mask"]
                P.tt(pt[:, 0:n], pt[:, 0:n], mk[:, lo - ch["qlo"]:hi - ch["qlo"]], ALU.mult)
            if ch.get("zero") is not None:
                p0, c0, c1 = ch["zero"]
                z0 = max(c0, lo); z1 = min(c1, hi)
                if z0 < z1:
                    P.memset(pt[p0:p0 + 64, z0 - lo:z1 - lo], 0.0)

        def p2(k):
            ch, t, lo, hi = jobs[k]
            n = hi - lo
            a0 = lo - 512 * t
            pt = ptb[k % len(ptb)]
            if ch.get("parts") is not None:
                for pr in ch["parts"]:
                    b0 = pr["qlo"] - 512 * t; b1 = pr["qhi"] - 512 * t
                    P.mm(PS[t][0:dv, b0:b1], pr["V"], pt[:, pr["qlo"] - lo:pr["qhi"] - lo], first[t])
                    P.mm(PS[2 + t][0:dv, b0:b1], ONES1[:, 0:dv], pt[:, pr["qlo"] - lo:pr["qhi"] - lo], first[t])
                    first[t] = False
                return
            P.mm(PS[t][0:dv, a0:a0 + n], ch["V"], pt[:, 0:n], first[t])
            P.mm(PS[2 + t][0:dv, a0:a0 + n], ONES1[:, 0:dv], pt[:, 0:n], first[t])
            first[t] = False

        jobs.sort(key=lambda jb: jb[1])
        nj = len(jobs)
        LA = 2
        last_of = {}
        for k, jb in enumerate(jobs):
            last_of[jb[1]] = k
        for k in range(min(LA, nj)):
            p1(k)
        for k in range(nj):
            if k + LA < nj:
                p1(k + LA)
            p2(k)
            if last_of[jobs[k][1]] == k:
                finish(jobs[k][1], PS[jobs[k][1]], PS[2 + jobs[k][1]])
        wide[0] = True

    def wout_group(l, tiles, K, row0):
        nk = len(tiles)
        cols = 4096 // nk
        mod_require(l, 24)
        for c0 in range(0, D, cols):
            W = wslab(w_out[l, row0:row0 + nk * K, c0:c0 + cols].rearrange("(h p) n -> p h n", p=K), [K, nk, cols])
            for j in range(cols // 128):
                c = (c0 // 128) + j
                for t in range(2):
                    acc = ab()
                    for i in range(nk):
                        P.mm(acc[:, :], W[:, i, j * 128:(j + 1) * 128], tiles[i][:, t * 512:(t + 1) * 512], i == 0)
                    P.stt(XF[:, c, t * 512:(t + 1) * 512], acc[:, :], MODV[:, l, 32 + c:33 + c], XF[:, c, t * 512:(t + 1) * 512], ALU.mult, ALU.add)
            mod_step()

    STAGE = _DBG["stage"]
    for l in range(2 if STAGE > 0 else 0):
        mod_ab(l, 0)
        modnorm(l, 0)
        if STAGE == 1:
            continue
        lay_mark = top[0]
        m = top[0]
        LXP = f32v(al(4 * 1036), 4 * 1036).rearrange("p (c s w) -> p c s w", c=4, s=4)
        P.memset(LXP, 0.0)
        Gb = bfv(al(2048), 4096).rearrange("p (c t) -> p c t", c=4)
        MIXL = bfv(al(2048), 4096).rearrange("p (c t) -> p c t", c=4)
        STB = f32v(al(32), 32).rearrange("p (s d c) -> p s d c", s=4, d=2)

        def lx_consume(c, t, ps):
            P.copy(LXP[:, c, 2 * t:2 * t + 2, 2:258], ps[:, :].rearrange("p (s w) -> p s w", s=2), eng="act")
        proj_fm(l, O_LX, 512, lx_consume)

        def lg_consume(c, t, ps):
            P.act(Gb[:, c, t * 512:(t + 1) * 512], ps[:, :], AF.Gelu_apprx_tanh)
        proj_fm(l, O_LG, 512, lg_consume)
        U = f32v(al(1024), 1024); UB = bfv(al(512), 1024)
        Rb = f32v(al(1024), 1024); Ib = f32v(al(1024), 1024); Ab = f32v(al(1024), 1024)
        Hb = [f32v(al(1024), 1024), f32v(al(1024), 1024)]
        WBD = bfv(al(256), 512).rearrange("p (k n) -> p k n", k=4)
        wst = f32v(al(512), 512).rearrange("p (k n) -> p k n", k=4)
        for c in range(4):
            P.ts(LXP[:, c, 1:4, 0:2], LXP[:, c, 0:3, 256:258], FS_, ALU.mult)
            P.ts(LXP[:, c, 0:3, 258:259], LXP[:, c, 1:4, 2:3], FS_, ALU.mult)
            U4 = U.rearrange("p (s w) -> p s w", s=4)
            P.act(U4, LXP[:, c, :, 0:256], AF.Identity, bias=CB[:, l, c:c + 1], scale=CW[:, l, 0, c:c + 1])
            for k in range(1, 4):
                P.stt(U4, LXP[:, c, :, k:k + 256], CW[:, l, k, c:c + 1], U4, ALU.mult, ALU.add)
            P.copy(UB, U, eng="act")
            P.memset(wst, 0.0)
            for d_ in range(2):
                for g_, wsrc in enumerate((lru_wa, lru_wx)):
                    for b_ in range(2):
                        P.dma(A[b_ * 64:(b_ + 1) * 64, wst.offset % NW + (d_ * 2 + g_) * 128 + b_ * 64: wst.offset % NW + (d_ * 2 + g_) * 128 + b_ * 64 + 64],
                              wsrc[l, d_, 2 * c + b_], "lw%d" % (d_ * 4 + g_ * 2 + b_))
            P.copy(WBD, wst)
            for d_ in range(2):
                for t in range(2):
                    ts_ = slice(t * 512, (t + 1) * 512)
                    pr = rb()
                    P.mm(pr[:, :], WBD[:, d_ * 2, :], UB[:, ts_], True)
                    P.act(Rb[:, ts_], pr[:, :], AF.Sigmoid, bias=LBA[:, l, d_, c:c + 1])
                    pi = rb()
                    P.mm(pi[:, :], WBD[:, d_ * 2 + 1, :], UB[:, ts_], True)
                    P.act(Ib[:, ts_], pi[:, :], AF.Sigmoid, bias=LBX[:, l, d_, c:c + 1])
                P.act(Ab, Rb, AF.Exp, scale=CL[:, l, d_, c:c + 1])
                P.act(Rb, Ab, AF.Square)
                P.ts(Rb, Rb, -1.0, ALU.mult, 1.0, ALU.add)
                P.act(Rb, Rb, AF.Sqrt)
                P.tt(Ib, Ib, U, ALU.mult)
                P.tt(Ib, Ib, Rb, ALU.mult)
                if d_ == 0:
                    P.ts(Ab[:, 256:1024:256], Ab[:, 256:1024:256], FS_, ALU.mult)
                    P.scan(Hb[0], Ab, Ib, H0S[:, l, 0, c:c + 1])
                else:
                    P.ts(Ab[:, 255:1023:256], Ab[:, 255:1023:256], FS_, ALU.mult)
                    P.scan(Hb[1][:, ::-1], Ab[:, ::-1], Ib[:, ::-1], H0S[:, l, 1, c:c + 1])
            P.copy(STB[:, :, 0, c], Hb[0][:, 255:1024:256])
            P.copy(STB[:, :, 1, c], Hb[1][:, 0:1024:256])
            P.tt(Hb[0], Hb[0], Hb[1], ALU.add)
            P.tt(MIXL[:, c, :], Hb[0], Gb[:, c, :], ALU.mult)
        P.dma(o_state[l].rearrange("s d (c p) -> p s d c", p=128), STB, "sst", allow_slow_non_contiguous=True)
        wout_group(l, [MIXL[:, c, :] for c in range(4)], 128, 1536)
        top[0] = m
        if STAGE == 2:
            continue

        def attn_mixer(kind):
            m = top[0]
            nq = 4
            nk = 1 if kind == "swa" else 4
            oq = {"swa": O_SQ, "na": O_NQ, "diff": O_DQ}[kind]
            ok = {"swa": O_SK, "na": O_NK, "diff": O_DK}[kind]
            ov = {"swa": O_SV, "na": O_NV, "diff": O_DV}[kind]
            nvc = 128 if kind == "swa" else 512
            odk = {"swa": o_swa_k, "na": o_na_k, "diff": o_diff_k}[kind]
            odv = {"swa": o_swa_v, "na": o_na_v, "diff": o_diff_v}[kind]
            ck = {"swa": c_swa_k, "na": c_na_k, "diff": c_diff_k}[kind]
            cv = {"swa": c_swa_v, "na": c_na_v, "diff": c_diff_v}[kind]
            gi = {"swa": 0, "na": 2, "diff": 4}[kind]
            rope = kind != "na"
            QN = [bfv(al(512), 1024) for _ in range(nq)]
            KN = [bfv(al(512), 1024) for _ in range(nk)]
            QR = [bfv(al(512), 1024) for _ in range(nq)] if rope else QN
            KR = [bfv(al(512), 1024) for _ in range(nk)] if rope else KN
            Vb = bfv(al(4 * nvc), 8 * nvc).rearrange("p (i n) -> p i n", i=8)
            CKF = bfv(al(128 * nk), 256 * nk).rearrange("p (c n) -> p c n", c=nk)
            CVb = bfv(al(nvc), 2 * nvc).rearrange("p (i n) -> p i n", i=2)
            scr = dict(sqb=[bfv(al(256), 512)], sd=f32v(al(512), 512), QN=[f32v(al(512), 512)], t1=f32v(al(512), 512), lo=[bfv(al(256), 512)])
            stg = [f32v(al(512), 512), f32v(al(512), 512)]
            ptb = [bfv(al(256), 512) for _ in range(3)]
            RI = f32v(al(512), 512)
            ex0 = top[0]
            scr["sqb"].append(bfv(al(256), 512)); scr["QN"].append(f32v(al(512), 512)); scr["lo"].append(bfv(al(256), 512)); scr["t2"] = f32v(al(512), 512)
            top[0] = ex0
            proj_fm(l, oq, 512, lambda c, t, ps: qk_unit(l, ps, t, gi, QN[c], QR[c] if rope else None, None, scr))
            if SUB == 1:
                qk_flush()
                top[0] = m
                return
            proj_fm(l, ok, 128 * nk, lambda c, t, ps: qk_unit(l, ps, t, gi + 1, KN[c], KR[c] if rope else None, (odk, c * 128, stg[t]), scr))
            qk_flush()
            if SUB == 2:
                top[0] = m
                return
            proj_v(l, ov, nvc, Vb, odv, stg)
            if SUB == 3:
                top[0] = m
                return
            for i in range(2):
                sg = stg[i]
                P.dma(sg[:, 0:nvc], cv[l, i * 128:(i + 1) * 128, :], "cl%d" % i)
                P.copy(CVb[:, i, :], sg[:, 0:nvc])
            ktm = bfv(al(256), 512)
            for i in range(2):
                sg = stg[i]
                P.dma(sg[:, 0:128 * nk], ck[l, i * 128:(i + 1) * 128, :], "cl%d" % i)
                P.copy(ktm[:, 0:128 * nk], sg[:, 0:128 * nk])
                tp = rb()
                for c in range(nk):
                    P.mm(tp[:, c * 128:(c + 1) * 128], ktm[:, c * 128:(c + 1) * 128], IDENT, c == 0)
                P.copy(CKF[:, :, i * 128:(i + 1) * 128], tp[:, 0:128 * nk].rearrange("p (c n) -> p c n", c=nk), eng="act")

            if SUB == 4:
                top[0] = m
                return
            if kind in ("swa", "na"):
                MT = [bfv(al(512), 1024) for _ in range(8)]
                TPR = None
                if kind == "na":
                    HK = f32v(al(960), 960)
                    HKb = bfv(al(480), 960)
                    TPR = bfv(al(480), 960)
                for h in range(8):
                    if kind == "swa":
                        qc, base = h % 4, (h // 4) * 64
                        kc_, kbase = 0, (h // 4) * 64
                        vsl = slice((h // 4) * 64, (h // 4) * 64 + 64)
                    else:
                        qc, base = h // 2, (h % 2) * 64
                        kc_, kbase = h // 2, (h % 2) * 64
                        vsl = slice(h * 64, h * 64 + 64)
                    bs = slice(base, base + 64)
                    es = ESINK[0:64, l, h:h + 1] if kind == "swa" else None
                    if kind == "na":
                        P.dma(HK[0:64, :].rearrange("p (e n) -> p e n", e=15),
                              bass.AP(rpbrev.tensor, (l * 8 + h) * 15 * 127, [[1, 64], [127, 15], [1, 64]]), "hk")
                        P.copy(HKb[0:64, :], HK[0:64, :])
                        for e0, en in ((0, 8), (8, 7)):
                            tq = rb()
                            P.mm(tq[:, 0:en * 64], JMW[:, h % 2, :], HKb[0:64, e0 * 64:(e0 + en) * 64], True)
                            ng = NEGC[bs, :]
                            P.stt(TPR[bs, e0 * 64:(e0 + en) * 64].rearrange("p (e n) -> p e n", e=en),
                                  tq[bs, 0:en * 64].rearrange("p (e n) -> p e n", e=en), 8.0,
                                  bass.AP(ng.tensor, ng.offset, [list(ng.ap[0]), [0, en], [1, 64]]), ALU.mult, ALU.add)
                    chunks = []
                    for t_ in range(2):
                        for kc in range(2):
                            parts = []
                            for s in (2 * t_, 2 * t_ + 1):
                                k0 = s * 256 + kc * 128
                                parts.append(dict(kT=KN[kc_][bs, k0:k0 + 128], V=Vb[:, 2 * s + kc, vsl], qlo=s * 256, qhi=s * 256 + 256))
                            chunks.append(dict(parts=parts, qlo=t_ * 512, qhi=t_ * 512 + 512))

                    def fin_p(t, o, d, h=h, es=es):
                        if es is not None:
                            P.ts(RI[0:64, :], d[0:64, :], es, ALU.add)
                            P.recip(RI[0:64, :], RI[0:64, :])
                        else:
                            P.recip(RI[0:64, :], d[0:64, :])
                        P.stt(MT[h][0:64, t * 512:(t + 1) * 512], o[0:64, :], FP_[0:64, :], RI[0:64, :], ALU.mult, ALU.mult)
                    if SUB == 5:
                        continue
                    attend(lambda lo, hi, qc=qc, bs=bs: QN[qc][bs, lo:hi], chunks, 64, fin_p, ptb)
                    if SUB == 6:
                        continue
                    chunks = []
                    for i in range(2):
                        chunks.append(dict(kT=CKF[bs, kc_, i * 128:(i + 1) * 128], V=CVb[:, i, vsl], qlo=0, qhi=1024))
                    if kind == "swa":
                        for j in range(8):
                            qlo = max(0, j - 1) * 128; qhi = min(8, j + 2) * 128
                            mk = BAND[:, (128 if j == 0 else 0):(128 if j == 0 else 0) + (qhi - qlo)]
                            chunks.append(dict(kT=KR[0][bs, j * 128:(j + 1) * 128], V=Vb[:, j, vsl], qlo=qlo, qhi=qhi, mask=mk))
                    else:
                        for j in range(8):
                            if j <= 3:
                                r0, r1 = 0, 2 * j + 5
                                zero = (0, (2 * j + 5) * 64, (2 * j + 6) * 64)
                            else:
                                r0, r1 = 2 * j - 3, 15
                                zero = (64, (2 * j - 3) * 64, (2 * j - 2) * 64)
                            bias = []
                            for i in range(2):
                                kr = 2 * j + i
                                bias.append((SELb[bs, i, :], (lambda lo, hi, kr=kr, bs=bs: TPR[bs, (lo // 64 - kr + 7) * 64:(hi // 64 - kr + 7) * 64])))
                            chunks.append(dict(kT=KR[kc_][bs, j * 128:(j + 1) * 128], V=Vb[:, j, vsl], qlo=r0 * 64, qhi=(r1 + 1) * 64,
                                               bias=bias, zero=zero))

                    def fin_s(t, o, d, h=h, es=es):
                        if es is not None:
                            P.ts(RI[0:64, :], d[0:64, :], es, ALU.add)
                            P.recip(RI[0:64, :], RI[0:64, :])
                        else:
                            P.recip(RI[0:64, :], d[0:64, :])
                        P.stt(scr["t1"][0:64, :], o[0:64, :], FS_[0:64, :], RI[0:64, :], ALU.mult, ALU.mult)
                        P.tt(MT[h][0:64, t * 512:(t + 1) * 512], MT[h][0:64, t * 512:(t + 1) * 512], scr["t1"][0:64, :], ALU.add)
                    attend(lambda lo, hi, qc=qc, bs=bs: QR[qc][bs, lo:hi], chunks, 64, fin_s, ptb)
                if SUB in (5, 6, 7):
                    top[0] = m
                    return
                row0 = 0 if kind == "swa" else 512
                if kind == "swa":
                    order = []
                    for h0 in range(8):
                        order.append(MT[h0])
                    wout_group(l, [MT[hh][0:64, :] for hh in range(8)], 64, row0)
                else:
                    wout_group(l, [MT[hh][0:64, :] for hh in range(8)], 64, row0)
            else:
                MD = QN
                D1 = f32v(al(1024), 1024); D2 = f32v(al(1024), 1024)
                for h in range(4):
                    for style in range(2):
                        QQ, KK = (QN, KN) if style == 0 else (QR, KR)
                        for i2 in range(2):
                            bs = slice(i2 * 64, i2 * 64 + 64)
                            Dd = D1 if i2 == 0 else D2
                            chunks = []
                            if style == 0:
                                for t_ in range(2):
                                    for kc in range(2):
                                        parts = []
                                        for s in (2 * t_, 2 * t_ + 1):
                                            k0 = s * 256 + kc * 128
                                            parts.append(dict(kT=KK[h][bs, k0:k0 + 128], V=Vb[:, 2 * s + kc, h * 128:(h + 1) * 128], qlo=s * 256, qhi=s * 256 + 256))
                                        chunks.append(dict(parts=parts, qlo=t_ * 512, qhi=t_ * 512 + 512))
                            else:
                                for j in range(8):
                                    chunks.append(dict(kT=KK[h][bs, j * 128:(j + 1) * 128], V=Vb[:, j, h * 128:(h + 1) * 128], qlo=0, qhi=1024))
                                for i in range(2):
                                    chunks.append(dict(kT=CKF[bs, h, i * 128:(i + 1) * 128], V=CVb[:, i, h * 128:(h + 1) * 128], qlo=0, qhi=1024))

                            def fin_d(t, o, d, Dd=Dd):
                                P.recip(RI, d[:, :])
                                P.tt(Dd[:, t * 512:(t + 1) * 512], o[:, :], RI, ALU.mult)
                            attend(lambda lo, hi, QQ=QQ, h=h, bs=bs: QQ[h][bs, lo:hi], chunks, 128, fin_d, ptb)
                        P.stt(D1, D2, LAMV[:, 4 * l + 3:4 * l + 4], D1, ALU.mult, ALU.add)
                        for t in range(2):
                            ts_ = slice(t * 512, (t + 1) * 512)
                            P.act(scr["sqb"][0], D1[:, ts_], AF.Square)
                            ms = rb()
                            P.mm(ms[:, :], ONES128, scr["sqb"][0], True)
                            P.act(scr["sd"], ms[:, :], AF.Sqrt, bias=EPSC)
                            P.recip(scr["sd"], scr["sd"])
                            if style == 0:
                                P.stt(MD[h][:, ts_], D1[:, ts_], DSC[:, l, 0:1], scr["sd"], ALU.mult, ALU.mult)
                            else:
                                P.stt(scr["t1"], D1[:, ts_], DSC[:, l, 1:2], scr["sd"], ALU.mult, ALU.mult)
                                P.tt(MD[h][:, ts_], MD[h][:, ts_], scr["t1"], ALU.add)
                wout_group(l, MD, 128, 1024)
            top[0] = m

        attn_mixer("swa")
        if STAGE == 3:
            continue
        attn_mixer("na")
        if STAGE == 4:
            continue
        attn_mixer("diff")
        if STAGE == 5:
            continue

        mod_ab(l, 1)
        modnorm(l, 1)
        if SUB == 11 or (SUB == 13 and l == 1):
            continue
        m = top[0]
        ACT_ = bfv(al(2048), 4096).rearrange("p (c t) -> p c t", c=4)
        sg_ = [f32v(al(512), 512) for _ in range(2)]
        for g in range(11):
            for half in range(2):
                c0 = g * 512 + half * 256
                Wg_ = wslab(wg[l].rearrange("(kc p) n -> p kc n", p=128)[:, :, c0:c0 + 256], [128, 16, 256])
                Wu_ = wslab(wu[l].rearrange("(kc p) n -> p kc n", p=128)[:, :, c0:c0 + 256], [128, 16, 256])
                pgs = {}
                for j in range(2):
                    for t in range(2):
                        ts_ = slice(t * 512, (t + 1) * 512)
                        pg = rb()
                        for kc in range(16):
                            P.mm(pg[:, :], Wg_[:, kc, j * 128:(j + 1) * 128], HB[:, kc, ts_], kc == 0)
                        pgs[(j, t)] = pg
                for j in range(2):
                    for t in range(2):
                        ts_ = slice(t * 512, (t + 1) * 512)
                        pg = pgs[(j, t)]
                        pu = rb()
                        for kc in range(16):
                            P.mm(pu[:, :], Wu_[:, kc, j * 128:(j + 1) * 128], HB[:, kc, ts_], kc == 0)
                        s_ = sg_[(j * 2 + t) % 2]
                        P.act(s_, pg[:, :], AF.Silu)
                        P.tt(ACT_[:, half * 2 + j, ts_], s_, pu[:, :], ALU.mult)
                mod_step(1)
            for c0 in range(0, D, 1024):
                if SUB == 12:
                    continue
                mod_require(l, 48)
                W = wslab(wd[l, g * 512:(g + 1) * 512, c0:c0 + 1024].rearrange("(h p) n -> p h n", p=128), [128, 4, 1024])
                for j in range(8):
                    c = c0 // 128 + j
                    for t in range(2):
                        ts_ = slice(t * 512, (t + 1) * 512)
                        acc = ab()
                        for i in range(4):
                            P.mm(acc[:, :], W[:, i, j * 128:(j + 1) * 128], ACT_[:, i, ts_], i == 0)
                        P.stt(XF[:, c, ts_], acc[:, :], MODV[:, l, 80 + c:81 + c], XF[:, c, ts_], ALU.mult, ALU.add)
                mod_step()
        top[0] = m

    ys = [al(2048), al(2048)]
    yh = bfv(al(1024), 2048).rearrange("p (c n) -> p c n", c=16)
    yl = bfv(al(1024), 2048).rearrange("p (c n) -> p c n", c=16)
    for i in range(8):
        tsl = slice(i * 128, (i + 1) * 128)
        yt = f32v(ys[i % 2], 2048)
        P.copy(yh, XF[:, :, tsl], eng="act")
        P.tt(yl, XF[:, :, tsl], yh, ALU.subtract)
        for g in range(4):
            ps = rb()
            for j in range(4):
                c = g * 4 + j
                P.mm(ps[:, j * 128:(j + 1) * 128], yh[:, c, :], IDENT, j == 0)
                P.mm(ps[:, j * 128:(j + 1) * 128], yl[:, c, :], IDENT, False)
            P.copy(yt[:, g * 512:(g + 1) * 512], ps[:, :], eng=("act" if g % 2 else "dve"))
        P.dma(y_d[tsl, :], yt, "y%d" % (i % 2))
    cnt = P.emit()
    st.close()
    return nc, cnt, len(P.ins)


def _consts():
    t = np.arange(T)
    pos = np.stack([t // 64, t % 64], 0).astype(np.float32)
    inv = (10000.0 ** (-np.arange(16, dtype=np.float32) / 16)).astype(np.float32)
    cos = np.zeros((128, T), np.float32); sin = np.zeros((128, T), np.float32)
    for p in range(128):
        f = p % 64
        ang = (pos[f // 32] * inv[f % 16]).astype(np.float32)
        cos[p] = np.cos(ang); sin[p] = np.sin(ang)
    cm = np.zeros((128, 6, 128), np.float32)
    cm[:, 0, :] = np.eye(128)
    for m in range(128):
        f = m % 64; a = f // 32; half = (f % 32) // 16; j = f % 16; hb = (m // 64) * 64
        if half == 0:
            k = hb + a * 32 + 16 + j; cm[k, 1, m] = -1.0
        else:
            k = hb + a * 32 + j; cm[k, 1, m] = 1.0
    for k in range(128):
        cm[k, 2, (k // 64) * 64:(k // 64) * 64 + 64] = 1.0 / 64
    cm[:, 3, :] = 1.0 / 2048
    cm[:, 4, :] = 1.0 / 128
    cm[:, 5, :] = 1.0
    jmw = np.zeros((64, 2, 128), np.float32)
    for k in range(64):
        jmw[k, 0, 63 - k] = 1.0
        jmw[k, 1, 64 + 63 - k] = 1.0
    sel = np.zeros((128, 2, 128), np.float32)
    for p in range(128):
        for i in range(2):
            sel[p, i, i * 64 + (p % 64)] = 1.0
    cq = np.arange(64)
    cstart = np.clip(cq - 8, 0, 48)
    ok = (cq[None, :] >= cstart[:, None]) & (cq[None, :] < cstart[:, None] + 16)
    negc1 = np.where(ok.T, 0.0, -1e30).astype(np.float32)
    negc = np.concatenate([negc1, negc1], 0)
    b = np.arange(128)[:, None]; a = np.arange(128)[None, :]
    band = np.concatenate([(b <= a), np.ones((128, 128), bool), (a <= b)], 1).astype(np.float32)
    return dict(cos_t=cos, sin_t=sin, cmat=cm, jmw=jmw, sel=sel, negc=negc, band=band)


_CACHE = {}


def kernel(x_prompt, x_sample, cache_swa_k, cache_swa_v, cache_na_k, cache_na_v, cache_diff_k, cache_diff_v,
           state_lru, c, c_ctx, norm_mix, norm_ffn, w_mod, b_mod, w_in, w_out, qk_gain, swa_sink, na_rpb,
           diff_lambda, diff_subln, conv_w, conv_b, lru_wa, lru_ba, lru_wx, lru_bx, lru_L,
           w_ffn_gate, w_ffn_up, w_ffn_down):
    f = lambda a: np.ascontiguousarray(np.asarray(a, dtype=np.float32))
    if "nc" not in _CACHE:
        _CACHE["nc"] = build_nc()[0]
    nc = _CACHE["nc"]
    perm = np.arange(PW)
    qperm = []
    for cc in range(4):
        qperm += list(range(cc * 64, cc * 64 + 64)) + list(range((4 + cc) * 64, (4 + cc) * 64 + 64))
    perm[0:512] = np.array(qperm)
    w_in_p = f(np.asarray(w_in)[:, :, perm])
    rp = np.asarray(na_rpb, np.float32)
    ppad = np.zeros((2, 8, 15, 127), np.float32)
    ppad[..., 48:79] = rp
    prev = ppad[..., ::-1]
    rpbrev = f(prev[:, :, ::-1, :])
    shared = dict(w_mod=f(w_mod), b_mod=f(b_mod), w_in=w_in_p, w_out=f(w_out), wg=f(w_ffn_gate), wu=f(w_ffn_up),
                  wd=f(w_ffn_down), norm_mix=f(norm_mix), norm_ffn=f(norm_ffn), qk_gain=f(np.asarray(qk_gain).reshape(2, 6, 64)),
                  swa_sink=f(swa_sink), rpbrev=rpbrev, dlam=f(np.asarray(diff_lambda).reshape(2, 256)), dsub=f(diff_subln),
                  conv_w=f(conv_w), conv_b=f(conv_b), lru_wa=f(lru_wa), lru_ba=f(lru_ba), lru_wx=f(lru_wx),
                  lru_bx=f(lru_bx), lru_L=f(lru_L))
    shared.update(_consts())
    xp = np.asarray(x_prompt, np.float32); xs = np.asarray(x_sample, np.float32)
    in_maps = []
    for i in range(8):
        m = dict(shared)
        if i < 4:
            m["x"] = f(xp[4 * i:4 * i + 4].reshape(T, D)); m["cvec"] = f(c_ctx)
            m["flags"] = f(np.tile(np.array([[1.0, 0.0]], np.float32), (128, 1)))
            m["c_swa_k"] = np.zeros((2, 256, 128), np.float32); m["c_swa_v"] = np.zeros((2, 256, 128), np.float32)
            m["c_na_k"] = np.zeros((2, 256, 512), np.float32); m["c_na_v"] = np.zeros((2, 256, 512), np.float32)
            m["c_diff_k"] = np.zeros((2, 256, 512), np.float32); m["c_diff_v"] = np.zeros((2, 256, 512), np.float32)
            m["state"] = np.zeros((2, 2, 512), np.float32)
        else:
            b = i - 4
            m["x"] = f(xs[b]); m["cvec"] = f(np.asarray(c)[b])
            m["flags"] = f(np.tile(np.array([[0.0, 1.0]], np.float32), (128, 1)))
            m["c_swa_k"] = f(np.asarray(cache_swa_k)[b].reshape(2, 256, 128)); m["c_swa_v"] = f(np.asarray(cache_swa_v)[b].reshape(2, 256, 128))
            m["c_na_k"] = f(np.asarray(cache_na_k)[b].reshape(2, 256, 512)); m["c_na_v"] = f(np.asarray(cache_na_v)[b].reshape(2, 256, 512))
            m["c_diff_k"] = f(np.asarray(cache_diff_k)[b].reshape(2, 256, 512)); m["c_diff_v"] = f(np.asarray(cache_diff_v)[b].reshape(2, 256, 512))
            m["state"] = f(np.asarray(state_lru)[b])
        in_maps.append(m)
    res = run_bass_kernel_spmd(nc, in_maps, core_ids=list(range(8)))
    R = res.results
    y_prompt = np.concatenate([R[i]["y"].reshape(4, 256, D) for i in range(4)], 0)
    y_sample = np.stack([R[4 + i]["y"] for i in range(4)], 0)

    def gat(name, shp):
        return np.concatenate([np.transpose(R[i][name].reshape((2, 4, 256) + shp), (1, 0, 2) + tuple(range(3, 3 + len(shp)))) for i in range(4)], 0)
    nsk = gat("o_swa_k", (2, 64)); nsv = gat("o_swa_v", (2, 64))
    nnk = gat("o_na_k", (8, 64)); nnv = gat("o_na_v", (8, 64))
    ndk = gat("o_diff_k", (4, 2, 64)); ndv = gat("o_diff_v", (4, 128))
    nst = np.concatenate([np.transpose(R[i]["o_state"], (1, 0, 2, 3)) for i in range(4)], 0)
    return (y_prompt.astype(np.float32), y_sample.astype(np.float32), nsk, nsv, nnk, nnv, ndk, ndv, nst.astype(np.float32))
```
